# Optimizing a Trainium2 kernel written in Bass

```python
import math
import jax, jax.numpy as jnp
from jax import lax
import numpy as np

D_MODEL = 1024
BATCH = 2
SEQ = 8192
DEPTH = 2

CHUNK = 64
N_MIXERS = 2
N_A = (DEPTH + 1) // 2
N_B = DEPTH // 2

D_RNN = D_MODEL
RG_BLOCKS = 4
RG_BW = D_RNN // RG_BLOCKS
CONV_W = 4
RG_C = 8.0

SB_HEADS = 16
SB_HEAD_DIM = D_MODEL // SB_HEADS
Q_BLOCK = 128

D_FF = int(math.ceil(8 * D_MODEL / 3 / 256) * 256)

RMS_EPS = 1e-6

kernel_name = "hybrid_rglru_stickbreaking_trunk"


def _rmsnorm(x, g):
    x32 = x.astype(jnp.float32)
    y = x32 * lax.rsqrt(jnp.mean(x32 * x32, axis=-1, keepdims=True) + RMS_EPS)
    return (y * g.astype(jnp.float32)).astype(x.dtype)


def _causal_depthwise_conv(x, w, b):
    c = x.shape[-1]
    y = lax.conv_general_dilated(
        x, w.astype(x.dtype)[:, None, :], window_strides=(1,),
        padding=[(CONV_W - 1, 0)], dimension_numbers=("NWC", "WIO", "NWC"),
        feature_group_count=c)
    return y + b.astype(x.dtype)


def _linear_scan(a, u):
    def combine(l, r):
        a_l, b_l = l
        a_r, b_r = r
        return a_l * a_r, a_r * b_l + b_r
    _, h = lax.associative_scan(combine, (a, u), axis=1)
    return h


def _rglru_mixer(h, w_in, conv_w, conv_b, w_r, b_r, w_i, b_i, lam, w_out):
    bsz, s, _ = h.shape
    proj = h @ w_in.astype(h.dtype)
    gate_br, x_br = proj[..., :D_RNN], proj[..., D_RNN:]
    gate = jax.nn.gelu(gate_br, approximate=True)
    xc = _causal_depthwise_conv(x_br, conv_w, conv_b)
    xg = xc.reshape(bsz, s, RG_BLOCKS, RG_BW).astype(jnp.float32)
    r = jax.nn.sigmoid(jnp.einsum("bsnc,ncd->bsnd", xg, w_r.astype(jnp.float32)).reshape(bsz, s, D_RNN)
                       + b_r.astype(jnp.float32))
    i = jax.nn.sigmoid(jnp.einsum("bsnc,ncd->bsnd", xg, w_i.astype(jnp.float32)).reshape(bsz, s, D_RNN)
                       + b_i.astype(jnp.float32))
    log_a = RG_C * r * jax.nn.log_sigmoid(lam.astype(jnp.float32))
    a = jnp.exp(log_a)
    mult = jnp.sqrt(jnp.maximum(-jnp.expm1(2.0 * log_a), 0.0))
    u = mult * (i * xc.astype(jnp.float32))
    hs = _linear_scan(a, u)
    y = (hs * gate.astype(jnp.float32)).astype(h.dtype)
    return y @ w_out.astype(h.dtype)


def _stick_breaking_attention(q, k, v):
    bsz, nh, s, dh = q.shape
    nq = s // Q_BLOCK
    scale = 1.0 / math.sqrt(dh)
    k32 = k.astype(jnp.float32)
    v32 = v.astype(jnp.float32)
    key_pos = jnp.arange(s)
    q_blocks = q.reshape(bsz, nh, nq, Q_BLOCK, dh).transpose(2, 0, 1, 3, 4)
    starts = jnp.arange(nq) * Q_BLOCK

    def block(args):
        qb, start = args
        z = jnp.einsum("bhqd,bhkd->bhqk", qb.astype(jnp.float32), k32) * scale
        q_pos = start + jnp.arange(Q_BLOCK)
        mask = key_pos[None, :] < q_pos[:, None]
        log_beta = jax.nn.log_sigmoid(z)
        log_1m = jnp.where(mask, jax.nn.log_sigmoid(-z), 0.0)
        suffix = lax.cumsum(log_1m, axis=3, reverse=True) - log_1m
        wts = jnp.where(mask, jnp.exp(log_beta + suffix), 0.0)
        return jnp.einsum("bhqk,bhkd->bhqd", wts, v32)

    out = lax.map(block, (q_blocks, starts))
    return out.transpose(1, 2, 0, 3, 4).reshape(bsz, nh, s, dh).astype(q.dtype)


def _sb_mixer(h, w_qkv, w_out):
    bsz, s, _ = h.shape
    qkv = (h @ w_qkv.astype(h.dtype)).reshape(bsz, s, 3, SB_HEADS, SB_HEAD_DIM)
    qkv = qkv.transpose(2, 0, 3, 1, 4)
    o = _stick_breaking_attention(qkv[0], qkv[1], qkv[2])
    o = o.transpose(0, 2, 1, 3).reshape(bsz, s, D_MODEL)
    return o @ w_out.astype(h.dtype)


def _swiglu(h, w_gate, w_up, w_down):
    g = h @ w_gate.astype(h.dtype)
    u = h @ w_up.astype(h.dtype)
    return (jax.nn.silu(g) * u) @ w_down.astype(h.dtype)


def setup_inputs(seed: int = 0) -> dict:
    key = jax.random.key(seed)
    ks = jax.random.split(key, 20)
    f32 = jnp.float32

    def nrm(k, shape, fan_in):
        return jax.random.normal(k, shape, f32) * (fan_in ** -0.5)

    a0 = jax.random.uniform(ks[10], (N_A, D_RNN), f32, 0.9, 0.999)
    base = a0 ** (1.0 / RG_C)
    lam = jnp.log(base) - jnp.log1p(-base)

    return {
        "x": jax.random.normal(ks[0], (BATCH, SEQ, D_MODEL), f32),
        "norm_mix_g": 1.0 + 0.02 * jax.random.normal(ks[1], (DEPTH, D_MODEL), f32),
        "norm_ffn_g": 1.0 + 0.02 * jax.random.normal(ks[2], (DEPTH, D_MODEL), f32),
        "a_w_in": nrm(ks[3], (N_A, D_MODEL, 2 * D_RNN), D_MODEL),
        "a_conv_w": nrm(ks[4], (N_A, CONV_W, D_RNN), CONV_W),
        "a_conv_b": 0.01 * jax.random.normal(ks[5], (N_A, D_RNN), f32),
        "a_w_r": nrm(ks[6], (N_A, RG_BLOCKS, RG_BW, RG_BW), RG_BW),
        "a_b_r": 0.01 * jax.random.normal(ks[7], (N_A, D_RNN), f32),
        "a_w_i": nrm(ks[8], (N_A, RG_BLOCKS, RG_BW, RG_BW), RG_BW),
        "a_b_i": 0.01 * jax.random.normal(ks[9], (N_A, D_RNN), f32),
        "a_lambda": lam,
        "a_w_out": nrm(ks[11], (N_A, D_RNN, D_MODEL), D_RNN),
        "b_w_qkv": nrm(ks[12], (N_B, D_MODEL, 3 * D_MODEL), D_MODEL),
        "b_w_out": nrm(ks[13], (N_B, D_MODEL, D_MODEL), D_MODEL),
        "ffn_w_gate": nrm(ks[14], (DEPTH, D_MODEL, D_FF), D_MODEL),
        "ffn_w_up": nrm(ks[15], (DEPTH, D_MODEL, D_FF), D_MODEL),
        "ffn_w_down": nrm(ks[16], (DEPTH, D_FF, D_MODEL), D_FF),
        "final_g": 1.0 + 0.02 * jax.random.normal(ks[17], (D_MODEL,), f32),
    }


def reference(x, norm_mix_g, norm_ffn_g, a_w_in, a_conv_w, a_conv_b, a_w_r, a_b_r,
              a_w_i, a_b_i, a_lambda, a_w_out, b_w_qkv, b_w_out,
              ffn_w_gate, ffn_w_up, ffn_w_down, final_g):
    for layer in range(DEPTH):
        h = _rmsnorm(x, norm_mix_g[layer])
        if layer % N_MIXERS == 0:
            j = layer // N_MIXERS
            mix = _rglru_mixer(h, a_w_in[j], a_conv_w[j], a_conv_b[j], a_w_r[j], a_b_r[j],
                               a_w_i[j], a_b_i[j], a_lambda[j], a_w_out[j])
        else:
            j = layer // N_MIXERS
            mix = _sb_mixer(h, b_w_qkv[j], b_w_out[j])
        x = x + mix
        h = _rmsnorm(x, norm_ffn_g[layer])
        x = x + _swiglu(h, ffn_w_gate[layer], ffn_w_up[layer], ffn_w_down[layer])
    return _rmsnorm(x, final_g)
```

```python
import contextlib
import numpy as np
import ml_dtypes
import concourse.bass as bass
import concourse.mybir as mybir
from concourse.bass_utils import run_bass_kernel_spmd

F32 = mybir.dt.float32
F32R = mybir.dt.float32r
BF16 = mybir.dt.bfloat16
AF = mybir.ActivationFunctionType
ALU = mybir.AluOpType
NPBF = ml_dtypes.bfloat16

D = 1024
DFF = 2816
NF = DFF // 128
T = 2048
S = 8192
TH = 4
EPS = 1e-6
COMPUTE = ("tensor", "vector", "scalar", "gpsimd")


class Prog:
    def __init__(self, nc, n_dma_sems=8):
        self.nc = nc
        self.ops = []
        self.state = {}
        self.n_dma_sems = n_dma_sems

    def add(self, eng, fn, reads=(), writes=(), dma=False):
        idx = len(self.ops)
        ops = self.ops
        deps = set()
        for k in reads:
            st = self.state.setdefault(k, [None, []])
            if st[0] is not None:
                deps.add(st[0])
        for k in writes:
            st = self.state.setdefault(k, [None, []])
            if st[0] is not None:
                deps.add(st[0])
            deps.update(st[1])
        for k in reads:
            st = self.state[k]
            if not dma:
                st[1] = [r for r in st[1] if ops[r]["dma"] or ops[r]["eng"] != eng]
            st[1].append(idx)
        for k in writes:
            self.state[k] = [idx, []]
        deps.discard(idx)
        ops.append(dict(eng=eng, fn=fn, deps=deps, dma=dma))
        return idx

    def mm(self, fn, reads=(), writes=()):
        return self.add("tensor", fn, reads, writes)

    def act(self, fn, reads=(), writes=()):
        return self.add("scalar", fn, reads, writes)

    def dve(self, fn, reads=(), writes=()):
        return self.add("vector", fn, reads, writes)

    def pool(self, fn, reads=(), writes=()):
        return self.add("gpsimd", fn, reads, writes)

    def dma(self, q, out, in_, reads=(), writes=()):
        return self.add(q, lambda e: e.dma_start(out=out, in_=in_), reads, writes, dma=True)

    def barrier(self):
        for e in ("sync", "scalar", "vector", "gpsimd", "tensor"):
            self.ops.append(dict(eng=e, fn=None, deps=set(), dma=False, barrier=True))

    def cc(self, fn, reads=(), writes=()):
        idx = self.add("gpsimd", fn, reads, writes)
        self.ops[idx]["cc"] = True
        return idx

    def emit(self, final_wait_ops=()):
        nc = self.nc
        ops = self.ops
        n = len(ops)

        def skip(d, e):
            od = ops[d]
            return od["eng"] == "tensor" and e == "tensor" and not od["dma"]

        has_dep = [False] * n
        last_op = {}
        for i, o in enumerate(ops):
            if o.get("barrier"):
                for e2 in COMPUTE:
                    if e2 in last_op:
                        has_dep[last_op[e2]] = True
                continue
            if o["eng"] in COMPUTE and not o["dma"] and not o.get("cc"):
                last_op[o["eng"]] = i
            for d in o["deps"]:
                if not skip(d, o["eng"]):
                    has_dep[d] = True
        for d in final_wait_ops:
            has_dep[d] = True
        engs = ("sync", "scalar", "vector", "gpsimd", "tensor")
        stack = contextlib.ExitStack()
        sems = {}
        for e in COMPUTE:
            sems[e] = stack.enter_context(nc.semaphore("s_" + e))
        dma_sems = {}
        for q in ("sync", "scalar", "gpsimd"):
            for j in range(self.n_dma_sems):
                dma_sems[(q, j)] = stack.enter_context(nc.semaphore("d_%s%d" % (q, j)))
        sems["cc"] = stack.enter_context(nc.semaphore("s_cc"))
        cnt = {k: 0 for k in list(sems) + list(dma_sems)}
        rr = {"sync": 0, "scalar": 0, "gpsimd": 0}
        sig = [None] * n
        waits = [None] * n
        waited = {e: {} for e in engs}
        for i, o in enumerate(ops):
            e = o["eng"]
            w = []

            def need(key, val):
                if val > 0 and waited[e].get(key, 0) < val:
                    waited[e][key] = val
                    w.append((key, val))

            if o.get("barrier"):
                for key in list(cnt):
                    if key != e:
                        need(key, cnt[key])
                waits[i] = w
                continue
            for d in sorted(o["deps"]):
                if skip(d, e):
                    continue
                need(*sig[d])
            if o.get("cc"):
                cnt["cc"] += 1
                sig[i] = ("cc", cnt["cc"])
            elif o["dma"]:
                j = rr[e]
                rr[e] = (j + 1) % self.n_dma_sems
                key = (e, j)
                need(key, cnt[key])
                cnt[key] += 16
                sig[i] = (key, cnt[key])
            elif has_dep[i]:
                cnt[e] += 1
                sig[i] = (e, cnt[e])
            waits[i] = w
        allsems = dict(sems)
        allsems.update(dma_sems)
        self.max_counts = dict(cnt)
        final = [sig[d] for d in final_wait_ops]

        with stack:
            with nc.Block() as block:
                def make(ename):
                    def body(eng):
                        for i, o in enumerate(ops):
                            if o["eng"] != ename:
                                continue
                            for key, val in waits[i]:
                                eng.wait_ge(allsems[key], val)
                            if o["fn"] is None:
                                continue
                            inst = o["fn"](eng)
                            if sig[i] is not None:
                                inst.then_inc(allsems[sig[i][0]], 16 if o["dma"] else 1)
                        if ename == "sync":
                            for key, val in final:
                                eng.wait_ge(allsems[key], val)
                    return body

                block.sync(make("sync"))
                block.scalar(make("scalar"))
                block.vector(make("vector"))
                block.gpsimd(make("gpsimd"))
                block.tensor(make("tensor"))


class Ctx:
    def __init__(self):
        self.nc = bass.Bass("TRN2", target_bir_lowering=False)
        self.stack = contextlib.ExitStack()
        self.gstack = contextlib.ExitStack()
        self.p = Prog(self.nc)
        self.ps = None
        self.psi = 0
        self.prefix = ""
        self.over = {}
        self.fused = False

    def din(self, name, shape, dt=F32):
        if name in self.over:
            return self.over[name]
        return self.nc.dram_tensor(self.prefix + name, list(shape), dt, kind="ExternalInput").ap()

    def dout(self, name, shape, dt=F32):
        if name in self.over:
            return self.over[name]
        return self.nc.dram_tensor(self.prefix + name, list(shape), dt, kind="ExternalOutput").ap()

    def dint(self, name, shape, dt=F32):
        return self.nc.dram_tensor(name, list(shape), dt, addr_space="Local", kind="Internal").ap()

    def sb(self, name, shape, dt=F32):
        return self.stack.enter_context(self.nc.sbuf_tensor(self.prefix + "s_" + name, list(shape), dt))

    def init_psum(self):
        if self.ps is None:
            self.psbig = self.gstack.enter_context(self.nc.psum_tensor("psbig", [128, 8, 512], F32))
            self.ps = [self.psbig[:, i, :] for i in range(8)]

    def phase(self, prefix):
        self.stack = contextlib.ExitStack()
        return self.stack

    def nextps(self):
        b = self.psi
        self.psi = (b + 1) % 8
        return b, self.ps[b]


def emit_norm(cx, x3, xkey, N, g_sb, h3, hkey, sq, rs, rstd, ones):
    p = cx.p
    p.act(lambda e: e.activation(out=sq[:, :, :N], in_=x3, func=AF.Square), reads=[xkey], writes=["sq"])
    b, ps = cx.nextps()
    for c in range(8):
        p.mm(lambda e, c=c: e.matmul(ps[:, :N], lhsT=ones[:], rhs=sq[:, c, :N], start=(c == 0), stop=(c == 7)),
             reads=["sq", "ones"], writes=[("ps", b)])
    p.act(lambda e, eps_t=cx.eps_t: e.activation(out=rs[:, :N], in_=ps[:, :N], func=AF.Sqrt, scale=1.0 / D, bias=eps_t[:, 0:1]),
          reads=[("ps", b), "eps"], writes=["rs"])
    p.dve(lambda e: e.reciprocal(out=rstd[:, :N], in_=rs[:, :N]), reads=["rs"], writes=["rstd"])
    for c in range(8):
        eng = "vector"
        p.add(eng, lambda e, c=c: e.scalar_tensor_tensor(out=h3[:, c, :], in0=x3[:, c, :], scalar=g_sb[:, c:c + 1],
                                                     in1=rstd[:, :N], op0=ALU.mult, op1=ALU.mult),
              reads=[xkey, "rstd", "gvec"], writes=[(hkey, c)])


def load_w_bf16(cx, dst, src, K, key, nsplit=None):
    v = src.rearrange("(k p) n -> p k n", p=128)
    for k in range(K):
        cx.p.dma("gpsimd", dst[:, k, :], v[:, k, :], writes=[(key, k)])


def build_p1(cx=None):
    standalone = cx is None
    if standalone:
        cx = Ctx()
    cx.prefix = "p1_" if cx.fused else ""
    nc, p = cx.nc, cx.p
    xT = cx.din("xT", [D, TH + T])
    gvec = cx.din("gvec", [128, 8])
    w_in = cx.din("w_in", [D, 2 * D])
    convw = cx.din("convw", [128, 32])
    convb = cx.din("convb", [128, 8])
    w_r = cx.din("w_r", [D, 256])
    w_i = cx.din("w_i", [D, 256])
    b_r = cx.din("b_r", [128, 8])
    b_i = cx.din("b_i", [128, 8])
    lam = cx.din("lam", [128, 8])
    ident = cx.din("ident", [128, 128])
    hloc = cx.dout("hloc", [D, T])
    acum = cx.dout("acum", [D, T])
    gate = cx.dout("gate", [D, T], BF16)

    xTv = xT.rearrange("(c p) t -> p c t", p=128)
    hlv = hloc.rearrange("(c p) t -> p c t", p=128)
    acv = acum.rearrange("(c p) t -> p c t", p=128)
    gtv = gate.rearrange("(c p) t -> p c t", p=128)

    with cx.phase("p1_"):
        cx.init_psum()
        sb = cx.sb
        xt = [sb("xt%d" % i, [128, 8, 512]) for i in range(1)]
        sq = sb("sq", [128, 8, 512], BF16)
        rs = sb("rs", [128, 512])
        rstd = sb("rstd", [128, 512])
        h = sb("h", [128, 8, 512], BF16)
        win = sb("win", [128, 8, 2 * D], BF16)
        wr = sb("wr", [128, 8, 256], BF16)
        wi = sb("wi", [128, 8, 256], BF16)
        diag = sb("diag", [128, 32, 128], BF16)
        identf = sb("identf", [128, 128])
        ones = sb("ones", [128, 128], BF16)
        zeros = sb("zeros", [128, 512])
        cx.eps_t = sb("eps", [128, 1])
        g_sb = sb("g_sb", [128, 8])
        cw = sb("cw", [128, 32])
        cb = sb("cb", [128, 8])
        br = sb("br", [128, 8])
        bi = sb("bi", [128, 8])
        lm = sb("lm", [128, 8])
        c8 = sb("c8", [128, 8])
        c16 = sb("c16", [128, 8])
        xb = sb("xb", [128, 8, TH + T], BF16)
        gt = [sb("gt%d" % i, [128, 8, 512], BF16) for i in range(2)]
        xc = sb("xc", [128, 2, 512])
        xcb = sb("xcb", [128, 2, 512], BF16)
        rt = [sb("rt%d" % i, [128, 512]) for i in range(2)]
        it = [sb("it%d" % i, [128, 512]) for i in range(2)]
        a2 = [sb("a2%d" % i, [128, 512]) for i in range(2)]
        at = sb("at", [128, 2, 512])
        ut = sb("ut", [128, 2, 512])
        hl = [sb("hl%d" % i, [128, 8, 512]) for i in range(1)]
        ac = [sb("ac%d" % i, [128, 8, 512]) for i in range(1)]
        hlast = sb("hlast", [128, 8])
        alast = sb("alast", [128, 8])

        for dst, src, key in ((g_sb, gvec, "gvec"), (cw, convw, "cw"), (cb, convb, "cb"), (br, b_r, "br"),
                              (bi, b_i, "bi"), (lm, lam, "lm"), (identf, ident, "identf")):
            p.dma("sync", dst[:], src, writes=[key])
        p.dve(lambda e: e.memset(ones[:], 1.0), writes=["ones"])
        p.dve(lambda e: e.memset(zeros[:], 0.0), writes=["zeros"])
        p.dve(lambda e, eps_t=cx.eps_t: e.memset(eps_t[:], EPS), writes=["eps"])
        load_w_bf16(cx, win, w_in, 8, "win")
        load_w_bf16(cx, wr, w_r, 8, "wr")
        load_w_bf16(cx, wi, w_i, 8, "wi")
        p.act(lambda e: e.activation(out=c8[:], in_=lm[:], func=AF.Sigmoid), reads=["lm"], writes=["c8"])
        p.act(lambda e: e.activation(out=c8[:], in_=c8[:], func=AF.Ln), reads=["c8"], writes=["c8"])
        p.act(lambda e: e.mul(out=c16[:], in_=c8[:], mul=16.0), reads=["c8"], writes=["c16"])
        p.act(lambda e: e.mul(out=c8[:], in_=c8[:], mul=8.0), reads=["c8"], writes=["c8"])
        for jc in range(32):
            p.dve(lambda e, jc=jc: e.tensor_scalar(out=diag[:, jc, :], in0=identf[:], scalar1=cw[:, jc:jc + 1],
                                                 scalar2=None, op0=ALU.mult),
                  reads=["identf", "cw"], writes=[("diag", jc)])

        winkeys = [("win", k) for k in range(8)]
        wrkeys = [("wr", k) for k in range(8)]
        wikeys = [("wi", k) for k in range(8)]

        def do_tile(ti):
            halo = ti < 0
            N = TH if halo else 512
            t0 = 0 if halo else TH + ti * 512
            xbuf = xt[0]
            xkey = ("xt", 0)
            x3 = xbuf[:, :, :N]
            p.dma("sync", x3, xTv[:, :, t0:t0 + N], writes=[xkey])
            emit_norm(cx, x3, xkey, N, g_sb, h[:, :, :N], "h", sq, rs, rstd, ones)
            hkeys = [("h", c) for c in range(8)]
            gbuf = gt[ti % 2]
            gkey = ("gt", ti % 2)
            for n in range(0 if not halo else 8, 16):
                b, ps = cx.nextps()
                for k in range(8):
                    p.mm(lambda e, k=k, n=n, ps=ps: e.matmul(ps[:, :N], lhsT=win[:, k, n * 128:(n + 1) * 128],
                                                          rhs=h[:, k, :N], start=(k == 0), stop=(k == 7)),
                         reads=hkeys + winkeys, writes=[("ps", b)])
                if n < 8:
                    p.act(lambda e, n=n, ps=ps: e.activation(out=gbuf[:, n, :], in_=ps[:, :N], func=AF.Gelu_apprx_tanh),
                          reads=[("ps", b)], writes=[gkey])
                else:
                    p.act(lambda e, n=n, ps=ps: e.copy(out=xb[:, n - 8, t0:t0 + N], in_=ps[:, :N]),
                          reads=[("ps", b)], writes=[("xb", n - 8, ti)])
            if halo:
                return
            p.dma("sync", gtv[:, :, ti * 512:(ti + 1) * 512], gbuf[:], reads=[gkey])
            yield
            hbuf, abuf = hl[0], ac[0]
            for blk in range(4):
                for cc in range(2):
                    c = 2 * blk + cc
                    b, ps = cx.nextps()
                    for j in range(4):
                        p.mm(lambda e, j=j, c=c, ps=ps: e.matmul(ps[:], lhsT=diag[:, j * 8 + c, :],
                                                              rhs=xb[:, c, t0 - 3 + j:t0 - 3 + j + 512],
                                                              start=(j == 0), stop=(j == 3)),
                             reads=[("xb", c, ti), ("xb", c, ti - 1), ("diag", j * 8 + c)], writes=[("ps", b)])
                    p.act(lambda e, c=c, cc=cc, ps=ps: e.activation(out=xc[:, cc, :], in_=ps[:], func=AF.Identity,
                                                                bias=cb[:, c:c + 1]),
                          reads=[("ps", b), "cb"], writes=[("xc", cc)])
                    p.dve(lambda e, cc=cc: e.tensor_copy(out=xcb[:, cc, :], in_=xc[:, cc, :]),
                          reads=[("xc", cc)], writes=[("xcb", cc)])
                for cc in range(2):
                    n = 2 * blk + cc
                    rb, ib, ab = rt[cc], it[cc], a2[cc]
                    b, ps = cx.nextps()
                    for kk in range(2):
                        p.mm(lambda e, kk=kk, cc=cc, ps=ps, blk=blk: e.matmul(ps[:], lhsT=wr[:, blk * 2 + kk, cc * 128:(cc + 1) * 128],
                                                                  rhs=xcb[:, kk, :], start=(kk == 0), stop=(kk == 1)),
                             reads=[("xcb", 0), ("xcb", 1)] + wrkeys, writes=[("ps", b)])
                    p.act(lambda e, n=n, ps=ps, rb=rb: e.activation(out=rb[:], in_=ps[:], func=AF.Sigmoid, bias=br[:, n:n + 1]),
                          reads=[("ps", b), "br"], writes=[("rt", cc)])
                    b, ps = cx.nextps()
                    for kk in range(2):
                        p.mm(lambda e, kk=kk, cc=cc, ps=ps, blk=blk: e.matmul(ps[:], lhsT=wi[:, blk * 2 + kk, cc * 128:(cc + 1) * 128],
                                                                  rhs=xcb[:, kk, :], start=(kk == 0), stop=(kk == 1)),
                             reads=[("xcb", 0), ("xcb", 1)] + wikeys, writes=[("ps", b)])
                    p.act(lambda e, n=n, ps=ps, ib=ib: e.activation(out=ib[:], in_=ps[:], func=AF.Sigmoid, bias=bi[:, n:n + 1]),
                          reads=[("ps", b), "bi"], writes=[("it", cc)])
                for cc in range(2):
                    n = 2 * blk + cc
                    rb, ib, ab = rt[cc], it[cc], a2[cc]
                    p.act(lambda e, n=n, rb=rb, cc=cc: e.activation(out=at[:, cc, :], in_=rb[:], func=AF.Exp, scale=c8[:, n:n + 1]),
                          reads=[("rt", cc), "c8"], writes=[("at", cc)])
                    p.act(lambda e, n=n, rb=rb, ab=ab: e.activation(out=ab[:], in_=rb[:], func=AF.Exp, scale=c16[:, n:n + 1]),
                          reads=[("rt", cc), "c16"], writes=[("a2", cc)])
                for cc in range(2):
                    n = 2 * blk + cc
                    rb, ib, ab = rt[cc], it[cc], a2[cc]
                    p.act(lambda e, ab=ab, one_t=cx.one_t: e.activation(out=ab[:], in_=ab[:], func=AF.Sqrt, scale=-1.0, bias=one_t[:, 0:1]),
                          reads=[("a2", cc), "one"], writes=[("a2", cc)])
                    p.pool(lambda e, ib=ib, cc=cc: e.tensor_tensor(out=ib[:], in0=ib[:], in1=xc[:, cc, :], op=ALU.mult),
                           reads=[("it", cc), ("xc", cc)], writes=[("it", cc)])
                    p.dve(lambda e, cc=cc, ib=ib, ab=ab: e.tensor_tensor(out=ut[:, cc, :], in0=ib[:], in1=ab[:], op=ALU.mult),
                          reads=[("it", cc), ("a2", cc)], writes=[("ut", cc)])
                    if ti == 0:
                        hinit, ainit = 0.0, 1.0
                        ir = []
                    else:
                        hinit, ainit = hlast[:, n:n + 1], alast[:, n:n + 1]
                        ir = [("hlast", n), ("alast", n)]
                    p.dve(lambda e, n=n, cc=cc, hinit=hinit: e.tensor_tensor_scan(out=hbuf[:, n, :], data0=at[:, cc, :], data1=ut[:, cc, :],
                                                                        initial=hinit, op0=ALU.mult, op1=ALU.add),
                          reads=[("at", cc), ("ut", cc)] + ir, writes=[("hl", 0, n)])
                    p.dve(lambda e, n=n, cc=cc, ainit=ainit: e.tensor_tensor_scan(out=abuf[:, n, :], data0=at[:, cc, :], data1=zeros[:],
                                                                        initial=ainit, op0=ALU.mult, op1=ALU.add),
                          reads=[("at", cc), "zeros"] + ir, writes=[("ac", 0, n)])
                    p.dve(lambda e, n=n: e.tensor_copy(out=hlast[:, n:n + 1], in_=hbuf[:, n, 511:512]),
                          reads=[("hl", 0, n)], writes=[("hlast", n)])
                    p.dve(lambda e, n=n: e.tensor_copy(out=alast[:, n:n + 1], in_=abuf[:, n, 511:512]),
                          reads=[("ac", 0, n)], writes=[("alast", n)])
            p.dma("sync", hlv[:, :, ti * 512:(ti + 1) * 512], hbuf[:], reads=[("hl", 0, n) for n in range(8)])
            return p.dma("sync", acv[:, :, ti * 512:(ti + 1) * 512], abuf[:], reads=[("ac", 0, n) for n in range(8)])

        cx.one_t = sb("one_t", [128, 1])
        p.dve(lambda e, one_t=cx.one_t: e.memset(one_t[:], 1.0), writes=["one"])
        def step(g):
            try:
                next(g)
            except StopIteration:
                pass

        step(do_tile(-1))
        gens = [do_tile(ti) for ti in range(4)]
        step(gens[0])
        for ti in range(4):
            if ti + 1 < 4:
                step(gens[ti + 1])
            step(gens[ti])
        if cx.fused:
            es = cx.over["ends_src"]
            p.dma("sync", es[:, 0:8], hlast[:], reads=[("hlast", n) for n in range(8)])
            p.dma("sync", es[:, 8:16], alast[:], reads=[("alast", n) for n in range(8)])
        if standalone:
            outs = [i for i, o in enumerate(p.ops) if o["dma"] and o["eng"] == "sync"]
            p.emit(final_wait_ops=outs[-12:])
    return nc


def build_p24(first, out_f32, cx=None):
    standalone = cx is None
    if standalone:
        cx = Ctx()
    fused = cx.fused
    cx.prefix = ("p2_" if first else "p4_") if fused else ""
    nc, p = cx.nc, cx.p
    xT = cx.din("xT", [D, T])
    if first:
        hloc = cx.din("hloc", [D, T])
        acum = cx.din("acum", [D, T])
        gate = cx.din("gate", [D, T], BF16)
        if not fused:
            ends = cx.din("ends", [128, 48])
        else:
            km_d = cx.din("km", [128, 4])
            hm_d = cx.din("hm", [128, 4])
    else:
        if not fused:
            oT = cx.din("oT", [D, T], BF16)
        else:
            sel_d = cx.din("sel", [128, 4])
    w_out = cx.din("w_out", [8, 128, 8 * 128])
    g_ffn = cx.din("g_ffn", [128, 8])
    g_next = cx.din("g_next", [128, 8])
    wg = cx.din("wg", [NF // 2, 128, 8 * 256])
    wu = cx.din("wu", [NF // 2, 128, 8 * 256])
    wd = cx.din("wd", [8, 128, NF * 128])
    x2T = cx.dout("x2T", [D, T])
    hnT = cx.dout("hnT", [D, T], F32 if out_f32 else BF16)

    v3 = lambda a: a.rearrange("(c p) t -> p c t", p=128)
    xTv, x2v, hnv = v3(xT), v3(x2T), v3(hnT)

    with cx.phase("p2_" if first else "p4_"):
        cx.init_psum()
        sb = cx.sb
        xr = sb("xr", [128, 8, 1024])
        y = [sb("y%d" % i, [128, 8, 512], BF16) for i in range(2)]
        sq = sb("sq", [128, 8, 512], BF16)
        rs = sb("rs", [128, 512])
        rstd = sb("rstd", [128, 512])
        h2 = sb("h2", [128, 8, 1024], BF16)
        actb = sb("actb", [128, NF, 1024], BF16)
        sg = [sb("sg%d" % i, [128, 512]) for i in range(2)]
        wo = sb("wo", [128, 8, 8 * 128], BF16)
        wgs = [sb("wgs%d" % i, [128, 8 * 256], BF16) for i in range(3)]
        wus = [sb("wus%d" % i, [128, 8 * 256], BF16) for i in range(3)]
        wds = [sb("wds%d" % i, [128, NF * 128], BF16) for i in range(2)]
        hn = [sb("hn%d" % i, [128, 8, 512], F32 if out_f32 else BF16) for i in range(1 if out_f32 else 2)]
        ones = sb("ones", [128, 128], BF16)
        cx.eps_t = sb("eps", [128, 1])
        gf = sb("gf", [128, 8])
        gn = sb("gn", [128, 8])
        p.dma("sync", gf[:], g_ffn, writes=["gf"])
        p.dma("sync", gn[:], g_next, writes=["gn"])
        p.dve(lambda e: e.memset(ones[:], 1.0), writes=["ones"])
        p.dve(lambda e, eps_t=cx.eps_t: e.memset(eps_t[:], EPS), writes=["eps"])
        for n in range(8):
            p.dma("gpsimd", wo[:, n, :], w_out[n], writes=[("wo", n)])
        if first:
            en = sb("en", [128, 48])
            carry = sb("carry", [128, 8])
            ctmp = sb("ctmp", [128, 8])
            hlc = [sb("hlc%d" % i, [128, 512]) for i in range(2)]
            acc = [sb("acc%d" % i, [128, 512]) for i in range(2)]
            gtc = [sb("gtc%d" % i, [128, 512], BF16) for i in range(2)]
            hlv, acv, gtv = v3(hloc), v3(acum), v3(gate)
            p.dve(lambda e: e.memset(carry[:], 0.0), writes=["carry"])
            if fused:
                en_all = sb("en_all", [128, 4, 16])
                km = sb("km", [128, 4])
                hm = sb("hm", [128, 4])
                ap_t = sb("ap_t", [128, 8])
                hp_t = sb("hp_t", [128, 8])
                p.dma("sync", en_all[:], cx.over["ends_all"].rearrange("(r p) f -> p r f", p=128), writes=["en_all"])
                p.dma("sync", km[:], km_d, writes=["km"])
                p.dma("sync", hm[:], hm_d, writes=["hm"])
                for r in range(4):
                    p.dve(lambda e, r=r: e.tensor_scalar(out=ap_t[:], in0=en_all[:, r, 8:16], scalar1=km[:, r:r + 1],
                                                       scalar2=hm[:, r:r + 1], op0=ALU.mult, op1=ALU.add),
                          reads=["en_all", "km", "hm"], writes=["ap_t"])
                    p.dve(lambda e, r=r: e.tensor_scalar(out=hp_t[:], in0=en_all[:, r, 0:8], scalar1=km[:, r:r + 1],
                                                       scalar2=None, op0=ALU.mult),
                          reads=["en_all", "km"], writes=["hp_t"])
                    p.dve(lambda e: e.tensor_tensor(out=ctmp[:], in0=ap_t[:], in1=carry[:], op=ALU.mult),
                          reads=["ap_t", "carry"], writes=["ctmp"])
                    p.dve(lambda e: e.tensor_tensor(out=carry[:], in0=ctmp[:], in1=hp_t[:], op=ALU.add),
                          reads=["ctmp", "hp_t"], writes=["carry"])
            else:
                p.dma("sync", en[:], ends, writes=["en"])
            for j in range(0 if fused else 3):
                p.dve(lambda e, j=j: e.tensor_tensor(out=ctmp[:], in0=en[:, j * 16 + 8:j * 16 + 16], in1=carry[:], op=ALU.mult),
                      reads=["en", "carry"], writes=["ctmp"])
                p.dve(lambda e, j=j: e.tensor_tensor(out=carry[:], in0=ctmp[:], in1=en[:, j * 16:j * 16 + 8], op=ALU.add),
                      reads=["en", "ctmp"], writes=["carry"])
        elif not fused:
            oTv = v3(oT)
        else:
            oallv = [a.rearrange("(c p) t -> p c t", p=128) for a in cx.over["o_all"]]
            sel = sb("sel", [128, 4])
            cand = [sb("cand%d" % i, [128, 8, 512], BF16) for i in range(2)]
            p.dma("sync", sel[:], sel_d, writes=["sel"])

        def norm_tile(x3, xkey, g_sb, gkey, h3, hkey):
            N = 512
            p.act(lambda e: e.activation(out=sq[:], in_=x3, func=AF.Square), reads=xkey, writes=["sq"])
            b, ps = cx.nextps()
            for c in range(8):
                p.mm(lambda e, c=c: e.matmul(ps[:], lhsT=ones[:], rhs=sq[:, c, :], start=(c == 0), stop=(c == 7)),
                     reads=["sq", "ones"], writes=[("ps", b)])
            p.act(lambda e, eps_t=cx.eps_t: e.activation(out=rs[:], in_=ps[:], func=AF.Sqrt, scale=1.0 / D, bias=eps_t[:, 0:1]),
                  reads=[("ps", b), "eps"], writes=["rs"])
            p.dve(lambda e: e.reciprocal(out=rstd[:], in_=rs[:]), reads=["rs"], writes=["rstd"])
            for c in range(8):
                p.dve(lambda e, c=c: e.scalar_tensor_tensor(out=h3[:, c, :], in0=x3[:, c, :], scalar=g_sb[:, c:c + 1],
                                                          in1=rstd[:], op0=ALU.mult, op1=ALU.mult),
                      reads=list(xkey) + ["rstd", gkey], writes=[(hkey, c)])

        wokeys = [("wo", n) for n in range(8)]
        cidx = [0]

        def load_gu(f2, slot):
            p.dma("gpsimd", wgs[slot][:], wg[f2], writes=[("wgs", slot)])
            p.dma("gpsimd", wus[slot][:], wu[f2], writes=[("wus", slot)])

        def load_d(n, slot):
            p.dma("gpsimd", wds[slot][:], wd[n], writes=[("wds", slot)])

        outs = []
        for half in range(2):
            hx = half * 1024
            xkeys = lambda tt: [("xr", tt, c) for c in range(8)]
            for tt in range(2):
                p.dma("sync", xr[:, :, tt * 512:(tt + 1) * 512], xTv[:, :, hx + tt * 512:hx + (tt + 1) * 512],
                      writes=xkeys(tt))
            load_gu(0, 0)
            load_gu(1, 1)
            for tt in range(2):
                t0 = hx + tt * 512
                yb = y[tt]
                ykeys = [("y", tt, c) for c in range(8)]
                if first:
                    for c in range(8):
                        s = cidx[0] % 2
                        cidx[0] += 1
                        p.dma("sync", hlc[s][:], hlv[:, c, t0:t0 + 512], writes=[("hlc", s)])
                        p.dma("sync", acc[s][:], acv[:, c, t0:t0 + 512], writes=[("acc", s)])
                        p.dma("sync", gtc[s][:], gtv[:, c, t0:t0 + 512], writes=[("gtc", s)])
                        p.dve(lambda e, s=s, c=c: e.scalar_tensor_tensor(out=hlc[s][:], in0=acc[s][:], scalar=carry[:, c:c + 1],
                                                                      in1=hlc[s][:], op0=ALU.mult, op1=ALU.add),
                              reads=[("hlc", s), ("acc", s), "carry"], writes=[("hlc", s)])
                        p.pool(lambda e, s=s, c=c, yb=yb: e.tensor_tensor(out=yb[:, c, :], in0=hlc[s][:], in1=gtc[s][:], op=ALU.mult),
                               reads=[("hlc", s), ("gtc", s)], writes=[("y", tt, c)])
                elif not fused:
                    p.dma("sync", yb[:], oTv[:, :, t0:t0 + 512], writes=ykeys)
                else:
                    for j in range(4):
                        cb = cand[j % 2]
                        ck = ("cand", j % 2)
                        p.dma("sync", cb[:], oallv[j][:, :, t0:t0 + 512], writes=[ck])
                        if j == 0:
                            p.dve(lambda e, cb=cb, yb=yb: e.tensor_scalar(out=yb[:], in0=cb[:], scalar1=sel[:, 0:1], scalar2=None,
                                                                       op0=ALU.mult),
                                  reads=[ck, "sel"], writes=ykeys)
                        else:
                            p.dve(lambda e, cb=cb, yb=yb, j=j: e.scalar_tensor_tensor(out=yb[:], in0=cb[:], scalar=sel[:, j:j + 1],
                                                                                   in1=yb[:], op0=ALU.mult, op1=ALU.add),
                                  reads=[ck, "sel"] + ykeys, writes=ykeys)
                for n in range(8):
                    b, ps = cx.nextps()
                    for k in range(8):
                        p.mm(lambda e, k=k, n=n, ps=ps, yb=yb: e.matmul(ps[:], lhsT=wo[:, n, k * 128:(k + 1) * 128], rhs=yb[:, k, :],
                                                                     start=(k == 0), stop=(k == 7)),
                             reads=ykeys + [("wo", n)], writes=[("ps", b)])
                    p.dve(lambda e, n=n, ps=ps, tt=tt: e.tensor_tensor(out=xr[:, n, tt * 512:(tt + 1) * 512],
                                                                     in0=xr[:, n, tt * 512:(tt + 1) * 512], in1=ps[:], op=ALU.add),
                          reads=[("ps", b), ("xr", tt, n)], writes=[("xr", tt, n)])
                norm_tile(xr[:, :, tt * 512:(tt + 1) * 512], xkeys(tt), gf, "gf", h2[:, :, tt * 512:(tt + 1) * 512], ("h2", tt))
            for f2 in range(NF // 2):
                slot = f2 % 3
                if f2 + 2 < NF // 2:
                    load_gu(f2 + 2, (f2 + 2) % 3)
                if f2 == NF // 2 - 2:
                    load_d(0, 0)
                if f2 == NF // 2 - 1:
                    load_d(1, 1)
                for ff in range(2):
                    f = 2 * f2 + ff
                    for tt in range(2):
                        h2k = [(("h2", tt), c) for c in range(8)]
                        bg, psg = cx.nextps()
                        for k in range(8):
                            p.mm(lambda e, k=k, ff=ff, psg=psg, slot=slot, tt=tt: e.matmul(
                                psg[:], lhsT=wgs[slot][:, k * 256 + ff * 128:k * 256 + (ff + 1) * 128],
                                rhs=h2[:, k, tt * 512:(tt + 1) * 512], start=(k == 0), stop=(k == 7)),
                                reads=h2k + [("wgs", slot)], writes=[("ps", bg)])
                        bu, psu = cx.nextps()
                        for k in range(8):
                            p.mm(lambda e, k=k, ff=ff, psu=psu, slot=slot, tt=tt: e.matmul(
                                psu[:], lhsT=wus[slot][:, k * 256 + ff * 128:k * 256 + (ff + 1) * 128],
                                rhs=h2[:, k, tt * 512:(tt + 1) * 512], start=(k == 0), stop=(k == 7)),
                                reads=h2k + [("wus", slot)], writes=[("ps", bu)])
                        s = (f * 2 + tt) % 2
                        p.act(lambda e, s=s, psg=psg: e.activation(out=sg[s][:], in_=psg[:], func=AF.Silu),
                              reads=[("ps", bg)], writes=[("sg", s)])
                        p.dve(lambda e, s=s, psu=psu, f=f, tt=tt: e.tensor_tensor(out=actb[:, f, tt * 512:(tt + 1) * 512],
                                                                               in0=sg[s][:], in1=psu[:], op=ALU.mult),
                              reads=[("sg", s), ("ps", bu)], writes=[("actb", f, tt)])
            for n in range(8):
                slot = n % 2
                for tt in range(2):
                    b, ps = cx.nextps()
                    for f in range(NF):
                        p.mm(lambda e, f=f, ps=ps, slot=slot, tt=tt: e.matmul(
                            ps[:], lhsT=wds[slot][:, f * 128:(f + 1) * 128], rhs=actb[:, f, tt * 512:(tt + 1) * 512],
                            start=(f == 0), stop=(f == NF - 1)),
                            reads=[("actb", f, tt), ("wds", slot)], writes=[("ps", b)])
                    p.dve(lambda e, n=n, ps=ps, tt=tt: e.tensor_tensor(out=xr[:, n, tt * 512:(tt + 1) * 512],
                                                                     in0=xr[:, n, tt * 512:(tt + 1) * 512], in1=ps[:], op=ALU.add),
                          reads=[("ps", b), ("xr", tt, n)], writes=[("xr", tt, n)])
                if n + 2 < 8:
                    load_d(n + 2, slot)
            for tt in range(2):
                t0 = hx + tt * 512
                if first or not fused:
                    outs.append(p.dma("sync", x2v[:, :, t0:t0 + 512], xr[:, :, tt * 512:(tt + 1) * 512], reads=xkeys(tt)))
                hi_ = tt % len(hn)
                norm_tile(xr[:, :, tt * 512:(tt + 1) * 512], xkeys(tt), gn, "gn", hn[hi_], ("hn", hi_))
                hdst = hnv[:, :, t0:t0 + 512] if not (fused and first) else cx.over["hn_tiles"][half * 2 + tt].rearrange("(c p) t -> p c t", p=128)
                gti = half * 2 + tt
                outs.append(p.dma("sync", hdst, hn[hi_][:], reads=[(("hn", hi_), c) for c in range(8)], writes=[("d_hn", gti)]))
                if fused and first:
                    p.cc(lambda e, a=cx.over["hn_tiles"][gti], b=cx.over["hn_all"][gti]: e.collective_compute(
                        "AllGather", ALU.bypass, replica_groups=[[0, 1, 2, 3], [4, 5, 6, 7]], ins=[a], outs=[b]),
                         reads=[("d_hn", gti)], writes=[("d_hnall", gti)])
        cx.last_outs = outs
        if standalone:
            p.emit(final_wait_ops=outs[-8:])
    return nc


def col8(v):
    return np.ascontiguousarray(np.asarray(v, np.float32).reshape(8, 128).T)


def slab_kn(w, ncol):
    K = w.shape[0] // 128
    ns = w.shape[1] // ncol
    a = np.asarray(w, np.float32).reshape(K, 128, ns, ncol).transpose(2, 1, 0, 3)
    return np.ascontiguousarray(a.reshape(ns, 128, K * ncol))


_NC_CACHE = {}


def get_nc(name, fn):
    if name not in _NC_CACHE:
        _NC_CACHE[name] = fn()
    return _NC_CACHE[name]


def run(nc, in_maps):
    res = run_bass_kernel_spmd(nc, in_maps, core_ids=list(range(8)))
    return res.results


def core_tokens(r):
    return r // 4, (r % 4) * T


def stage_p1(inp):
    x = np.asarray(inp["x"], np.float32)
    maps = []
    common = dict(
        gvec=col8(inp["norm_mix_g"][0]),
        w_in=np.ascontiguousarray(inp["a_w_in"][0], dtype=np.float32),
        convw=np.ascontiguousarray(np.concatenate([col8(inp["a_conv_w"][0][j]) for j in range(4)], axis=1)),
        convb=col8(inp["a_conv_b"][0]),
        w_r=np.ascontiguousarray(np.asarray(inp["a_w_r"][0], np.float32).reshape(D, 256)),
        w_i=np.ascontiguousarray(np.asarray(inp["a_w_i"][0], np.float32).reshape(D, 256)),
        b_r=col8(inp["a_b_r"][0]), b_i=col8(inp["a_b_i"][0]), lam=col8(inp["a_lambda"][0]),
        ident=np.eye(128, dtype=np.float32),
    )
    for r in range(8):
        b, t0 = core_tokens(r)
        xs = np.zeros((TH + T, D), np.float32)
        if t0 == 0:
            xs[TH:] = x[b, 0:T]
        else:
            xs[:] = x[b, t0 - TH:t0 + T]
        m = dict(common)
        m["xT"] = np.ascontiguousarray(xs.T)
        maps.append(m)
    return maps


def ffn_weights(inp, layer):
    return dict(
        g_ffn=col8(inp["norm_ffn_g"][layer]),
        wg=slab_kn(inp["ffn_w_gate"][layer], 256),
        wu=slab_kn(inp["ffn_w_up"][layer], 256),
        wd=slab_kn(inp["ffn_w_down"][layer], 128),
    )


def stage_p2(inp, r1):
    x = np.asarray(inp["x"], np.float32)
    common = ffn_weights(inp, 0)
    common["w_out"] = slab_kn(inp["a_w_out"][0], 128)
    common["g_next"] = col8(inp["norm_mix_g"][1])
    maps = []
    for r in range(8):
        b, t0 = core_tokens(r)
        q = r % 4
        m = dict(common)
        m["xT"] = np.ascontiguousarray(x[b, t0:t0 + T].T)
        m["hloc"], m["acum"], m["gate"] = r1[r]["hloc"], r1[r]["acum"], r1[r]["gate"]
        ends = np.zeros((128, 3, 2, 8), np.float32)
        for j in range(3):
            src = q - 3 + j
            if src >= 0:
                rr = b * 4 + src
                ends[:, j, 0, :] = col8(r1[rr]["hloc"][:, T - 1])
                ends[:, j, 1, :] = col8(r1[rr]["acum"][:, T - 1])
        m["ends"] = np.ascontiguousarray(ends.reshape(128, 48))
        maps.append(m)
    return maps


def build_p3(cx=None):
    standalone = cx is None
    if standalone:
        cx = Ctx()
    fused = cx.fused
    cx.prefix = "p3_" if fused else ""
    nc, p = cx.nc, cx.p
    hn = cx.din("hn", [D, S], BF16)
    wq = cx.din("wq", [128, 8 * 256])
    wk = cx.din("wk", [128, 8 * 256])
    wv = cx.din("wv", [128, 8 * 256])
    umat = cx.din("umat", [128, 128])
    masks = cx.din("masks", [128, 4 * 512])
    oT = cx.dout("oT", [256, S], BF16)
    NQ = S // 512
    if not fused:
        hnv = hn.rearrange("(c p) t -> p c t", p=128)
        hn_tile = lambda ti: hnv[:, :, ti * 512:(ti + 1) * 512]
        o_dst = lambda head, Qi: oT[head * 64:(head + 1) * 64, Qi * 512:(Qi + 1) * 512]
    else:
        hn_tile = lambda ti: hn[ti % 4][(ti // 4) * D:(ti // 4 + 1) * D, :].rearrange("(c p) t -> p c t", p=128)
        o_dst = lambda head, Qi: oT[Qi // 4][head * 64:(head + 1) * 64, (Qi % 4) * 512:(Qi % 4 + 1) * 512]

    with cx.phase("p3_"):
        cx.init_psum()
        sb = cx.sb
        hb = [sb("hb%d" % i, [128, 8, 512], BF16) for i in range(2)]
        wqs = sb("wqs", [128, 8 * 256], BF16)
        wks = sb("wks", [128, 8 * 256], BF16)
        wvs = sb("wvs", [128, 8 * 256], BF16)
        qs = sb("qs", [128, 2, S], BF16)
        kt = sb("kt", [128, 2, S], BF16)
        vv = sb("vv", [128, S // 128, 256], BF16)
        um = sb("um", [128, 128])
        onesr = sb("onesr", [128, 128])
        mk = sb("mk", [128, 4, 512])
        ob = [sb("ob%d" % i, [64, 512], BF16) for i in range(2)]

        p.dma("sync", mk[:], masks.rearrange("p (m f) -> p m f", m=4), writes=["mk"])
        umf = sb("umf", [128, 128])
        onesf = sb("onesf", [128, 128])
        p.dma("sync", umf[:], umat, writes=["umf"])
        p.dve(lambda e: e.memset(onesf[:], -1.0), writes=["onesf"])
        p.dve(lambda e: e.tensor_scalar(out=um[:].bitcast(F32R), in0=umf[:], scalar1=-1.0, scalar2=None, op0=ALU.mult), reads=["umf"], writes=["um"])
        p.dve(lambda e: e.tensor_copy(out=onesr[:].bitcast(F32R), in_=onesf[:]), reads=["onesf"], writes=["onesr"])
        p.dma("gpsimd", wqs[:], wq, writes=["wqs"])
        p.dma("gpsimd", wks[:], wk, writes=["wks"])
        p.dma("gpsimd", wvs[:], wv, writes=["wvs"])

        import os
        for oi_, ti in enumerate(sorted(range(int(os.environ.get("P3_PT", str(NQ)))), key=lambda a: (a % 4, a // 4))):
            hbuf = hb[oi_ % 2]
            hkey = ("hb", oi_ % 2)
            p.dma("sync", hbuf[:], hn_tile(ti), writes=[hkey])
            tsl = slice(ti * 512, (ti + 1) * 512)
            SKIP = os.environ.get("P3_SKIP", "")
            for pair in range(2):
                if "q" in SKIP:
                    break
                b, ps = cx.nextps()
                for k in range(8):
                    p.mm(lambda e, k=k, pair=pair, ps=ps, hbuf=hbuf: e.matmul(
                        ps[:], lhsT=wqs[:, k * 256 + pair * 128:k * 256 + (pair + 1) * 128], rhs=hbuf[:, k, :],
                        start=(k == 0), stop=(k == 7)), reads=[hkey, "wqs"], writes=[("ps", b)])
                p.act(lambda e, pair=pair, ps=ps, tsl=tsl: e.mul(out=qs[:, pair, tsl], in_=ps[:], mul=0.125),
                      reads=[("ps", b)], writes=[("qs", pair, ti)])
                b, ps = cx.nextps()
                for k in range(8):
                    p.mm(lambda e, k=k, pair=pair, ps=ps, hbuf=hbuf: e.matmul(
                        ps[:], lhsT=wks[:, k * 256 + pair * 128:k * 256 + (pair + 1) * 128], rhs=hbuf[:, k, :],
                        start=(k == 0), stop=(k == 7)), reads=[hkey, "wks"], writes=[("ps", b)])
                p.act(lambda e, pair=pair, ps=ps, tsl=tsl: e.copy(out=kt[:, pair, tsl], in_=ps[:]),
                      reads=[("ps", b)], writes=[("kt", pair, ti)])
            for tb in range(4):
                if "v" in SKIP:
                    break
                b, ps = cx.nextps()
                for k in range(8):
                    p.mm(lambda e, k=k, tb=tb, ps=ps, hbuf=hbuf: e.matmul(
                        ps[:, 0:256], lhsT=hbuf[:, k, tb * 128:(tb + 1) * 128], rhs=wvs[:, k * 256:(k + 1) * 256],
                        start=(k == 0), stop=(k == 7)), reads=[hkey, "wvs"], writes=[("ps", b)])
                p.dve(lambda e, tb=tb, ps=ps, ti=ti: e.tensor_copy(out=vv[:, ti * 4 + tb, :], in_=ps[:, 0:256]),
                      reads=[("ps", b)], writes=[("vv", ti)])

        outs = []
        um_r = um[:].bitcast(F32R)
        ones_r = onesr[:].bitcast(F32R)
        import os
        DBG_H = int(os.environ.get("P3_HEADS", "4"))
        DBG_Q = int(os.environ.get("P3_NQ", str(NQ)))
        LB = 1
        LC = 2
        NSP = LB + 2
        NWT = LC - LB + 2
        NRS = LB + 2
        etmp = [sb("etmp%d" % i, [128, 2, 512]) for i in range(NSP)]
        pw = [sb("pw%d" % i, [128, 2, 512]) for i in range(2)]
        spt = [sb("sptp%d" % i, [128, 2, 512]) for i in range(NSP)]
        wt = [sb("wtp%d" % i, [128, 2, 512], BF16) for i in range(NWT)]
        rsum = [sb("rsump%d" % i, [128, 2, 512]) for i in range(NRS)]
        one_t = sb("one_t", [128, 1])
        p.dve(lambda e: e.memset(one_t[:], 1.0), writes=["one"])
        psb = cx.psbig
        pairs = []
        gi = 0
        for Qi in range(DBG_Q):
            for pair in range(2):
                for hh in range(2):
                    head = pair * 2 + hh
                    if head >= DBG_H:
                        continue
                    nkb = 4 * (Qi + 1)
                    for kb in range(nkb - 1, -1, -2):
                        pairs.append(dict(pair=pair, hh=hh, head=head, Qi=Qi, kb=kb, first=(kb == nkb - 1), last=(kb == 1),
                                          diag=(kb - 4 * Qi >= 0), dpi=(0 if kb - 4 * Qi == 3 else 1), g=gi))
                    gi += 1
        n = len(pairs)
        for j, t in enumerate(pairs):
            t["zb"] = 2 * (j % 3)
            t["db"] = t["zb"]
            t["sp"] = j % NSP
            t["wt"] = j % NWT
            t["rs"] = j % NRS
            t["ob"] = 6 + t["g"] % 2
            t["rsp"] = None if t["first"] else pairs[j - 1]["rs"]

        def stage_a(t):
            pair, prt = t["pair"], slice(64 * t["hh"], 64 * t["hh"] + 64)
            qsl = slice(t["Qi"] * 512, (t["Qi"] + 1) * 512)
            zb, sp_t, e_t = t["zb"], spt[t["sp"]], etmp[t["sp"]]
            zkeys = [("ps", zb), ("ps", zb + 1)]
            for u in range(2):
                kb = t["kb"] - u
                ksl = slice(kb * 128, (kb + 1) * 128)
                p.mm(lambda e, u=u, ksl=ksl: e.matmul(psb[:, zb + u, :], lhsT=kt[prt, pair, ksl], rhs=qs[prt, pair, qsl],
                                                      start=True, stop=False),
                     reads=[("qs", pair, t["Qi"]), ("kt", pair, kb // 4)], writes=[("ps", zb + u)])
            p.act(lambda e: e.activation(out=e_t[:], in_=psb[:, zb:zb + 2, :], func=AF.Exp),
                  reads=zkeys, writes=[("etmp", t["sp"])])
            p.act(lambda e: e.activation(out=sp_t[:].bitcast(F32R), in_=e_t[:], func=AF.Ln, bias=1.0),
                  reads=[("etmp", t["sp"])], writes=[("spt", t["sp"])])
            if t["diag"]:
                d0 = 2 * t["dpi"]
                p.dve(lambda e: e.tensor_tensor(out=sp_t[:].bitcast(F32R), in0=sp_t[:], in1=mk[:, d0:d0 + 2, :], op=ALU.mult),
                      reads=[("spt", t["sp"]), "mk"], writes=[("spt", t["sp"])])
            rn = rsum[t["rs"]]
            if t["first"]:
                p.dve(lambda e: e.tensor_copy(out=rn[:, 0, :].bitcast(F32R), in_=sp_t[:, 0, :]),
                      reads=[("spt", t["sp"])], writes=[("rsum", t["rs"], 0)])
            else:
                rp = rsum[t["rsp"]]
                p.dve(lambda e: e.tensor_tensor(out=rn[:, 0, :].bitcast(F32R), in0=rp[:, 1, :], in1=sp_t[:, 0, :], op=ALU.add),
                      reads=[("spt", t["sp"]), ("rsum", t["rsp"], 1)], writes=[("rsum", t["rs"], 0)])
            if not t["last"]:
                p.dve(lambda e: e.tensor_tensor(out=rn[:, 1, :].bitcast(F32R), in0=rn[:, 0, :], in1=sp_t[:, 1, :], op=ALU.add),
                      reads=[("spt", t["sp"]), ("rsum", t["rs"], 0)], writes=[("rsum", t["rs"], 1)])

        def stage_b(t):
            db, sp_t, w_t, e_t = t["db"], spt[t["sp"]], wt[t["wt"]], etmp[t["sp"]]
            rn = rsum[t["rs"]]
            p_t = pw[t["wt"] % 2]
            pk = ("pw", t["wt"] % 2)
            for u in range(2):
                first_tile = t["first"] and u == 0
                p.mm(lambda e, u=u, first_tile=first_tile: e.matmul(psb[:, db + u, :], lhsT=um_r, rhs=sp_t[:, u, :].bitcast(F32R),
                                                                    start=False, stop=first_tile),
                     reads=[("spt", t["sp"]), "um"], writes=[("ps", db + u)])
            for u in range(2):
                first_tile = t["first"] and u == 0
                if not first_tile:
                    if u == 0:
                        rsrc, rkey = rsum[t["rsp"]][:, 1, :], ("rsum", t["rsp"], 1)
                    else:
                        rsrc, rkey = rn[:, 0, :], ("rsum", t["rs"], 0)
                    p.mm(lambda e, u=u, rsrc=rsrc: e.matmul(psb[:, db + u, :], lhsT=ones_r, rhs=rsrc.bitcast(F32R),
                                                            start=False, stop=True),
                         reads=[rkey, "onesr"], writes=[("ps", db + u)])
            p.act(lambda e: e.activation(out=w_t[:], in_=psb[:, db:db + 2, :], func=AF.Exp),
                  reads=[("ps", db), ("ps", db + 1)], writes=[("wt", t["wt"], 0), ("wt", t["wt"], 1)])
            if t["diag"]:
                d0 = 2 * t["dpi"]
                p.dve(lambda e: e.tensor_tensor(out=w_t[:], in0=w_t[:], in1=mk[:, d0:d0 + 2, :], op=ALU.mult),
                      reads=[("wt", t["wt"], 0), ("wt", t["wt"], 1), "mk"], writes=[("wt", t["wt"], 0), ("wt", t["wt"], 1)])

        def stage_c(t):
            ops_, w_t, head = cx.ps[t["ob"]], wt[t["wt"]], t["head"]
            for u in range(2):
                kb = t["kb"] - u
                p.mm(lambda e, u=u, kb=kb: e.matmul(ops_[0:64, :], lhsT=vv[:, kb, head * 64:(head + 1) * 64], rhs=w_t[:, u, :],
                                                    start=(t["first"] and u == 0), stop=(t["last"] and u == 1)),
                     reads=[("wt", t["wt"], u), ("vv", kb // 4)], writes=[("ps", t["ob"])])
            if t["last"]:
                obuf = ob[t["g"] % 2]
                p.act(lambda e: e.copy(out=obuf[:], in_=ops_[0:64, :]),
                      reads=[("ps", t["ob"])], writes=[("ob", t["g"] % 2)])
                qk = t["Qi"] // 4
                outs.append(p.dma("sync", o_dst(head, t["Qi"]), obuf[:], reads=[("ob", t["g"] % 2)], writes=[("d_o", qk, t["Qi"] % 4, head)]))
                if fused and t["Qi"] % 4 == 3 and head == 3:
                    p.cc(lambda e, a=cx.over["oT"][qk], b=cx.over["o_all"][qk]: e.collective_compute(
                        "AllGather", ALU.bypass, replica_groups=[[0, 1, 2, 3], [4, 5, 6, 7]], ins=[a], outs=[b]),
                         reads=[("d_o", qk, a, b) for a in range(4) for b in range(4)], writes=[("d_oall", qk)])

        for i in range(n + LC):
            if i < n:
                stage_a(pairs[i])
            if 0 <= i - LB < n:
                stage_b(pairs[i - LB])
            if 0 <= i - LC < n:
                stage_c(pairs[i - LC])
        cx.last_outs = outs
        if standalone:
            p.emit(final_wait_ops=outs[-8:])
    return nc


def attn_consts():
    pp = np.arange(128)[:, None]
    um = (pp >= np.arange(128)[None, :]).astype(np.float32)
    f = np.arange(512)[None, :]
    mk = np.concatenate([(f > (128 * m + pp)).astype(np.float32) for m in (3, 2, 1, 0)], axis=1)
    return um, np.ascontiguousarray(mk)


def stage_p3(inp, r2):
    um, mk = attn_consts()
    wqkv = np.asarray(inp["b_w_qkv"][0], np.float32)
    maps = []
    for r in range(8):
        b, hq = r // 4, r % 4
        hn = np.concatenate([r2[b * 4 + q]["hnT"] for q in range(4)], axis=1)
        cols = slice(hq * 256, (hq + 1) * 256)
        m = dict(
            hn=np.ascontiguousarray(hn),
            wq=slab_kn(wqkv[:, 0:D][:, cols], 256)[0],
            wk=slab_kn(wqkv[:, D:2 * D][:, cols], 256)[0],
            wv=slab_kn(wqkv[:, 2 * D:3 * D][:, cols], 256)[0],
            umat=um, masks=mk,
        )
        maps.append(m)
    return maps


def stage_p4(inp, r2, r3):
    common = ffn_weights(inp, 1)
    common["w_out"] = slab_kn(inp["b_w_out"][0], 128)
    common["g_next"] = col8(inp["final_g"])
    maps = []
    for r in range(8):
        b, q = r // 4, r % 4
        m = dict(common)
        m["xT"] = r2[r]["x2T"]
        m["oT"] = np.ascontiguousarray(
            np.concatenate([r3[b * 4 + hq]["oT"][:, q * T:(q + 1) * T] for hq in range(4)], axis=0))
        maps.append(m)
    return maps


def build_fused():
    cx = Ctx()
    cx.fused = True
    nc, p = cx.nc, cx.p
    xT = nc.dram_tensor("p1_xT", [D, TH + T], F32, kind="ExternalInput").ap()
    out = nc.dram_tensor("out", [D, T], F32, kind="ExternalOutput").ap()
    hloc = cx.dint("i_hloc", [D, T])
    acum = cx.dint("i_acum", [D, T])
    gate = cx.dint("i_gate", [D, T], BF16)
    ends_src = cx.dint("i_ends_src", [128, 16])
    ends_all = cx.dint("i_ends_all", [4 * 128, 16])
    x2 = cx.dint("i_x2", [D, T])
    hn_src = [cx.dint("i_hn_src%d" % k, [D, 512], BF16) for k in range(4)]
    hn_all = [cx.dint("i_hn_all%d" % k, [4 * D, 512], BF16) for k in range(4)]
    o_src = [cx.dint("i_o_src%d" % k, [256, T], BF16) for k in range(4)]
    o_all = [cx.dint("i_o_all%d" % k, [4 * 256, T], BF16) for k in range(4)]
    grp4 = [[0, 1, 2, 3], [4, 5, 6, 7]]
    grp8 = [list(range(8))]

    cx.over = dict(xT=xT, hloc=hloc, acum=acum, gate=gate, ends_src=ends_src)
    build_p1(cx)
    p.barrier()
    import os
    STOP = int(os.environ.get("FUSED_STOP", "99"))
    if STOP <= 0:
        p.emit()
        return nc
    p.cc(lambda e: e.collective_compute("AllGather", ALU.bypass, replica_groups=grp4, ins=[ends_src], outs=[ends_all]))
    p.barrier()
    if STOP <= 1:
        p.emit()
        return nc
    cx.over = dict(xT=xT[:, TH:TH + T], hloc=hloc, acum=acum, gate=gate, ends_all=ends_all, x2T=x2, hnT=x2, hn_tiles=hn_src, hn_all=hn_all)
    build_p24(True, False, cx)
    p.barrier()
    if STOP <= 2:
        p.emit()
        return nc
    p.barrier()
    if STOP <= 3:
        p.emit()
        return nc
    cx.over = dict(hn=hn_all, oT=o_src, o_all=o_all)
    build_p3(cx)
    p.barrier()
    if STOP <= 4:
        p.emit()
        return nc
    p.barrier()
    if STOP <= 5:
        p.emit()
        return nc
    cx.over = dict(xT=x2, x2T=x2, hnT=out, o_all=o_all)
    build_p24(False, True, cx)
    p.emit(final_wait_ops=cx.last_outs[-8:])
    return nc


def stage_fused(inp):
    m1 = stage_p1(inp)
    um, mk = attn_consts()
    wqkv = np.asarray(inp["b_w_qkv"][0], np.float32)
    f0 = ffn_weights(inp, 0)
    f0["w_out"] = slab_kn(inp["a_w_out"][0], 128)
    f0["g_next"] = col8(inp["norm_mix_g"][1])
    f1 = ffn_weights(inp, 1)
    f1["w_out"] = slab_kn(inp["b_w_out"][0], 128)
    f1["g_next"] = col8(inp["final_g"])
    maps = []
    for r in range(8):
        b, q = r // 4, r % 4
        m = {}
        for k, v in m1[r].items():
            m["p1_" + k] = v
        for k, v in f0.items():
            m["p2_" + k] = v
        for k, v in f1.items():
            m["p4_" + k] = v
        keep = np.zeros((128, 4), np.float32)
        for rr in range(q):
            keep[:, rr] = 1.0
        m["p2_km"] = keep
        m["p2_hm"] = 1.0 - keep
        sel = np.zeros((128, 4), np.float32)
        sel[:, q] = 1.0
        m["p4_sel"] = sel
        cols = slice(q * 256, (q + 1) * 256)
        m["p3_wq"] = slab_kn(wqkv[:, 0:D][:, cols], 256)[0]
        m["p3_wk"] = slab_kn(wqkv[:, D:2 * D][:, cols], 256)[0]
        m["p3_wv"] = slab_kn(wqkv[:, 2 * D:3 * D][:, cols], 256)[0]
        m["p3_umat"] = um
        m["p3_masks"] = mk
        maps.append(m)
    return maps


def kernel(**inp):
    inp = {k: np.asarray(v) for k, v in inp.items()}
    res = run(get_nc("fused", build_fused), stage_fused(inp))
    out = np.stack([np.concatenate([res[b * 4 + q]["out"].T for q in range(4)], axis=0) for b in range(2)])
    return np.ascontiguousarray(out.astype(np.float32))
```

```python
import contextlib
import numpy as np
import ml_dtypes
import concourse.bass as bass
import concourse.mybir as mybir
from concourse.bass_utils import run_bass_kernel_spmd

F32 = mybir.dt.float32
F32R = mybir.dt.float32r
BF16 = mybir.dt.bfloat16
AF = mybir.ActivationFunctionType
ALU = mybir.AluOpType
NPBF = ml_dtypes.bfloat16

D = 1024
DFF = 2816
NF = DFF // 128
T = 2048
S = 8192
TH = 4
EPS = 1e-6
COMPUTE = ("tensor", "vector", "scalar", "gpsimd")


class Prog:
    def __init__(self, nc, n_dma_sems=8):
        self.nc = nc
        self.ops = []
        self.state = {}
        self.n_dma_sems = n_dma_sems

    def add(self, eng, fn, reads=(), writes=(), dma=False):
        idx = len(self.ops)
        ops = self.ops
        deps = set()
        for k in reads:
            st = self.state.setdefault(k, [None, []])
            if st[0] is not None:
                deps.add(st[0])
        for k in writes:
            st = self.state.setdefault(k, [None, []])
            if st[0] is not None:
                deps.add(st[0])
            deps.update(st[1])
        for k in reads:
            st = self.state[k]
            if not dma:
                st[1] = [r for r in st[1] if ops[r]["dma"] or ops[r]["eng"] != eng]
            st[1].append(idx)
        for k in writes:
            self.state[k] = [idx, []]
        deps.discard(idx)
        ops.append(dict(eng=eng, fn=fn, deps=deps, dma=dma))
        return idx

    def mm(self, fn, reads=(), writes=()):
        return self.add("tensor", fn, reads, writes)

    def act(self, fn, reads=(), writes=()):
        return self.add("scalar", fn, reads, writes)

    def dve(self, fn, reads=(), writes=()):
        return self.add("vector", fn, reads, writes)

    def pool(self, fn, reads=(), writes=()):
        return self.add("gpsimd", fn, reads, writes)

    def dma(self, q, out, in_, reads=(), writes=()):
        return self.add(q, lambda e: e.dma_start(out=out, in_=in_), reads, writes, dma=True)

    def barrier(self, skip_cc=False):
        for e in ("sync", "scalar", "vector", "gpsimd", "tensor"):
            self.ops.append(dict(eng=e, fn=None, deps=set(), dma=False, barrier=True, skip_cc=skip_cc))

    def cc(self, fn, reads=(), writes=()):
        idx = self.add("gpsimd", fn, reads, writes)
        self.ops[idx]["cc"] = True
        return idx

    def emit(self, final_wait_ops=()):
        nc = self.nc
        ops = self.ops
        n = len(ops)

        def skip(d, e):
            od = ops[d]
            return od["eng"] == "tensor" and e == "tensor" and not od["dma"]

        has_dep = [False] * n
        last_op = {}
        for i, o in enumerate(ops):
            if o.get("barrier"):
                for e2 in COMPUTE:
                    if e2 in last_op:
                        has_dep[last_op[e2]] = True
                continue
            if o["eng"] in COMPUTE and not o["dma"] and not o.get("cc"):
                last_op[o["eng"]] = i
            for d in o["deps"]:
                if not skip(d, o["eng"]):
                    has_dep[d] = True
        for d in final_wait_ops:
            has_dep[d] = True
        engs = ("sync", "scalar", "vector", "gpsimd", "tensor")
        stack = contextlib.ExitStack()
        sems = {}
        for e in COMPUTE:
            sems[e] = stack.enter_context(nc.semaphore("s_" + e))
        dma_sems = {}
        for q in ("sync", "scalar", "gpsimd"):
            for j in range(self.n_dma_sems):
                dma_sems[(q, j)] = stack.enter_context(nc.semaphore("d_%s%d" % (q, j)))
        sems["cc"] = stack.enter_context(nc.semaphore("s_cc"))
        cnt = {k: 0 for k in list(sems) + list(dma_sems)}
        rr = {"sync": 0, "scalar": 0, "gpsimd": 0}
        sig = [None] * n
        waits = [None] * n
        waited = {e: {} for e in engs}
        for i, o in enumerate(ops):
            e = o["eng"]
            w = []

            def need(key, val):
                if val > 0 and waited[e].get(key, 0) < val:
                    waited[e][key] = val
                    w.append((key, val))

            if o.get("barrier"):
                for key in list(cnt):
                    if key != e and not (o.get("skip_cc") and key == "cc"):
                        need(key, cnt[key])
                waits[i] = w
                continue
            for d in sorted(o["deps"]):
                if skip(d, e):
                    continue
                need(*sig[d])
            if o.get("cc"):
                cnt["cc"] += 1
                sig[i] = ("cc", cnt["cc"])
            elif o["dma"]:
                j = rr[e]
                rr[e] = (j + 1) % self.n_dma_sems
                key = (e, j)
                need(key, cnt[key])
                cnt[key] += 16
                sig[i] = (key, cnt[key])
            elif has_dep[i]:
                cnt[e] += 1
                sig[i] = (e, cnt[e])
            waits[i] = w
        allsems = dict(sems)
        allsems.update(dma_sems)
        self.max_counts = dict(cnt)
        final = [sig[d] for d in final_wait_ops]

        with stack:
            with nc.Block() as block:
                def make(ename):
                    def body(eng):
                        for i, o in enumerate(ops):
                            if o["eng"] != ename:
                                continue
                            for key, val in waits[i]:
                                eng.wait_ge(allsems[key], val)
                            if o["fn"] is None:
                                continue
                            inst = o["fn"](eng)
                            if sig[i] is not None:
                                inst.then_inc(allsems[sig[i][0]], 16 if o["dma"] else 1)
                        if ename == "sync":
                            for key, val in final:
                                eng.wait_ge(allsems[key], val)
                    return body

                block.sync(make("sync"))
                block.scalar(make("scalar"))
                block.vector(make("vector"))
                block.gpsimd(make("gpsimd"))
                block.tensor(make("tensor"))


class Ctx:
    def __init__(self):
        self.nc = bass.Bass("TRN2", target_bir_lowering=False)
        self.stack = contextlib.ExitStack()
        self.gstack = contextlib.ExitStack()
        self.p = Prog(self.nc)
        self.ps = None
        self.psi = 0
        self.prefix = ""
        self.over = {}
        self.fused = False

    def din(self, name, shape, dt=F32):
        if name in self.over:
            return self.over[name]
        return self.nc.dram_tensor(self.prefix + name, list(shape), dt, kind="ExternalInput").ap()

    def dout(self, name, shape, dt=F32):
        if name in self.over:
            return self.over[name]
        return self.nc.dram_tensor(self.prefix + name, list(shape), dt, kind="ExternalOutput").ap()

    def dint(self, name, shape, dt=F32):
        return self.nc.dram_tensor(name, list(shape), dt, addr_space="Local", kind="Internal").ap()

    def sb(self, name, shape, dt=F32):
        return self.stack.enter_context(self.nc.sbuf_tensor(self.prefix + "s_" + name, list(shape), dt))

    def init_psum(self):
        if self.ps is None:
            self.psbig = self.gstack.enter_context(self.nc.psum_tensor("psbig", [128, 8, 512], F32))
            self.ps = [self.psbig[:, i, :] for i in range(8)]

    def phase(self, prefix):
        self.stack = contextlib.ExitStack()
        return self.stack

    def nextps(self):
        b = self.psi
        self.psi = (b + 1) % 8
        return b, self.ps[b]


def emit_norm(cx, x3, xkey, N, g_sb, h3, hkey, sq, rs, rstd, ones):
    p = cx.p
    p.act(lambda e: e.activation(out=sq[:, :, :N], in_=x3, func=AF.Square), reads=[xkey], writes=["sq"])
    b, ps = cx.nextps()
    for c in range(8):
        p.mm(lambda e, c=c: e.matmul(ps[:, :N], lhsT=ones[:], rhs=sq[:, c, :N], start=(c == 0), stop=(c == 7)),
             reads=["sq", "ones"], writes=[("ps", b)])
    p.act(lambda e, eps_t=cx.eps_t: e.activation(out=rs[:, :N], in_=ps[:, :N], func=AF.Sqrt, scale=1.0 / D, bias=eps_t[:, 0:1]),
          reads=[("ps", b), "eps"], writes=["rs"])
    p.dve(lambda e: e.reciprocal(out=rstd[:, :N], in_=rs[:, :N]), reads=["rs"], writes=["rstd"])
    for c in range(8):
        eng = "vector"
        p.add(eng, lambda e, c=c: e.scalar_tensor_tensor(out=h3[:, c, :], in0=x3[:, c, :], scalar=g_sb[:, c:c + 1],
                                                     in1=rstd[:, :N], op0=ALU.mult, op1=ALU.mult),
              reads=[xkey, "rstd", "gvec"], writes=[(hkey, c)])


def load_w_bf16(cx, dst, src, K, key, nsplit=None):
    v = src.rearrange("(k p) n -> p k n", p=128)
    for k in range(K):
        cx.p.dma("gpsimd", dst[:, k, :], v[:, k, :], writes=[(key, k)])


def build_p1(cx=None):
    standalone = cx is None
    if standalone:
        cx = Ctx()
    cx.prefix = "p1_" if cx.fused else ""
    nc, p = cx.nc, cx.p
    xT = cx.din("xT", [D, TH + T])
    gvec = cx.din("gvec", [128, 8])
    w_in = cx.din("w_in", [D, 2 * D])
    convw = cx.din("convw", [128, 32])
    convb = cx.din("convb", [128, 8])
    w_r = cx.din("w_r", [D, 256])
    w_i = cx.din("w_i", [D, 256])
    b_r = cx.din("b_r", [128, 8])
    b_i = cx.din("b_i", [128, 8])
    lam = cx.din("lam", [128, 8])
    ident = cx.din("ident", [128, 128])
    hloc = cx.dout("hloc", [D, T])
    acum = cx.dout("acum", [D, T])
    gate = cx.dout("gate", [D, T], BF16)

    xTv = xT.rearrange("(c p) t -> p c t", p=128)
    hlv = hloc.rearrange("(c p) t -> p c t", p=128)
    acv = acum.rearrange("(c p) t -> p c t", p=128)
    gtv = gate.rearrange("(c p) t -> p c t", p=128)

    with cx.phase("p1_"):
        cx.init_psum()
        sb = cx.sb
        xt = [sb("xt%d" % i, [128, 8, 512]) for i in range(1)]
        sq = sb("sq", [128, 8, 512], BF16)
        rs = sb("rs", [128, 512])
        rstd = sb("rstd", [128, 512])
        h = sb("h", [128, 8, 512], BF16)
        win = sb("win", [128, 8, 2 * D], BF16)
        wr = sb("wr", [128, 8, 256], BF16)
        wi = sb("wi", [128, 8, 256], BF16)
        diag = sb("diag", [128, 32, 128], BF16)
        identf = sb("identf", [128, 128])
        ones = sb("ones", [128, 128], BF16)
        zeros = sb("zeros", [128, 512])
        cx.eps_t = sb("eps", [128, 1])
        g_sb = sb("g_sb", [128, 8])
        cw = sb("cw", [128, 32])
        cb = sb("cb", [128, 8])
        br = sb("br", [128, 8])
        bi = sb("bi", [128, 8])
        lm = sb("lm", [128, 8])
        c8 = sb("c8", [128, 8])
        c16 = sb("c16", [128, 8])
        xb = sb("xb", [128, 8, TH + T], BF16)
        gt = [sb("gt%d" % i, [128, 8, 512], BF16) for i in range(2)]
        xc = sb("xc", [128, 2, 512])
        xcb = sb("xcb", [128, 2, 512], BF16)
        rt = [sb("rt%d" % i, [128, 512]) for i in range(2)]
        it = [sb("it%d" % i, [128, 512]) for i in range(2)]
        a2 = [sb("a2%d" % i, [128, 512]) for i in range(2)]
        at = sb("at", [128, 2, 512])
        ut = sb("ut", [128, 2, 512])
        hl = [sb("hl%d" % i, [128, 8, 512]) for i in range(1)]
        ac = [sb("ac%d" % i, [128, 8, 512]) for i in range(1)]
        hlast = sb("hlast", [128, 8])
        alast = sb("alast", [128, 8])

        for dst, src, key in ((g_sb, gvec, "gvec"), (cw, convw, "cw"), (cb, convb, "cb"), (br, b_r, "br"),
                              (bi, b_i, "bi"), (lm, lam, "lm"), (identf, ident, "identf")):
            p.dma("sync", dst[:], src, writes=[key])
        p.dve(lambda e: e.memset(ones[:], 1.0), writes=["ones"])
        p.dve(lambda e: e.memset(zeros[:], 0.0), writes=["zeros"])
        p.dve(lambda e, eps_t=cx.eps_t: e.memset(eps_t[:], EPS), writes=["eps"])
        load_w_bf16(cx, win, w_in, 8, "win")
        load_w_bf16(cx, wr, w_r, 8, "wr")
        load_w_bf16(cx, wi, w_i, 8, "wi")
        p.act(lambda e: e.activation(out=c8[:], in_=lm[:], func=AF.Sigmoid), reads=["lm"], writes=["c8"])
        p.act(lambda e: e.activation(out=c8[:], in_=c8[:], func=AF.Ln), reads=["c8"], writes=["c8"])
        p.act(lambda e: e.mul(out=c16[:], in_=c8[:], mul=16.0), reads=["c8"], writes=["c16"])
        p.act(lambda e: e.mul(out=c8[:], in_=c8[:], mul=8.0), reads=["c8"], writes=["c8"])
        for jc in range(32):
            p.dve(lambda e, jc=jc: e.tensor_scalar(out=diag[:, jc, :], in0=identf[:], scalar1=cw[:, jc:jc + 1],
                                                 scalar2=None, op0=ALU.mult),
                  reads=["identf", "cw"], writes=[("diag", jc)])

        winkeys = [("win", k) for k in range(8)]
        wrkeys = [("wr", k) for k in range(8)]
        wikeys = [("wi", k) for k in range(8)]

        def do_tile(ti):
            halo = ti < 0
            N = TH if halo else 512
            t0 = 0 if halo else TH + ti * 512
            xbuf = xt[0]
            xkey = ("xt", 0)
            x3 = xbuf[:, :, :N]
            p.dma("sync", x3, xTv[:, :, t0:t0 + N], writes=[xkey])
            emit_norm(cx, x3, xkey, N, g_sb, h[:, :, :N], "h", sq, rs, rstd, ones)
            hkeys = [("h", c) for c in range(8)]
            gbuf = gt[ti % 2]
            gkey = ("gt", ti % 2)
            for n in range(0 if not halo else 8, 16):
                b, ps = cx.nextps()
                for k in range(8):
                    p.mm(lambda e, k=k, n=n, ps=ps: e.matmul(ps[:, :N], lhsT=win[:, k, n * 128:(n + 1) * 128],
                                                          rhs=h[:, k, :N], start=(k == 0), stop=(k == 7)),
                         reads=hkeys + winkeys, writes=[("ps", b)])
                if n < 8:
                    p.act(lambda e, n=n, ps=ps: e.activation(out=gbuf[:, n, :], in_=ps[:, :N], func=AF.Gelu_apprx_tanh),
                          reads=[("ps", b)], writes=[gkey])
                else:
                    p.act(lambda e, n=n, ps=ps: e.copy(out=xb[:, n - 8, t0:t0 + N], in_=ps[:, :N]),
                          reads=[("ps", b)], writes=[("xb", n - 8, ti)])
            if halo:
                return
            p.dma("sync", gtv[:, :, ti * 512:(ti + 1) * 512], gbuf[:], reads=[gkey])
            hbuf, abuf = hl[0], ac[0]
            for blk in range(4):
                for cc in range(2):
                    c = 2 * blk + cc
                    b, ps = cx.nextps()
                    for j in range(4):
                        p.mm(lambda e, j=j, c=c, ps=ps: e.matmul(ps[:], lhsT=diag[:, j * 8 + c, :],
                                                              rhs=xb[:, c, t0 - 3 + j:t0 - 3 + j + 512],
                                                              start=(j == 0), stop=(j == 3)),
                             reads=[("xb", c, ti), ("xb", c, ti - 1), ("diag", j * 8 + c)], writes=[("ps", b)])
                    p.act(lambda e, c=c, cc=cc, ps=ps: e.activation(out=xc[:, cc, :], in_=ps[:], func=AF.Identity,
                                                                bias=cb[:, c:c + 1]),
                          reads=[("ps", b), "cb"], writes=[("xc", cc)])
                    p.dve(lambda e, cc=cc: e.tensor_copy(out=xcb[:, cc, :], in_=xc[:, cc, :]),
                          reads=[("xc", cc)], writes=[("xcb", cc)])
                for cc in range(2):
                    n = 2 * blk + cc
                    rb, ib, ab = rt[cc], it[cc], a2[cc]
                    b, ps = cx.nextps()
                    for kk in range(2):
                        p.mm(lambda e, kk=kk, cc=cc, ps=ps, blk=blk: e.matmul(ps[:], lhsT=wr[:, blk * 2 + kk, cc * 128:(cc + 1) * 128],
                                                                  rhs=xcb[:, kk, :], start=(kk == 0), stop=(kk == 1)),
                             reads=[("xcb", 0), ("xcb", 1)] + wrkeys, writes=[("ps", b)])
                    p.act(lambda e, n=n, ps=ps, rb=rb: e.activation(out=rb[:], in_=ps[:], func=AF.Sigmoid, bias=br[:, n:n + 1]),
                          reads=[("ps", b), "br"], writes=[("rt", cc)])
                    b, ps = cx.nextps()
                    for kk in range(2):
                        p.mm(lambda e, kk=kk, cc=cc, ps=ps, blk=blk: e.matmul(ps[:], lhsT=wi[:, blk * 2 + kk, cc * 128:(cc + 1) * 128],
                                                                  rhs=xcb[:, kk, :], start=(kk == 0), stop=(kk == 1)),
                             reads=[("xcb", 0), ("xcb", 1)] + wikeys, writes=[("ps", b)])
                    p.act(lambda e, n=n, ps=ps, ib=ib: e.activation(out=ib[:], in_=ps[:], func=AF.Sigmoid, bias=bi[:, n:n + 1]),
                          reads=[("ps", b), "bi"], writes=[("it", cc)])
                for cc in range(2):
                    n = 2 * blk + cc
                    rb, ib, ab = rt[cc], it[cc], a2[cc]
                    p.act(lambda e, n=n, rb=rb, cc=cc: e.activation(out=at[:, cc, :], in_=rb[:], func=AF.Exp, scale=c8[:, n:n + 1]),
                          reads=[("rt", cc), "c8"], writes=[("at", cc)])
                    p.act(lambda e, n=n, rb=rb, ab=ab: e.activation(out=ab[:], in_=rb[:], func=AF.Exp, scale=c16[:, n:n + 1]),
                          reads=[("rt", cc), "c16"], writes=[("a2", cc)])
                for cc in range(2):
                    n = 2 * blk + cc
                    rb, ib, ab = rt[cc], it[cc], a2[cc]
                    p.act(lambda e, ab=ab, one_t=cx.one_t: e.activation(out=ab[:], in_=ab[:], func=AF.Sqrt, scale=-1.0, bias=one_t[:, 0:1]),
                          reads=[("a2", cc), "one"], writes=[("a2", cc)])
                    p.pool(lambda e, ib=ib, cc=cc: e.tensor_tensor(out=ib[:], in0=ib[:], in1=xc[:, cc, :], op=ALU.mult),
                           reads=[("it", cc), ("xc", cc)], writes=[("it", cc)])
                    p.dve(lambda e, cc=cc, ib=ib, ab=ab: e.tensor_tensor(out=ut[:, cc, :], in0=ib[:], in1=ab[:], op=ALU.mult),
                          reads=[("it", cc), ("a2", cc)], writes=[("ut", cc)])
                    if ti == 0:
                        hinit, ainit = 0.0, 1.0
                        ir = []
                    else:
                        hinit, ainit = hlast[:, n:n + 1], alast[:, n:n + 1]
                        ir = [("hlast", n), ("alast", n)]
                    p.dve(lambda e, n=n, cc=cc, hinit=hinit: e.tensor_tensor_scan(out=hbuf[:, n, :], data0=at[:, cc, :], data1=ut[:, cc, :],
                                                                        initial=hinit, op0=ALU.mult, op1=ALU.add),
                          reads=[("at", cc), ("ut", cc)] + ir, writes=[("hl", 0, n)])
                    p.dve(lambda e, n=n, cc=cc, ainit=ainit: e.tensor_tensor_scan(out=abuf[:, n, :], data0=at[:, cc, :], data1=zeros[:],
                                                                        initial=ainit, op0=ALU.mult, op1=ALU.add),
                          reads=[("at", cc), "zeros"] + ir, writes=[("ac", 0, n)])
                    p.dve(lambda e, n=n: e.tensor_copy(out=hlast[:, n:n + 1], in_=hbuf[:, n, 511:512]),
                          reads=[("hl", 0, n)], writes=[("hlast", n)])
                    p.dve(lambda e, n=n: e.tensor_copy(out=alast[:, n:n + 1], in_=abuf[:, n, 511:512]),
                          reads=[("ac", 0, n)], writes=[("alast", n)])
            p.dma("sync", hlv[:, :, ti * 512:(ti + 1) * 512], hbuf[:], reads=[("hl", 0, n) for n in range(8)])
            return p.dma("sync", acv[:, :, ti * 512:(ti + 1) * 512], abuf[:], reads=[("ac", 0, n) for n in range(8)])

        cx.one_t = sb("one_t", [128, 1])
        p.dve(lambda e, one_t=cx.one_t: e.memset(one_t[:], 1.0), writes=["one"])
        do_tile(-1)
        last = None
        for ti in range(4):
            last = do_tile(ti)
        if cx.fused:
            es = cx.over["ends_src"]
            p.dma("sync", es[:, 0:8], hlast[:], reads=[("hlast", n) for n in range(8)])
            p.dma("sync", es[:, 8:16], alast[:], reads=[("alast", n) for n in range(8)])
        if standalone:
            outs = [i for i, o in enumerate(p.ops) if o["dma"] and o["eng"] == "sync"]
            p.emit(final_wait_ops=outs[-12:])
    return nc


def build_p24(first, out_f32, cx=None):
    standalone = cx is None
    if standalone:
        cx = Ctx()
    fused = cx.fused
    cx.prefix = ("p2_" if first else "p4_") if fused else ""
    nc, p = cx.nc, cx.p
    xT = cx.din("xT", [D, T])
    if first:
        hloc = cx.din("hloc", [D, T])
        acum = cx.din("acum", [D, T])
        gate = cx.din("gate", [D, T], BF16)
        if not fused:
            ends = cx.din("ends", [128, 48])
        else:
            km_d = cx.din("km", [128, 4])
            hm_d = cx.din("hm", [128, 4])
    else:
        if not fused:
            oT = cx.din("oT", [D, T], BF16)
        else:
            sel_d = cx.din("sel", [128, 4])
    w_out = cx.din("w_out", [8, 128, 8 * 128])
    g_ffn = cx.din("g_ffn", [128, 8])
    g_next = cx.din("g_next", [128, 8])
    wg = cx.din("wg", [NF // 2, 128, 8 * 256])
    wu = cx.din("wu", [NF // 2, 128, 8 * 256])
    wd = cx.din("wd", [8, 128, NF * 128])
    x2T = cx.dout("x2T", [D, T])
    hnT = cx.dout("hnT", [D, T], F32 if out_f32 else BF16)

    v3 = lambda a: a.rearrange("(c p) t -> p c t", p=128)
    xTv, x2v, hnv = v3(xT), v3(x2T), v3(hnT)

    with cx.phase("p2_" if first else "p4_"):
        cx.init_psum()
        sb = cx.sb
        xr = sb("xr", [128, 8, 1024])
        y = [sb("y%d" % i, [128, 8, 512], BF16) for i in range(2)]
        sq = sb("sq", [128, 8, 512], BF16)
        rs = sb("rs", [128, 512])
        rstd = sb("rstd", [128, 512])
        h2 = sb("h2", [128, 8, 1024], BF16)
        actb = sb("actb", [128, NF, 1024], BF16)
        sg = [sb("sg%d" % i, [128, 512]) for i in range(2)]
        wo = sb("wo", [128, 8, 8 * 128], BF16)
        wgs = [sb("wgs%d" % i, [128, 8 * 256], BF16) for i in range(3)]
        wus = [sb("wus%d" % i, [128, 8 * 256], BF16) for i in range(3)]
        wds = [sb("wds%d" % i, [128, NF * 128], BF16) for i in range(2)]
        hn = [sb("hn%d" % i, [128, 8, 512], F32 if out_f32 else BF16) for i in range(1 if out_f32 else 2)]
        ones = sb("ones", [128, 128], BF16)
        cx.eps_t = sb("eps", [128, 1])
        gf = sb("gf", [128, 8])
        gn = sb("gn", [128, 8])
        p.dma("sync", gf[:], g_ffn, writes=["gf"])
        p.dma("sync", gn[:], g_next, writes=["gn"])
        p.dve(lambda e: e.memset(ones[:], 1.0), writes=["ones"])
        p.dve(lambda e, eps_t=cx.eps_t: e.memset(eps_t[:], EPS), writes=["eps"])
        for n in range(8):
            p.dma("gpsimd", wo[:, n, :], w_out[n], writes=[("wo", n)])
        if first:
            en = sb("en", [128, 48])
            carry = sb("carry", [128, 8])
            ctmp = sb("ctmp", [128, 8])
            hlc = [sb("hlc%d" % i, [128, 512]) for i in range(2)]
            acc = [sb("acc%d" % i, [128, 512]) for i in range(2)]
            gtc = [sb("gtc%d" % i, [128, 512], BF16) for i in range(2)]
            hlv, acv, gtv = v3(hloc), v3(acum), v3(gate)
            p.dve(lambda e: e.memset(carry[:], 0.0), writes=["carry"])
            if fused:
                en_all = sb("en_all", [128, 4, 16])
                km = sb("km", [128, 4])
                hm = sb("hm", [128, 4])
                ap_t = sb("ap_t", [128, 8])
                hp_t = sb("hp_t", [128, 8])
                p.dma("sync", en_all[:], cx.over["ends_all"].rearrange("(r p) f -> p r f", p=128), reads=["d_ends_all"], writes=["en_all"])
                p.dma("sync", km[:], km_d, writes=["km"])
                p.dma("sync", hm[:], hm_d, writes=["hm"])
                for r in range(4):
                    p.dve(lambda e, r=r: e.tensor_scalar(out=ap_t[:], in0=en_all[:, r, 8:16], scalar1=km[:, r:r + 1],
                                                       scalar2=hm[:, r:r + 1], op0=ALU.mult, op1=ALU.add),
                          reads=["en_all", "km", "hm"], writes=["ap_t"])
                    p.dve(lambda e, r=r: e.tensor_scalar(out=hp_t[:], in0=en_all[:, r, 0:8], scalar1=km[:, r:r + 1],
                                                       scalar2=None, op0=ALU.mult),
                          reads=["en_all", "km"], writes=["hp_t"])
                    p.dve(lambda e: e.tensor_tensor(out=ctmp[:], in0=ap_t[:], in1=carry[:], op=ALU.mult),
                          reads=["ap_t", "carry"], writes=["ctmp"])
                    p.dve(lambda e: e.tensor_tensor(out=carry[:], in0=ctmp[:], in1=hp_t[:], op=ALU.add),
                          reads=["ctmp", "hp_t"], writes=["carry"])
            else:
                p.dma("sync", en[:], ends, writes=["en"])
            for j in range(0 if fused else 3):
                p.dve(lambda e, j=j: e.tensor_tensor(out=ctmp[:], in0=en[:, j * 16 + 8:j * 16 + 16], in1=carry[:], op=ALU.mult),
                      reads=["en", "carry"], writes=["ctmp"])
                p.dve(lambda e, j=j: e.tensor_tensor(out=carry[:], in0=ctmp[:], in1=en[:, j * 16:j * 16 + 8], op=ALU.add),
                      reads=["en", "ctmp"], writes=["carry"])
        elif not fused:
            oTv = v3(oT)
        else:
            oallv = [a.rearrange("(c p) t -> p c t", p=128) for a in cx.over["o_all"]]
            sel = sb("sel", [128, 4])
            cand = [sb("cand%d" % i, [128, 8, 512], BF16) for i in range(2)]
            p.dma("sync", sel[:], sel_d, writes=["sel"])

        def norm_tile(x3, xkey, g_sb, gkey, h3, hkey):
            N = 512
            p.act(lambda e: e.activation(out=sq[:], in_=x3, func=AF.Square), reads=xkey, writes=["sq"])
            b, ps = cx.nextps()
            for c in range(8):
                p.mm(lambda e, c=c: e.matmul(ps[:], lhsT=ones[:], rhs=sq[:, c, :], start=(c == 0), stop=(c == 7)),
                     reads=["sq", "ones"], writes=[("ps", b)])
            p.act(lambda e, eps_t=cx.eps_t: e.activation(out=rs[:], in_=ps[:], func=AF.Sqrt, scale=1.0 / D, bias=eps_t[:, 0:1]),
                  reads=[("ps", b), "eps"], writes=["rs"])
            p.dve(lambda e: e.reciprocal(out=rstd[:], in_=rs[:]), reads=["rs"], writes=["rstd"])
            for c in range(8):
                p.dve(lambda e, c=c: e.scalar_tensor_tensor(out=h3[:, c, :], in0=x3[:, c, :], scalar=g_sb[:, c:c + 1],
                                                          in1=rstd[:], op0=ALU.mult, op1=ALU.mult),
                      reads=list(xkey) + ["rstd", gkey], writes=[(hkey, c)])

        wokeys = [("wo", n) for n in range(8)]
        cidx = [0]

        def load_gu(f2, slot):
            p.dma("gpsimd", wgs[slot][:], wg[f2], writes=[("wgs", slot)])
            p.dma("gpsimd", wus[slot][:], wu[f2], writes=[("wus", slot)])

        def load_d(n, slot):
            p.dma("gpsimd", wds[slot][:], wd[n], writes=[("wds", slot)])

        outs = []
        for half in range(2):
            hx = half * 1024
            xkeys = lambda tt: [("xr", tt, c) for c in range(8)]
            for tt in range(2):
                p.dma("sync", xr[:, :, tt * 512:(tt + 1) * 512], xTv[:, :, hx + tt * 512:hx + (tt + 1) * 512],
                      writes=xkeys(tt))
            load_gu(0, 0)
            load_gu(1, 1)
            for tt in range(2):
                t0 = hx + tt * 512
                yb = y[tt]
                ykeys = [("y", tt, c) for c in range(8)]
                if first:
                    for c in range(8):
                        s = cidx[0] % 2
                        cidx[0] += 1
                        p.dma("sync", hlc[s][:], hlv[:, c, t0:t0 + 512], writes=[("hlc", s)])
                        p.dma("sync", acc[s][:], acv[:, c, t0:t0 + 512], writes=[("acc", s)])
                        p.dma("sync", gtc[s][:], gtv[:, c, t0:t0 + 512], writes=[("gtc", s)])
                        p.dve(lambda e, s=s, c=c: e.scalar_tensor_tensor(out=hlc[s][:], in0=acc[s][:], scalar=carry[:, c:c + 1],
                                                                      in1=hlc[s][:], op0=ALU.mult, op1=ALU.add),
                              reads=[("hlc", s), ("acc", s), "carry"], writes=[("hlc", s)])
                        p.pool(lambda e, s=s, c=c, yb=yb: e.tensor_tensor(out=yb[:, c, :], in0=hlc[s][:], in1=gtc[s][:], op=ALU.mult),
                               reads=[("hlc", s), ("gtc", s)], writes=[("y", tt, c)])
                elif not fused:
                    p.dma("sync", yb[:], oTv[:, :, t0:t0 + 512], writes=ykeys)
                else:
                    for j in range(4):
                        cb = cand[j % 2]
                        ck = ("cand", j % 2)
                        p.dma("sync", cb[:], oallv[j][:, :, t0:t0 + 512], reads=[("d_oall", j)], writes=[ck])
                        if j == 0:
                            p.dve(lambda e, cb=cb, yb=yb: e.tensor_scalar(out=yb[:], in0=cb[:], scalar1=sel[:, 0:1], scalar2=None,
                                                                       op0=ALU.mult),
                                  reads=[ck, "sel"], writes=ykeys)
                        else:
                            p.dve(lambda e, cb=cb, yb=yb, j=j: e.scalar_tensor_tensor(out=yb[:], in0=cb[:], scalar=sel[:, j:j + 1],
                                                                                   in1=yb[:], op0=ALU.mult, op1=ALU.add),
                                  reads=[ck, "sel"] + ykeys, writes=ykeys)
                for n in range(8):
                    b, ps = cx.nextps()
                    for k in range(8):
                        p.mm(lambda e, k=k, n=n, ps=ps, yb=yb: e.matmul(ps[:], lhsT=wo[:, n, k * 128:(k + 1) * 128], rhs=yb[:, k, :],
                                                                     start=(k == 0), stop=(k == 7)),
                             reads=ykeys + [("wo", n)], writes=[("ps", b)])
                    p.dve(lambda e, n=n, ps=ps, tt=tt: e.tensor_tensor(out=xr[:, n, tt * 512:(tt + 1) * 512],
                                                                     in0=xr[:, n, tt * 512:(tt + 1) * 512], in1=ps[:], op=ALU.add),
                          reads=[("ps", b), ("xr", tt, n)], writes=[("xr", tt, n)])
                norm_tile(xr[:, :, tt * 512:(tt + 1) * 512], xkeys(tt), gf, "gf", h2[:, :, tt * 512:(tt + 1) * 512], ("h2", tt))
            for f2 in range(NF // 2):
                slot = f2 % 3
                if f2 + 2 < NF // 2:
                    load_gu(f2 + 2, (f2 + 2) % 3)
                if f2 == NF // 2 - 2:
                    load_d(0, 0)
                if f2 == NF // 2 - 1:
                    load_d(1, 1)
                for ff in range(2):
                    f = 2 * f2 + ff
                    for tt in range(2):
                        h2k = [(("h2", tt), c) for c in range(8)]
                        bg, psg = cx.nextps()
                        for k in range(8):
                            p.mm(lambda e, k=k, ff=ff, psg=psg, slot=slot, tt=tt: e.matmul(
                                psg[:], lhsT=wgs[slot][:, k * 256 + ff * 128:k * 256 + (ff + 1) * 128],
                                rhs=h2[:, k, tt * 512:(tt + 1) * 512], start=(k == 0), stop=(k == 7)),
                                reads=h2k + [("wgs", slot)], writes=[("ps", bg)])
                        bu, psu = cx.nextps()
                        for k in range(8):
                            p.mm(lambda e, k=k, ff=ff, psu=psu, slot=slot, tt=tt: e.matmul(
                                psu[:], lhsT=wus[slot][:, k * 256 + ff * 128:k * 256 + (ff + 1) * 128],
                                rhs=h2[:, k, tt * 512:(tt + 1) * 512], start=(k == 0), stop=(k == 7)),
                                reads=h2k + [("wus", slot)], writes=[("ps", bu)])
                        s = (f * 2 + tt) % 2
                        p.act(lambda e, s=s, psg=psg: e.activation(out=sg[s][:], in_=psg[:], func=AF.Silu),
                              reads=[("ps", bg)], writes=[("sg", s)])
                        p.dve(lambda e, s=s, psu=psu, f=f, tt=tt: e.tensor_tensor(out=actb[:, f, tt * 512:(tt + 1) * 512],
                                                                               in0=sg[s][:], in1=psu[:], op=ALU.mult),
                              reads=[("sg", s), ("ps", bu)], writes=[("actb", f, tt)])
            for n in range(8):
                slot = n % 2
                for tt in range(2):
                    b, ps = cx.nextps()
                    for f in range(NF):
                        p.mm(lambda e, f=f, ps=ps, slot=slot, tt=tt: e.matmul(
                            ps[:], lhsT=wds[slot][:, f * 128:(f + 1) * 128], rhs=actb[:, f, tt * 512:(tt + 1) * 512],
                            start=(f == 0), stop=(f == NF - 1)),
                            reads=[("actb", f, tt), ("wds", slot)], writes=[("ps", b)])
                    p.dve(lambda e, n=n, ps=ps, tt=tt: e.tensor_tensor(out=xr[:, n, tt * 512:(tt + 1) * 512],
                                                                     in0=xr[:, n, tt * 512:(tt + 1) * 512], in1=ps[:], op=ALU.add),
                          reads=[("ps", b), ("xr", tt, n)], writes=[("xr", tt, n)])
                if n + 2 < 8:
                    load_d(n + 2, slot)
            for tt in range(2):
                t0 = hx + tt * 512
                if first or not fused:
                    outs.append(p.dma("sync", x2v[:, :, t0:t0 + 512], xr[:, :, tt * 512:(tt + 1) * 512], reads=xkeys(tt)))
                hi_ = tt % len(hn)
                norm_tile(xr[:, :, tt * 512:(tt + 1) * 512], xkeys(tt), gn, "gn", hn[hi_], ("hn", hi_))
                hdst = hnv[:, :, t0:t0 + 512] if not (fused and first) else cx.over["hn_tiles"][half * 2 + tt].rearrange("(c p) t -> p c t", p=128)
                gti = half * 2 + tt
                outs.append(p.dma("sync", hdst, hn[hi_][:], reads=[(("hn", hi_), c) for c in range(8)], writes=[("d_hn", gti)]))
                if fused and first:
                    p.cc(lambda e, a=cx.over["hn_tiles"][gti], b=cx.over["hn_all"][gti]: e.collective_compute(
                        "AllGather", ALU.bypass, replica_groups=[[0, 1, 2, 3], [4, 5, 6, 7]], ins=[a], outs=[b]),
                         reads=[("d_hn", gti)], writes=[("d_hnall", gti)])
        cx.last_outs = outs
        if standalone:
            p.emit(final_wait_ops=outs[-8:])
    return nc


def col8(v):
    return np.ascontiguousarray(np.asarray(v, np.float32).reshape(8, 128).T)


def slab_kn(w, ncol):
    K = w.shape[0] // 128
    ns = w.shape[1] // ncol
    a = np.asarray(w, np.float32).reshape(K, 128, ns, ncol).transpose(2, 1, 0, 3)
    return np.ascontiguousarray(a.reshape(ns, 128, K * ncol))


_NC_CACHE = {}


def get_nc(name, fn):
    if name not in _NC_CACHE:
        _NC_CACHE[name] = fn()
    return _NC_CACHE[name]


def run(nc, in_maps):
    res = run_bass_kernel_spmd(nc, in_maps, core_ids=list(range(8)))
    return res.results


def core_tokens(r):
    return r // 4, (r % 4) * T


def stage_p1(inp):
    x = np.asarray(inp["x"], np.float32)
    maps = []
    common = dict(
        gvec=col8(inp["norm_mix_g"][0]),
        w_in=np.ascontiguousarray(inp["a_w_in"][0], dtype=np.float32),
        convw=np.ascontiguousarray(np.concatenate([col8(inp["a_conv_w"][0][j]) for j in range(4)], axis=1)),
        convb=col8(inp["a_conv_b"][0]),
        w_r=np.ascontiguousarray(np.asarray(inp["a_w_r"][0], np.float32).reshape(D, 256)),
        w_i=np.ascontiguousarray(np.asarray(inp["a_w_i"][0], np.float32).reshape(D, 256)),
        b_r=col8(inp["a_b_r"][0]), b_i=col8(inp["a_b_i"][0]), lam=col8(inp["a_lambda"][0]),
        ident=np.eye(128, dtype=np.float32),
    )
    for r in range(8):
        b, t0 = core_tokens(r)
        xs = np.zeros((TH + T, D), np.float32)
        if t0 == 0:
            xs[TH:] = x[b, 0:T]
        else:
            xs[:] = x[b, t0 - TH:t0 + T]
        m = dict(common)
        m["xT"] = np.ascontiguousarray(xs.T)
        maps.append(m)
    return maps


def ffn_weights(inp, layer):
    return dict(
        g_ffn=col8(inp["norm_ffn_g"][layer]),
        wg=slab_kn(inp["ffn_w_gate"][layer], 256),
        wu=slab_kn(inp["ffn_w_up"][layer], 256),
        wd=slab_kn(inp["ffn_w_down"][layer], 128),
    )


def stage_p2(inp, r1):
    x = np.asarray(inp["x"], np.float32)
    common = ffn_weights(inp, 0)
    common["w_out"] = slab_kn(inp["a_w_out"][0], 128)
    common["g_next"] = col8(inp["norm_mix_g"][1])
    maps = []
    for r in range(8):
        b, t0 = core_tokens(r)
        q = r % 4
        m = dict(common)
        m["xT"] = np.ascontiguousarray(x[b, t0:t0 + T].T)
        m["hloc"], m["acum"], m["gate"] = r1[r]["hloc"], r1[r]["acum"], r1[r]["gate"]
        ends = np.zeros((128, 3, 2, 8), np.float32)
        for j in range(3):
            src = q - 3 + j
            if src >= 0:
                rr = b * 4 + src
                ends[:, j, 0, :] = col8(r1[rr]["hloc"][:, T - 1])
                ends[:, j, 1, :] = col8(r1[rr]["acum"][:, T - 1])
        m["ends"] = np.ascontiguousarray(ends.reshape(128, 48))
        maps.append(m)
    return maps


def build_p3(cx=None):
    standalone = cx is None
    if standalone:
        cx = Ctx()
    fused = cx.fused
    cx.prefix = "p3_" if fused else ""
    nc, p = cx.nc, cx.p
    hn = cx.din("hn", [D, S], BF16)
    wq = cx.din("wq", [128, 8 * 256])
    wk = cx.din("wk", [128, 8 * 256])
    wv = cx.din("wv", [128, 8 * 256])
    umat = cx.din("umat", [128, 128])
    masks = cx.din("masks", [128, 4 * 512])
    oT = cx.dout("oT", [256, S], BF16)
    NQ = S // 512
    if not fused:
        hnv = hn.rearrange("(c p) t -> p c t", p=128)
        hn_tile = lambda ti: hnv[:, :, ti * 512:(ti + 1) * 512]
        o_dst = lambda head, Qi: oT[head * 64:(head + 1) * 64, Qi * 512:(Qi + 1) * 512]
    else:
        hn_tile = lambda ti: hn[ti % 4][(ti // 4) * D:(ti // 4 + 1) * D, :].rearrange("(c p) t -> p c t", p=128)
        o_dst = lambda head, Qi: oT[Qi // 4][head * 64:(head + 1) * 64, (Qi % 4) * 512:(Qi % 4 + 1) * 512]

    with cx.phase("p3_"):
        cx.init_psum()
        sb = cx.sb
        hb = [sb("hb%d" % i, [128, 8, 512], BF16) for i in range(2)]
        wqs = sb("wqs", [128, 8 * 256], BF16)
        wks = sb("wks", [128, 8 * 256], BF16)
        wvs = sb("wvs", [128, 8 * 256], BF16)
        qs = sb("qs", [128, 2, S], BF16)
        kt = sb("kt", [128, 2, S], BF16)
        vv = sb("vv", [128, S // 128, 256], BF16)
        um = sb("um", [128, 128])
        onesr = sb("onesr", [128, 128])
        mk = sb("mk", [128, 4, 512])
        ob = [sb("ob%d" % i, [64, 512], BF16) for i in range(2)]

        p.dma("sync", mk[:], masks.rearrange("p (m f) -> p m f", m=4), writes=["mk"])
        umf = sb("umf", [128, 128])
        onesf = sb("onesf", [128, 128])
        p.dma("sync", umf[:], umat, writes=["umf"])
        p.dve(lambda e: e.memset(onesf[:], -1.0), writes=["onesf"])
        p.dve(lambda e: e.tensor_scalar(out=um[:].bitcast(F32R), in0=umf[:], scalar1=-1.0, scalar2=None, op0=ALU.mult), reads=["umf"], writes=["um"])
        p.dve(lambda e: e.tensor_copy(out=onesr[:].bitcast(F32R), in_=onesf[:]), reads=["onesf"], writes=["onesr"])
        p.dma("gpsimd", wqs[:], wq, writes=["wqs"])
        p.dma("gpsimd", wks[:], wk, writes=["wks"])
        p.dma("gpsimd", wvs[:], wv, writes=["wvs"])

        import os
        for oi_, ti in enumerate(sorted(range(int(os.environ.get("P3_PT", str(NQ)))), key=lambda a: (a % 4, a // 4))):
            hbuf = hb[oi_ % 2]
            hkey = ("hb", oi_ % 2)
            p.dma("sync", hbuf[:], hn_tile(ti), reads=[("d_hnall", ti % 4)], writes=[hkey])
            tsl = slice(ti * 512, (ti + 1) * 512)
            SKIP = os.environ.get("P3_SKIP", "")
            for pair in range(2):
                if "q" in SKIP:
                    break
                b, ps = cx.nextps()
                for k in range(8):
                    p.mm(lambda e, k=k, pair=pair, ps=ps, hbuf=hbuf: e.matmul(
                        ps[:], lhsT=wqs[:, k * 256 + pair * 128:k * 256 + (pair + 1) * 128], rhs=hbuf[:, k, :],
                        start=(k == 0), stop=(k == 7)), reads=[hkey, "wqs"], writes=[("ps", b)])
                p.act(lambda e, pair=pair, ps=ps, tsl=tsl: e.mul(out=qs[:, pair, tsl], in_=ps[:], mul=0.125),
                      reads=[("ps", b)], writes=[("qs", pair, ti)])
                b, ps = cx.nextps()
                for k in range(8):
                    p.mm(lambda e, k=k, pair=pair, ps=ps, hbuf=hbuf: e.matmul(
                        ps[:], lhsT=wks[:, k * 256 + pair * 128:k * 256 + (pair + 1) * 128], rhs=hbuf[:, k, :],
                        start=(k == 0), stop=(k == 7)), reads=[hkey, "wks"], writes=[("ps", b)])
                p.act(lambda e, pair=pair, ps=ps, tsl=tsl: e.copy(out=kt[:, pair, tsl], in_=ps[:]),
                      reads=[("ps", b)], writes=[("kt", pair, ti)])
            for tb in range(4):
                if "v" in SKIP:
                    break
                b, ps = cx.nextps()
                for k in range(8):
                    p.mm(lambda e, k=k, tb=tb, ps=ps, hbuf=hbuf: e.matmul(
                        ps[:, 0:256], lhsT=hbuf[:, k, tb * 128:(tb + 1) * 128], rhs=wvs[:, k * 256:(k + 1) * 256],
                        start=(k == 0), stop=(k == 7)), reads=[hkey, "wvs"], writes=[("ps", b)])
                p.dve(lambda e, tb=tb, ps=ps, ti=ti: e.tensor_copy(out=vv[:, ti * 4 + tb, :], in_=ps[:, 0:256]),
                      reads=[("ps", b)], writes=[("vv", ti)])

        outs = []
        um_r = um[:].bitcast(F32R)
        ones_r = onesr[:].bitcast(F32R)
        import os
        DBG_H = int(os.environ.get("P3_HEADS", "4"))
        DBG_Q = int(os.environ.get("P3_NQ", str(NQ)))
        LB = 1
        LC = 2
        NSP = LB + 2
        NWT = LC - LB + 2
        NRS = LB + 2
        etmp = [sb("etmp%d" % i, [128, 2, 512]) for i in range(NSP)]
        pw = [sb("pw%d" % i, [128, 2, 512]) for i in range(2)]
        spt = [sb("sptp%d" % i, [128, 2, 512]) for i in range(NSP)]
        wt = [sb("wtp%d" % i, [128, 2, 512], BF16) for i in range(NWT)]
        rsum = [sb("rsump%d" % i, [128, 2, 512]) for i in range(NRS)]
        one_t = sb("one_t", [128, 1])
        p.dve(lambda e: e.memset(one_t[:], 1.0), writes=["one"])
        psb = cx.psbig
        pairs = []
        gi = 0
        for Qi in range(DBG_Q):
            for pair in range(2):
                for hh in range(2):
                    head = pair * 2 + hh
                    if head >= DBG_H:
                        continue
                    nkb = 4 * (Qi + 1)
                    for kb in range(nkb - 1, -1, -2):
                        pairs.append(dict(pair=pair, hh=hh, head=head, Qi=Qi, kb=kb, first=(kb == nkb - 1), last=(kb == 1),
                                          diag=(kb - 4 * Qi >= 0), dpi=(0 if kb - 4 * Qi == 3 else 1), g=gi))
                    gi += 1
        n = len(pairs)
        for j, t in enumerate(pairs):
            t["zb"] = 2 * (j % 3)
            t["db"] = t["zb"]
            t["sp"] = j % NSP
            t["wt"] = j % NWT
            t["rs"] = j % NRS
            t["ob"] = 6 + t["g"] % 2
            t["rsp"] = None if t["first"] else pairs[j - 1]["rs"]

        def stage_a(t):
            pair, prt = t["pair"], slice(64 * t["hh"], 64 * t["hh"] + 64)
            qsl = slice(t["Qi"] * 512, (t["Qi"] + 1) * 512)
            zb, sp_t, e_t = t["zb"], spt[t["sp"]], etmp[t["sp"]]
            zkeys = [("ps", zb), ("ps", zb + 1)]
            for u in range(2):
                kb = t["kb"] - u
                ksl = slice(kb * 128, (kb + 1) * 128)
                p.mm(lambda e, u=u, ksl=ksl: e.matmul(psb[:, zb + u, :], lhsT=kt[prt, pair, ksl], rhs=qs[prt, pair, qsl],
                                                      start=True, stop=False),
                     reads=[("qs", pair, t["Qi"]), ("kt", pair, kb // 4)], writes=[("ps", zb + u)])
            p.act(lambda e: e.activation(out=e_t[:], in_=psb[:, zb:zb + 2, :], func=AF.Exp),
                  reads=zkeys, writes=[("etmp", t["sp"])])
            p.act(lambda e: e.activation(out=sp_t[:].bitcast(F32R), in_=e_t[:], func=AF.Ln, bias=1.0),
                  reads=[("etmp", t["sp"])], writes=[("spt", t["sp"])])
            if t["diag"]:
                d0 = 2 * t["dpi"]
                p.dve(lambda e: e.tensor_tensor(out=sp_t[:].bitcast(F32R), in0=sp_t[:], in1=mk[:, d0:d0 + 2, :], op=ALU.mult),
                      reads=[("spt", t["sp"]), "mk"], writes=[("spt", t["sp"])])
            rn = rsum[t["rs"]]
            if t["first"]:
                p.dve(lambda e: e.tensor_copy(out=rn[:, 0, :].bitcast(F32R), in_=sp_t[:, 0, :]),
                      reads=[("spt", t["sp"])], writes=[("rsum", t["rs"], 0)])
            else:
                rp = rsum[t["rsp"]]
                p.dve(lambda e: e.tensor_tensor(out=rn[:, 0, :].bitcast(F32R), in0=rp[:, 1, :], in1=sp_t[:, 0, :], op=ALU.add),
                      reads=[("spt", t["sp"]), ("rsum", t["rsp"], 1)], writes=[("rsum", t["rs"], 0)])
            if not t["last"]:
                p.dve(lambda e: e.tensor_tensor(out=rn[:, 1, :].bitcast(F32R), in0=rn[:, 0, :], in1=sp_t[:, 1, :], op=ALU.add),
                      reads=[("spt", t["sp"]), ("rsum", t["rs"], 0)], writes=[("rsum", t["rs"], 1)])

        def stage_b(t):
            db, sp_t, w_t, e_t = t["db"], spt[t["sp"]], wt[t["wt"]], etmp[t["sp"]]
            rn = rsum[t["rs"]]
            p_t = pw[t["wt"] % 2]
            pk = ("pw", t["wt"] % 2)
            for u in range(2):
                first_tile = t["first"] and u == 0
                p.mm(lambda e, u=u, first_tile=first_tile: e.matmul(psb[:, db + u, :], lhsT=um_r, rhs=sp_t[:, u, :].bitcast(F32R),
                                                                    start=False, stop=first_tile),
                     reads=[("spt", t["sp"]), "um"], writes=[("ps", db + u)])
            for u in range(2):
                first_tile = t["first"] and u == 0
                if not first_tile:
                    if u == 0:
                        rsrc, rkey = rsum[t["rsp"]][:, 1, :], ("rsum", t["rsp"], 1)
                    else:
                        rsrc, rkey = rn[:, 0, :], ("rsum", t["rs"], 0)
                    p.mm(lambda e, u=u, rsrc=rsrc: e.matmul(psb[:, db + u, :], lhsT=ones_r, rhs=rsrc.bitcast(F32R),
                                                            start=False, stop=True),
                         reads=[rkey, "onesr"], writes=[("ps", db + u)])
            p.act(lambda e: e.activation(out=w_t[:], in_=psb[:, db:db + 2, :], func=AF.Exp),
                  reads=[("ps", db), ("ps", db + 1)], writes=[("wt", t["wt"], 0), ("wt", t["wt"], 1)])
            if t["diag"]:
                d0 = 2 * t["dpi"]
                p.dve(lambda e: e.tensor_tensor(out=w_t[:], in0=w_t[:], in1=mk[:, d0:d0 + 2, :], op=ALU.mult),
                      reads=[("wt", t["wt"], 0), ("wt", t["wt"], 1), "mk"], writes=[("wt", t["wt"], 0), ("wt", t["wt"], 1)])

        def stage_c(t):
            ops_, w_t, head = cx.ps[t["ob"]], wt[t["wt"]], t["head"]
            for u in range(2):
                kb = t["kb"] - u
                p.mm(lambda e, u=u, kb=kb: e.matmul(ops_[0:64, :], lhsT=vv[:, kb, head * 64:(head + 1) * 64], rhs=w_t[:, u, :],
                                                    start=(t["first"] and u == 0), stop=(t["last"] and u == 1)),
                     reads=[("wt", t["wt"], u), ("vv", kb // 4)], writes=[("ps", t["ob"])])
            if t["last"]:
                obuf = ob[t["g"] % 2]
                p.act(lambda e: e.copy(out=obuf[:], in_=ops_[0:64, :]),
                      reads=[("ps", t["ob"])], writes=[("ob", t["g"] % 2)])
                qk = t["Qi"] // 4
                outs.append(p.dma("sync", o_dst(head, t["Qi"]), obuf[:], reads=[("ob", t["g"] % 2)], writes=[("d_o", qk, t["Qi"] % 4, head)]))
                if fused and t["Qi"] % 4 == 3 and head == 3:
                    p.cc(lambda e, a=cx.over["oT"][qk], b=cx.over["o_all"][qk]: e.collective_compute(
                        "AllGather", ALU.bypass, replica_groups=[[0, 1, 2, 3], [4, 5, 6, 7]], ins=[a], outs=[b]),
                         reads=[("d_o", qk, a, b) for a in range(4) for b in range(4)], writes=[("d_oall", qk)])

        for i in range(n + LC):
            if i < n:
                stage_a(pairs[i])
            if 0 <= i - LB < n:
                stage_b(pairs[i - LB])
            if 0 <= i - LC < n:
                stage_c(pairs[i - LC])
        cx.last_outs = outs
        if standalone:
            p.emit(final_wait_ops=outs[-8:])
    return nc


def attn_consts():
    pp = np.arange(128)[:, None]
    um = (pp >= np.arange(128)[None, :]).astype(np.float32)
    f = np.arange(512)[None, :]
    mk = np.concatenate([(f > (128 * m + pp)).astype(np.float32) for m in (3, 2, 1, 0)], axis=1)
    return um, np.ascontiguousarray(mk)


def stage_p3(inp, r2):
    um, mk = attn_consts()
    wqkv = np.asarray(inp["b_w_qkv"][0], np.float32)
    maps = []
    for r in range(8):
        b, hq = r // 4, r % 4
        hn = np.concatenate([r2[b * 4 + q]["hnT"] for q in range(4)], axis=1)
        cols = slice(hq * 256, (hq + 1) * 256)
        m = dict(
            hn=np.ascontiguousarray(hn),
            wq=slab_kn(wqkv[:, 0:D][:, cols], 256)[0],
            wk=slab_kn(wqkv[:, D:2 * D][:, cols], 256)[0],
            wv=slab_kn(wqkv[:, 2 * D:3 * D][:, cols], 256)[0],
            umat=um, masks=mk,
        )
        maps.append(m)
    return maps


def stage_p4(inp, r2, r3):
    common = ffn_weights(inp, 1)
    common["w_out"] = slab_kn(inp["b_w_out"][0], 128)
    common["g_next"] = col8(inp["final_g"])
    maps = []
    for r in range(8):
        b, q = r // 4, r % 4
        m = dict(common)
        m["xT"] = r2[r]["x2T"]
        m["oT"] = np.ascontiguousarray(
            np.concatenate([r3[b * 4 + hq]["oT"][:, q * T:(q + 1) * T] for hq in range(4)], axis=0))
        maps.append(m)
    return maps


def build_fused():
    cx = Ctx()
    cx.fused = True
    nc, p = cx.nc, cx.p
    xT = nc.dram_tensor("p1_xT", [D, TH + T], F32, kind="ExternalInput").ap()
    out = nc.dram_tensor("out", [D, T], F32, kind="ExternalOutput").ap()
    hloc = cx.dint("i_hloc", [D, T])
    acum = cx.dint("i_acum", [D, T])
    gate = cx.dint("i_gate", [D, T], BF16)
    ends_src = cx.dint("i_ends_src", [128, 16])
    ends_all = cx.dint("i_ends_all", [4 * 128, 16])
    x2 = cx.dint("i_x2", [D, T])
    hn_src = [cx.dint("i_hn_src%d" % k, [D, 512], BF16) for k in range(4)]
    hn_all = [cx.dint("i_hn_all%d" % k, [4 * D, 512], BF16) for k in range(4)]
    o_src = [cx.dint("i_o_src%d" % k, [256, T], BF16) for k in range(4)]
    o_all = [cx.dint("i_o_all%d" % k, [4 * 256, T], BF16) for k in range(4)]
    grp4 = [[0, 1, 2, 3], [4, 5, 6, 7]]
    grp8 = [list(range(8))]

    cx.over = dict(xT=xT, hloc=hloc, acum=acum, gate=gate, ends_src=ends_src)
    build_p1(cx)
    p.barrier()
    import os
    STOP = int(os.environ.get("FUSED_STOP", "99"))
    if STOP <= 0:
        p.emit()
        return nc
    p.cc(lambda e: e.collective_compute("AllGather", ALU.bypass, replica_groups=grp4, ins=[ends_src], outs=[ends_all]),
         writes=["d_ends_all"])
    pass
    if STOP <= 1:
        p.emit()
        return nc
    cx.over = dict(xT=xT[:, TH:TH + T], hloc=hloc, acum=acum, gate=gate, ends_all=ends_all, x2T=x2, hnT=x2, hn_tiles=hn_src, hn_all=hn_all)
    build_p24(True, False, cx)
    p.barrier(skip_cc=True)
    if STOP <= 2:
        p.emit()
        return nc
    pass
    if STOP <= 3:
        p.emit()
        return nc
    cx.over = dict(hn=hn_all, oT=o_src, o_all=o_all)
    build_p3(cx)
    p.barrier(skip_cc=True)
    if STOP <= 4:
        p.emit()
        return nc
    pass
    if STOP <= 5:
        p.emit()
        return nc
    cx.over = dict(xT=x2, x2T=x2, hnT=out, o_all=o_all)
    build_p24(False, True, cx)
    p.emit(final_wait_ops=cx.last_outs[-8:])
    return nc


def stage_fused(inp):
    m1 = stage_p1(inp)
    um, mk = attn_consts()
    wqkv = np.asarray(inp["b_w_qkv"][0], np.float32)
    f0 = ffn_weights(inp, 0)
    f0["w_out"] = slab_kn(inp["a_w_out"][0], 128)
    f0["g_next"] = col8(inp["norm_mix_g"][1])
    f1 = ffn_weights(inp, 1)
    f1["w_out"] = slab_kn(inp["b_w_out"][0], 128)
    f1["g_next"] = col8(inp["final_g"])
    maps = []
    for r in range(8):
        b, q = r // 4, r % 4
        m = {}
        for k, v in m1[r].items():
            m["p1_" + k] = v
        for k, v in f0.items():
            m["p2_" + k] = v
        for k, v in f1.items():
            m["p4_" + k] = v
        keep = np.zeros((128, 4), np.float32)
        for rr in range(q):
            keep[:, rr] = 1.0
        m["p2_km"] = keep
        m["p2_hm"] = 1.0 - keep
        sel = np.zeros((128, 4), np.float32)
        sel[:, q] = 1.0
        m["p4_sel"] = sel
        cols = slice(q * 256, (q + 1) * 256)
        m["p3_wq"] = slab_kn(wqkv[:, 0:D][:, cols], 256)[0]
        m["p3_wk"] = slab_kn(wqkv[:, D:2 * D][:, cols], 256)[0]
        m["p3_wv"] = slab_kn(wqkv[:, 2 * D:3 * D][:, cols], 256)[0]
        m["p3_umat"] = um
        m["p3_masks"] = mk
        maps.append(m)
    return maps


def kernel(**inp):
    inp = {k: np.asarray(v) for k, v in inp.items()}
    res = run(get_nc("fused", build_fused), stage_fused(inp))
    out = np.stack([np.concatenate([res[b * 4 + q]["out"].T for q in range(4)], axis=0) for b in range(2)])
    return np.ascontiguousarray(out.astype(np.float32))
```

```python
import contextlib
import numpy as np
import ml_dtypes
import concourse.bass as bass
import concourse.mybir as mybir
from concourse.bass_utils import run_bass_kernel_spmd

F32 = mybir.dt.float32
F32R = mybir.dt.float32r
BF16 = mybir.dt.bfloat16
AF = mybir.ActivationFunctionType
ALU = mybir.AluOpType
NPBF = ml_dtypes.bfloat16

D = 1024
DFF = 2816
NF = DFF // 128
T = 2048
S = 8192
TH = 4
EPS = 1e-6
COMPUTE = ("tensor", "vector", "scalar", "gpsimd")


class Prog:
    def __init__(self, nc, n_dma_sems=8):
        self.nc = nc
        self.ops = []
        self.state = {}
        self.n_dma_sems = n_dma_sems

    def add(self, eng, fn, reads=(), writes=(), dma=False):
        idx = len(self.ops)
        ops = self.ops
        deps = set()
        for k in reads:
            st = self.state.setdefault(k, [None, []])
            if st[0] is not None:
                deps.add(st[0])
        for k in writes:
            st = self.state.setdefault(k, [None, []])
            if st[0] is not None:
                deps.add(st[0])
            deps.update(st[1])
        for k in reads:
            st = self.state[k]
            if not dma:
                st[1] = [r for r in st[1] if ops[r]["dma"] or ops[r]["eng"] != eng]
            st[1].append(idx)
        for k in writes:
            self.state[k] = [idx, []]
        deps.discard(idx)
        ops.append(dict(eng=eng, fn=fn, deps=deps, dma=dma))
        return idx

    def mm(self, fn, reads=(), writes=()):
        return self.add("tensor", fn, reads, writes)

    def act(self, fn, reads=(), writes=()):
        return self.add("scalar", fn, reads, writes)

    def dve(self, fn, reads=(), writes=()):
        return self.add("vector", fn, reads, writes)

    def pool(self, fn, reads=(), writes=()):
        return self.add("gpsimd", fn, reads, writes)

    def dma(self, q, out, in_, reads=(), writes=()):
        return self.add(q, lambda e: e.dma_start(out=out, in_=in_), reads, writes, dma=True)

    def barrier(self, skip_cc=False):
        for e in ("sync", "scalar", "vector", "gpsimd", "tensor"):
            self.ops.append(dict(eng=e, fn=None, deps=set(), dma=False, barrier=True, skip_cc=skip_cc))

    def cc(self, fn, reads=(), writes=()):
        idx = self.add("gpsimd", fn, reads, writes)
        self.ops[idx]["cc"] = True
        return idx

    def emit(self, final_wait_ops=()):
        nc = self.nc
        ops = self.ops
        n = len(ops)

        def skip(d, e):
            od = ops[d]
            return od["eng"] == "tensor" and e == "tensor" and not od["dma"]

        has_dep = [False] * n
        last_op = {}
        for i, o in enumerate(ops):
            if o.get("barrier"):
                for e2 in COMPUTE:
                    if e2 in last_op:
                        has_dep[last_op[e2]] = True
                continue
            if o["eng"] in COMPUTE and not o["dma"] and not o.get("cc"):
                last_op[o["eng"]] = i
            for d in o["deps"]:
                if not skip(d, o["eng"]):
                    has_dep[d] = True
        for d in final_wait_ops:
            has_dep[d] = True
        engs = ("sync", "scalar", "vector", "gpsimd", "tensor")
        stack = contextlib.ExitStack()
        sems = {}
        for e in COMPUTE:
            sems[e] = stack.enter_context(nc.semaphore("s_" + e))
        dma_sems = {}
        for q in ("sync", "scalar", "gpsimd"):
            for j in range(self.n_dma_sems):
                dma_sems[(q, j)] = stack.enter_context(nc.semaphore("d_%s%d" % (q, j)))
        sems["cc"] = stack.enter_context(nc.semaphore("s_cc"))
        cnt = {k: 0 for k in list(sems) + list(dma_sems)}
        rr = {"sync": 0, "scalar": 0, "gpsimd": 0}
        sig = [None] * n
        waits = [None] * n
        waited = {e: {} for e in engs}
        for i, o in enumerate(ops):
            e = o["eng"]
            w = []

            def need(key, val):
                if val > 0 and waited[e].get(key, 0) < val:
                    waited[e][key] = val
                    w.append((key, val))

            if o.get("barrier"):
                for key in list(cnt):
                    if key != e and not (o.get("skip_cc") and key == "cc"):
                        need(key, cnt[key])
                waits[i] = w
                continue
            for d in sorted(o["deps"]):
                if skip(d, e):
                    continue
                need(*sig[d])
            if o.get("cc"):
                cnt["cc"] += 1
                sig[i] = ("cc", cnt["cc"])
            elif o["dma"]:
                j = rr[e]
                rr[e] = (j + 1) % self.n_dma_sems
                key = (e, j)
                need(key, cnt[key])
                cnt[key] += 16
                sig[i] = (key, cnt[key])
            elif has_dep[i]:
                cnt[e] += 1
                sig[i] = (e, cnt[e])
            waits[i] = w
        allsems = dict(sems)
        allsems.update(dma_sems)
        self.max_counts = dict(cnt)
        final = [sig[d] for d in final_wait_ops]

        with stack:
            with nc.Block() as block:
                def make(ename):
                    def body(eng):
                        for i, o in enumerate(ops):
                            if o["eng"] != ename:
                                continue
                            for key, val in waits[i]:
                                eng.wait_ge(allsems[key], val)
                            if o["fn"] is None:
                                continue
                            inst = o["fn"](eng)
                            if sig[i] is not None:
                                inst.then_inc(allsems[sig[i][0]], 16 if o["dma"] else 1)
                        if ename == "sync":
                            for key, val in final:
                                eng.wait_ge(allsems[key], val)
                    return body

                block.sync(make("sync"))
                block.scalar(make("scalar"))
                block.vector(make("vector"))
                block.gpsimd(make("gpsimd"))
                block.tensor(make("tensor"))


class Ctx:
    def __init__(self):
        self.nc = bass.Bass("TRN2", target_bir_lowering=False)
        self.stack = contextlib.ExitStack()
        self.gstack = contextlib.ExitStack()
        self.p = Prog(self.nc)
        self.ps = None
        self.psi = 0
        self.prefix = ""
        self.over = {}
        self.fused = False

    def din(self, name, shape, dt=F32):
        if name in self.over:
            return self.over[name]
        return self.nc.dram_tensor(self.prefix + name, list(shape), dt, kind="ExternalInput").ap()

    def dout(self, name, shape, dt=F32):
        if name in self.over:
            return self.over[name]
        return self.nc.dram_tensor(self.prefix + name, list(shape), dt, kind="ExternalOutput").ap()

    def dint(self, name, shape, dt=F32):
        return self.nc.dram_tensor(name, list(shape), dt, addr_space="Local", kind="Internal").ap()

    def sb(self, name, shape, dt=F32):
        return self.stack.enter_context(self.nc.sbuf_tensor(self.prefix + "s_" + name, list(shape), dt))

    def init_psum(self):
        if self.ps is None:
            self.psbig = self.gstack.enter_context(self.nc.psum_tensor("psbig", [128, 8, 512], F32))
            self.ps = [self.psbig[:, i, :] for i in range(8)]

    def phase(self, prefix):
        self.stack = contextlib.ExitStack()
        return self.stack

    def nextps(self):
        b = self.psi
        self.psi = (b + 1) % 8
        return b, self.ps[b]


def emit_norm(cx, x3, xkey, N, g_sb, h3, hkey, sq, rs, rstd, ones):
    p = cx.p
    p.act(lambda e: e.activation(out=sq[:, :, :N], in_=x3, func=AF.Square), reads=[xkey], writes=["sq"])
    b, ps = cx.nextps()
    for c in range(8):
        p.mm(lambda e, c=c: e.matmul(ps[:, :N], lhsT=ones[:], rhs=sq[:, c, :N], start=(c == 0), stop=(c == 7)),
             reads=["sq", "ones"], writes=[("ps", b)])
    p.act(lambda e, eps_t=cx.eps_t: e.activation(out=rs[:, :N], in_=ps[:, :N], func=AF.Sqrt, scale=1.0 / D, bias=eps_t[:, 0:1]),
          reads=[("ps", b), "eps"], writes=["rs"])
    p.dve(lambda e: e.reciprocal(out=rstd[:, :N], in_=rs[:, :N]), reads=["rs"], writes=["rstd"])
    for c in range(8):
        eng = "vector"
        p.add(eng, lambda e, c=c: e.scalar_tensor_tensor(out=h3[:, c, :], in0=x3[:, c, :], scalar=g_sb[:, c:c + 1],
                                                     in1=rstd[:, :N], op0=ALU.mult, op1=ALU.mult),
              reads=[xkey, "rstd", "gvec"], writes=[(hkey, c)])


def load_w_bf16(cx, dst, src, K, key, nsplit=None):
    v = src.rearrange("(k p) n -> p k n", p=128)
    for k in range(K):
        cx.p.dma("gpsimd", dst[:, k, :], v[:, k, :], writes=[(key, k)])


def build_p1(cx=None):
    standalone = cx is None
    if standalone:
        cx = Ctx()
    cx.prefix = "p1_" if cx.fused else ""
    nc, p = cx.nc, cx.p
    xT = cx.din("xT", [D, TH + T])
    gvec = cx.din("gvec", [128, 8])
    w_in = cx.din("w_in", [D, 2 * D])
    convw = cx.din("convw", [128, 32])
    convb = cx.din("convb", [128, 8])
    w_r = cx.din("w_r", [D, 256])
    w_i = cx.din("w_i", [D, 256])
    b_r = cx.din("b_r", [128, 8])
    b_i = cx.din("b_i", [128, 8])
    lam = cx.din("lam", [128, 8])
    ident = cx.din("ident", [128, 128])
    hloc = cx.dout("hloc", [D, T])
    acum = cx.dout("acum", [D, T])
    gate = cx.dout("gate", [D, T], BF16)

    xTv = xT.rearrange("(c p) t -> p c t", p=128)
    hlv = hloc.rearrange("(c p) t -> p c t", p=128)
    acv = acum.rearrange("(c p) t -> p c t", p=128)
    gtv = gate.rearrange("(c p) t -> p c t", p=128)

    with cx.phase("p1_"):
        cx.init_psum()
        sb = cx.sb
        xt = [sb("xt%d" % i, [128, 8, 512]) for i in range(1)]
        sq = sb("sq", [128, 8, 512], BF16)
        rs = sb("rs", [128, 512])
        rstd = sb("rstd", [128, 512])
        h = sb("h", [128, 8, 512], BF16)
        win = sb("win", [128, 8, 2 * D], BF16)
        wr = sb("wr", [128, 8, 256], BF16)
        wi = sb("wi", [128, 8, 256], BF16)
        diag = sb("diag", [128, 32, 128], BF16)
        identf = sb("identf", [128, 128])
        ones = sb("ones", [128, 128], BF16)
        zeros = sb("zeros", [128, 512])
        cx.eps_t = sb("eps", [128, 1])
        g_sb = sb("g_sb", [128, 8])
        cw = sb("cw", [128, 32])
        cb = sb("cb", [128, 8])
        br = sb("br", [128, 8])
        bi = sb("bi", [128, 8])
        lm = sb("lm", [128, 8])
        c8 = sb("c8", [128, 8])
        c16 = sb("c16", [128, 8])
        xb = sb("xb", [128, 8, TH + T], BF16)
        gt = [sb("gt%d" % i, [128, 8, 512], BF16) for i in range(2)]
        xc = sb("xc", [128, 2, 512])
        xcb = sb("xcb", [128, 2, 512], BF16)
        rt = [sb("rt%d" % i, [128, 512]) for i in range(2)]
        it = [sb("it%d" % i, [128, 512]) for i in range(2)]
        a2 = [sb("a2%d" % i, [128, 512]) for i in range(2)]
        at = sb("at", [128, 2, 512])
        ut = sb("ut", [128, 2, 512])
        hl = [sb("hl%d" % i, [128, 8, 512]) for i in range(1)]
        ac = [sb("ac%d" % i, [128, 8, 512]) for i in range(1)]
        hlast = sb("hlast", [128, 8])
        alast = sb("alast", [128, 8])

        for dst, src, key in ((g_sb, gvec, "gvec"), (cw, convw, "cw"), (cb, convb, "cb"), (br, b_r, "br"),
                              (bi, b_i, "bi"), (lm, lam, "lm"), (identf, ident, "identf")):
            p.dma("sync", dst[:], src, writes=[key])
        p.dve(lambda e: e.memset(ones[:], 1.0), writes=["ones"])
        p.dve(lambda e: e.memset(zeros[:], 0.0), writes=["zeros"])
        p.dve(lambda e, eps_t=cx.eps_t: e.memset(eps_t[:], EPS), writes=["eps"])
        load_w_bf16(cx, win, w_in, 8, "win")
        load_w_bf16(cx, wr, w_r, 8, "wr")
        load_w_bf16(cx, wi, w_i, 8, "wi")
        p.act(lambda e: e.activation(out=c8[:], in_=lm[:], func=AF.Sigmoid), reads=["lm"], writes=["c8"])
        p.act(lambda e: e.activation(out=c8[:], in_=c8[:], func=AF.Ln), reads=["c8"], writes=["c8"])
        p.act(lambda e: e.mul(out=c16[:], in_=c8[:], mul=16.0), reads=["c8"], writes=["c16"])
        p.act(lambda e: e.mul(out=c8[:], in_=c8[:], mul=8.0), reads=["c8"], writes=["c8"])
        for jc in range(32):
            p.dve(lambda e, jc=jc: e.tensor_scalar(out=diag[:, jc, :], in0=identf[:], scalar1=cw[:, jc:jc + 1],
                                                 scalar2=None, op0=ALU.mult),
                  reads=["identf", "cw"], writes=[("diag", jc)])

        winkeys = [("win", k) for k in range(8)]
        wrkeys = [("wr", k) for k in range(8)]
        wikeys = [("wi", k) for k in range(8)]

        def do_tile(ti):
            halo = ti < 0
            N = TH if halo else 512
            t0 = 0 if halo else TH + ti * 512
            xbuf = xt[0]
            xkey = ("xt", 0)
            x3 = xbuf[:, :, :N]
            p.dma("sync", x3, xTv[:, :, t0:t0 + N], writes=[xkey])
            emit_norm(cx, x3, xkey, N, g_sb, h[:, :, :N], "h", sq, rs, rstd, ones)
            hkeys = [("h", c) for c in range(8)]
            gbuf = gt[ti % 2]
            gkey = ("gt", ti % 2)
            for n in range(0 if not halo else 8, 16):
                b, ps = cx.nextps()
                for k in range(8):
                    p.mm(lambda e, k=k, n=n, ps=ps: e.matmul(ps[:, :N], lhsT=win[:, k, n * 128:(n + 1) * 128],
                                                          rhs=h[:, k, :N], start=(k == 0), stop=(k == 7)),
                         reads=hkeys + winkeys, writes=[("ps", b)])
                if n < 8:
                    p.act(lambda e, n=n, ps=ps: e.activation(out=gbuf[:, n, :], in_=ps[:, :N], func=AF.Gelu_apprx_tanh),
                          reads=[("ps", b)], writes=[gkey])
                else:
                    p.act(lambda e, n=n, ps=ps: e.copy(out=xb[:, n - 8, t0:t0 + N], in_=ps[:, :N]),
                          reads=[("ps", b)], writes=[("xb", n - 8, ti)])
            if halo:
                return
            p.dma("sync", gtv[:, :, ti * 512:(ti + 1) * 512], gbuf[:], reads=[gkey])
            hbuf, abuf = hl[0], ac[0]
            for blk in range(4):
                for cc in range(2):
                    c = 2 * blk + cc
                    b, ps = cx.nextps()
                    for j in range(4):
                        p.mm(lambda e, j=j, c=c, ps=ps: e.matmul(ps[:], lhsT=diag[:, j * 8 + c, :],
                                                              rhs=xb[:, c, t0 - 3 + j:t0 - 3 + j + 512],
                                                              start=(j == 0), stop=(j == 3)),
                             reads=[("xb", c, ti), ("xb", c, ti - 1), ("diag", j * 8 + c)], writes=[("ps", b)])
                    p.act(lambda e, c=c, cc=cc, ps=ps: e.activation(out=xc[:, cc, :], in_=ps[:], func=AF.Identity,
                                                                bias=cb[:, c:c + 1]),
                          reads=[("ps", b), "cb"], writes=[("xc", cc)])
                    p.dve(lambda e, cc=cc: e.tensor_copy(out=xcb[:, cc, :], in_=xc[:, cc, :]),
                          reads=[("xc", cc)], writes=[("xcb", cc)])
                for cc in range(2):
                    n = 2 * blk + cc
                    rb, ib, ab = rt[cc], it[cc], a2[cc]
                    b, ps = cx.nextps()
                    for kk in range(2):
                        p.mm(lambda e, kk=kk, cc=cc, ps=ps, blk=blk: e.matmul(ps[:], lhsT=wr[:, blk * 2 + kk, cc * 128:(cc + 1) * 128],
                                                                  rhs=xcb[:, kk, :], start=(kk == 0), stop=(kk == 1)),
                             reads=[("xcb", 0), ("xcb", 1)] + wrkeys, writes=[("ps", b)])
                    p.act(lambda e, n=n, ps=ps, rb=rb: e.activation(out=rb[:], in_=ps[:], func=AF.Sigmoid, bias=br[:, n:n + 1]),
                          reads=[("ps", b), "br"], writes=[("rt", cc)])
                    b, ps = cx.nextps()
                    for kk in range(2):
                        p.mm(lambda e, kk=kk, cc=cc, ps=ps, blk=blk: e.matmul(ps[:], lhsT=wi[:, blk * 2 + kk, cc * 128:(cc + 1) * 128],
                                                                  rhs=xcb[:, kk, :], start=(kk == 0), stop=(kk == 1)),
                             reads=[("xcb", 0), ("xcb", 1)] + wikeys, writes=[("ps", b)])
                    p.act(lambda e, n=n, ps=ps, ib=ib: e.activation(out=ib[:], in_=ps[:], func=AF.Sigmoid, bias=bi[:, n:n + 1]),
                          reads=[("ps", b), "bi"], writes=[("it", cc)])
                for cc in range(2):
                    n = 2 * blk + cc
                    rb, ib, ab = rt[cc], it[cc], a2[cc]
                    p.act(lambda e, n=n, rb=rb, cc=cc: e.activation(out=at[:, cc, :], in_=rb[:], func=AF.Exp, scale=c8[:, n:n + 1]),
                          reads=[("rt", cc), "c8"], writes=[("at", cc)])
                    p.act(lambda e, n=n, rb=rb, ab=ab: e.activation(out=ab[:], in_=rb[:], func=AF.Exp, scale=c16[:, n:n + 1]),
                          reads=[("rt", cc), "c16"], writes=[("a2", cc)])
                for cc in range(2):
                    n = 2 * blk + cc
                    rb, ib, ab = rt[cc], it[cc], a2[cc]
                    p.act(lambda e, ab=ab, one_t=cx.one_t: e.activation(out=ab[:], in_=ab[:], func=AF.Sqrt, scale=-1.0, bias=one_t[:, 0:1]),
                          reads=[("a2", cc), "one"], writes=[("a2", cc)])
                    p.pool(lambda e, ib=ib, cc=cc: e.tensor_tensor(out=ib[:], in0=ib[:], in1=xc[:, cc, :], op=ALU.mult),
                           reads=[("it", cc), ("xc", cc)], writes=[("it", cc)])
                    p.dve(lambda e, cc=cc, ib=ib, ab=ab: e.tensor_tensor(out=ut[:, cc, :], in0=ib[:], in1=ab[:], op=ALU.mult),
                          reads=[("it", cc), ("a2", cc)], writes=[("ut", cc)])
                    if ti == 0:
                        hinit, ainit = 0.0, 1.0
                        ir = []
                    else:
                        hinit, ainit = hlast[:, n:n + 1], alast[:, n:n + 1]
                        ir = [("hlast", n), ("alast", n)]
                    p.dve(lambda e, n=n, cc=cc, hinit=hinit: e.tensor_tensor_scan(out=hbuf[:, n, :], data0=at[:, cc, :], data1=ut[:, cc, :],
                                                                        initial=hinit, op0=ALU.mult, op1=ALU.add),
                          reads=[("at", cc), ("ut", cc)] + ir, writes=[("hl", 0, n)])
                    p.dve(lambda e, n=n, cc=cc, ainit=ainit: e.tensor_tensor_scan(out=abuf[:, n, :], data0=at[:, cc, :], data1=zeros[:],
                                                                        initial=ainit, op0=ALU.mult, op1=ALU.add),
                          reads=[("at", cc), "zeros"] + ir, writes=[("ac", 0, n)])
                    p.dve(lambda e, n=n: e.tensor_copy(out=hlast[:, n:n + 1], in_=hbuf[:, n, 511:512]),
                          reads=[("hl", 0, n)], writes=[("hlast", n)])
                    p.dve(lambda e, n=n: e.tensor_copy(out=alast[:, n:n + 1], in_=abuf[:, n, 511:512]),
                          reads=[("ac", 0, n)], writes=[("alast", n)])
            p.dma("sync", hlv[:, :, ti * 512:(ti + 1) * 512], hbuf[:], reads=[("hl", 0, n) for n in range(8)])
            return p.dma("sync", acv[:, :, ti * 512:(ti + 1) * 512], abuf[:], reads=[("ac", 0, n) for n in range(8)])

        cx.one_t = sb("one_t", [128, 1])
        p.dve(lambda e, one_t=cx.one_t: e.memset(one_t[:], 1.0), writes=["one"])
        do_tile(-1)
        last = None
        for ti in range(4):
            last = do_tile(ti)
        if cx.fused:
            es = cx.over["ends_src"]
            p.dma("sync", es[:, 0:8], hlast[:], reads=[("hlast", n) for n in range(8)])
            p.dma("sync", es[:, 8:16], alast[:], reads=[("alast", n) for n in range(8)])
        if standalone:
            outs = [i for i, o in enumerate(p.ops) if o["dma"] and o["eng"] == "sync"]
            p.emit(final_wait_ops=outs[-12:])
    return nc


def build_p24(first, out_f32, cx=None):
    standalone = cx is None
    if standalone:
        cx = Ctx()
    fused = cx.fused
    cx.prefix = ("p2_" if first else "p4_") if fused else ""
    nc, p = cx.nc, cx.p
    xT = cx.din("xT", [D, T])
    if first:
        hloc = cx.din("hloc", [D, T])
        acum = cx.din("acum", [D, T])
        gate = cx.din("gate", [D, T], BF16)
        if not fused:
            ends = cx.din("ends", [128, 48])
        else:
            km_d = cx.din("km", [128, 4])
            hm_d = cx.din("hm", [128, 4])
    else:
        if not fused:
            oT = cx.din("oT", [D, T], BF16)
        else:
            sel_d = cx.din("sel", [128, 4])
    w_out = cx.din("w_out", [8, 128, 8 * 128])
    g_ffn = cx.din("g_ffn", [128, 8])
    g_next = cx.din("g_next", [128, 8])
    wg = cx.din("wg", [NF // 2, 128, 8 * 256])
    wu = cx.din("wu", [NF // 2, 128, 8 * 256])
    wd = cx.din("wd", [8, 128, NF * 128])
    x2T = cx.dout("x2T", [D, T])
    hnT = cx.dout("hnT", [D, T], F32 if out_f32 else BF16)

    v3 = lambda a: a.rearrange("(c p) t -> p c t", p=128)
    xTv, x2v, hnv = v3(xT), v3(x2T), v3(hnT)

    with cx.phase("p2_" if first else "p4_"):
        cx.init_psum()
        sb = cx.sb
        xr = sb("xr", [128, 8, 1024])
        y = [sb("y%d" % i, [128, 8, 512], BF16) for i in range(2)]
        sq = sb("sq", [128, 8, 512], BF16)
        rs = sb("rs", [128, 512])
        rstd = sb("rstd", [128, 512])
        h2 = sb("h2", [128, 8, 1024], BF16)
        actb = sb("actb", [128, NF, 1024], BF16)
        sg = [sb("sg%d" % i, [128, 512]) for i in range(2)]
        wo = sb("wo", [128, 8, 8 * 128], BF16)
        wgs = [sb("wgs%d" % i, [128, 8 * 256], BF16) for i in range(3)]
        wus = [sb("wus%d" % i, [128, 8 * 256], BF16) for i in range(3)]
        wds = [sb("wds%d" % i, [128, NF * 128], BF16) for i in range(2)]
        hn = [sb("hn%d" % i, [128, 8, 512], F32 if out_f32 else BF16) for i in range(1 if out_f32 else 2)]
        ones = sb("ones", [128, 128], BF16)
        cx.eps_t = sb("eps", [128, 1])
        gf = sb("gf", [128, 8])
        gn = sb("gn", [128, 8])
        p.dma("sync", gf[:], g_ffn, writes=["gf"])
        p.dma("sync", gn[:], g_next, writes=["gn"])
        p.dve(lambda e: e.memset(ones[:], 1.0), writes=["ones"])
        p.dve(lambda e, eps_t=cx.eps_t: e.memset(eps_t[:], EPS), writes=["eps"])
        for n in range(8):
            p.dma("gpsimd", wo[:, n, :], w_out[n], writes=[("wo", n)])
        if first:
            en = sb("en", [128, 48])
            carry = sb("carry", [128, 8])
            ctmp = sb("ctmp", [128, 8])
            hlc = [sb("hlc%d" % i, [128, 512]) for i in range(2)]
            acc = [sb("acc%d" % i, [128, 512]) for i in range(2)]
            gtc = [sb("gtc%d" % i, [128, 512], BF16) for i in range(2)]
            hlv, acv, gtv = v3(hloc), v3(acum), v3(gate)
            p.dve(lambda e: e.memset(carry[:], 0.0), writes=["carry"])
            if fused:
                en_all = sb("en_all", [128, 4, 16])
                km = sb("km", [128, 4])
                hm = sb("hm", [128, 4])
                ap_t = sb("ap_t", [128, 8])
                hp_t = sb("hp_t", [128, 8])
                p.dma("sync", en_all[:], cx.over["ends_all"].rearrange("(r p) f -> p r f", p=128), reads=["d_ends_all"], writes=["en_all"])
                p.dma("sync", km[:], km_d, writes=["km"])
                p.dma("sync", hm[:], hm_d, writes=["hm"])
                for r in range(4):
                    p.dve(lambda e, r=r: e.tensor_scalar(out=ap_t[:], in0=en_all[:, r, 8:16], scalar1=km[:, r:r + 1],
                                                       scalar2=hm[:, r:r + 1], op0=ALU.mult, op1=ALU.add),
                          reads=["en_all", "km", "hm"], writes=["ap_t"])
                    p.dve(lambda e, r=r: e.tensor_scalar(out=hp_t[:], in0=en_all[:, r, 0:8], scalar1=km[:, r:r + 1],
                                                       scalar2=None, op0=ALU.mult),
                          reads=["en_all", "km"], writes=["hp_t"])
                    p.dve(lambda e: e.tensor_tensor(out=ctmp[:], in0=ap_t[:], in1=carry[:], op=ALU.mult),
                          reads=["ap_t", "carry"], writes=["ctmp"])
                    p.dve(lambda e: e.tensor_tensor(out=carry[:], in0=ctmp[:], in1=hp_t[:], op=ALU.add),
                          reads=["ctmp", "hp_t"], writes=["carry"])
            else:
                p.dma("sync", en[:], ends, writes=["en"])
            for j in range(0 if fused else 3):
                p.dve(lambda e, j=j: e.tensor_tensor(out=ctmp[:], in0=en[:, j * 16 + 8:j * 16 + 16], in1=carry[:], op=ALU.mult),
                      reads=["en", "carry"], writes=["ctmp"])
                p.dve(lambda e, j=j: e.tensor_tensor(out=carry[:], in0=ctmp[:], in1=en[:, j * 16:j * 16 + 8], op=ALU.add),
                      reads=["en", "ctmp"], writes=["carry"])
        elif not fused:
            oTv = v3(oT)
        else:
            oallv = [a.rearrange("(c p) t -> p c t", p=128) for a in cx.over["o_all"]]
            sel = sb("sel", [128, 4])
            cand = [sb("cand%d" % i, [128, 8, 512], BF16) for i in range(2)]
            p.dma("sync", sel[:], sel_d, writes=["sel"])

        def norm_tile(x3, xkey, g_sb, gkey, h3, hkey):
            N = 512
            p.act(lambda e: e.activation(out=sq[:], in_=x3, func=AF.Square), reads=xkey, writes=["sq"])
            b, ps = cx.nextps()
            for c in range(8):
                p.mm(lambda e, c=c: e.matmul(ps[:], lhsT=ones[:], rhs=sq[:, c, :], start=(c == 0), stop=(c == 7)),
                     reads=["sq", "ones"], writes=[("ps", b)])
            p.act(lambda e, eps_t=cx.eps_t: e.activation(out=rs[:], in_=ps[:], func=AF.Sqrt, scale=1.0 / D, bias=eps_t[:, 0:1]),
                  reads=[("ps", b), "eps"], writes=["rs"])
            p.dve(lambda e: e.reciprocal(out=rstd[:], in_=rs[:]), reads=["rs"], writes=["rstd"])
            for c in range(8):
                p.dve(lambda e, c=c: e.scalar_tensor_tensor(out=h3[:, c, :], in0=x3[:, c, :], scalar=g_sb[:, c:c + 1],
                                                          in1=rstd[:], op0=ALU.mult, op1=ALU.mult),
                      reads=list(xkey) + ["rstd", gkey], writes=[(hkey, c)])

        wokeys = [("wo", n) for n in range(8)]
        cidx = [0]

        def load_gu(f2, slot):
            p.dma("gpsimd", wgs[slot][:], wg[f2], writes=[("wgs", slot)])
            p.dma("gpsimd", wus[slot][:], wu[f2], writes=[("wus", slot)])

        def load_d(n, slot):
            p.dma("gpsimd", wds[slot][:], wd[n], writes=[("wds", slot)])

        outs = []
        for half in range(2):
            hx = half * 1024
            xkeys = lambda tt: [("xr", tt, c) for c in range(8)]
            for tt in range(2):
                p.dma("sync", xr[:, :, tt * 512:(tt + 1) * 512], xTv[:, :, hx + tt * 512:hx + (tt + 1) * 512],
                      writes=xkeys(tt))
            load_gu(0, 0)
            load_gu(1, 1)
            for tt in range(2):
                t0 = hx + tt * 512
                yb = y[tt]
                ykeys = [("y", tt, c) for c in range(8)]
                if first:
                    for c in range(8):
                        s = cidx[0] % 2
                        cidx[0] += 1
                        p.dma("sync", hlc[s][:], hlv[:, c, t0:t0 + 512], writes=[("hlc", s)])
                        p.dma("sync", acc[s][:], acv[:, c, t0:t0 + 512], writes=[("acc", s)])
                        p.dma("sync", gtc[s][:], gtv[:, c, t0:t0 + 512], writes=[("gtc", s)])
                        p.dve(lambda e, s=s, c=c: e.scalar_tensor_tensor(out=hlc[s][:], in0=acc[s][:], scalar=carry[:, c:c + 1],
                                                                      in1=hlc[s][:], op0=ALU.mult, op1=ALU.add),
                              reads=[("hlc", s), ("acc", s), "carry"], writes=[("hlc", s)])
                        p.pool(lambda e, s=s, c=c, yb=yb: e.tensor_tensor(out=yb[:, c, :], in0=hlc[s][:], in1=gtc[s][:], op=ALU.mult),
                               reads=[("hlc", s), ("gtc", s)], writes=[("y", tt, c)])
                elif not fused:
                    p.dma("sync", yb[:], oTv[:, :, t0:t0 + 512], writes=ykeys)
                else:
                    for j in range(4):
                        cb = cand[j % 2]
                        ck = ("cand", j % 2)
                        p.dma("sync", cb[:], oallv[j][:, :, t0:t0 + 512], reads=[("d_oall", j)], writes=[ck])
                        if j == 0:
                            p.dve(lambda e, cb=cb, yb=yb: e.tensor_scalar(out=yb[:], in0=cb[:], scalar1=sel[:, 0:1], scalar2=None,
                                                                       op0=ALU.mult),
                                  reads=[ck, "sel"], writes=ykeys)
                        else:
                            p.dve(lambda e, cb=cb, yb=yb, j=j: e.scalar_tensor_tensor(out=yb[:], in0=cb[:], scalar=sel[:, j:j + 1],
                                                                                   in1=yb[:], op0=ALU.mult, op1=ALU.add),
                                  reads=[ck, "sel"] + ykeys, writes=ykeys)
                for n in range(8):
                    b, ps = cx.nextps()
                    for k in range(8):
                        p.mm(lambda e, k=k, n=n, ps=ps, yb=yb: e.matmul(ps[:], lhsT=wo[:, n, k * 128:(k + 1) * 128], rhs=yb[:, k, :],
                                                                     start=(k == 0), stop=(k == 7)),
                             reads=ykeys + [("wo", n)], writes=[("ps", b)])
                    p.dve(lambda e, n=n, ps=ps, tt=tt: e.tensor_tensor(out=xr[:, n, tt * 512:(tt + 1) * 512],
                                                                     in0=xr[:, n, tt * 512:(tt + 1) * 512], in1=ps[:], op=ALU.add),
                          reads=[("ps", b), ("xr", tt, n)], writes=[("xr", tt, n)])
                norm_tile(xr[:, :, tt * 512:(tt + 1) * 512], xkeys(tt), gf, "gf", h2[:, :, tt * 512:(tt + 1) * 512], ("h2", tt))
            for f2 in range(NF // 2):
                slot = f2 % 3
                if f2 + 2 < NF // 2:
                    load_gu(f2 + 2, (f2 + 2) % 3)
                if f2 == NF // 2 - 2:
                    load_d(0, 0)
                if f2 == NF // 2 - 1:
                    load_d(1, 1)
                for ff in range(2):
                    f = 2 * f2 + ff
                    for tt in range(2):
                        h2k = [(("h2", tt), c) for c in range(8)]
                        bg, psg = cx.nextps()
                        for k in range(8):
                            p.mm(lambda e, k=k, ff=ff, psg=psg, slot=slot, tt=tt: e.matmul(
                                psg[:], lhsT=wgs[slot][:, k * 256 + ff * 128:k * 256 + (ff + 1) * 128],
                                rhs=h2[:, k, tt * 512:(tt + 1) * 512], start=(k == 0), stop=(k == 7)),
                                reads=h2k + [("wgs", slot)], writes=[("ps", bg)])
                        bu, psu = cx.nextps()
                        for k in range(8):
                            p.mm(lambda e, k=k, ff=ff, psu=psu, slot=slot, tt=tt: e.matmul(
                                psu[:], lhsT=wus[slot][:, k * 256 + ff * 128:k * 256 + (ff + 1) * 128],
                                rhs=h2[:, k, tt * 512:(tt + 1) * 512], start=(k == 0), stop=(k == 7)),
                                reads=h2k + [("wus", slot)], writes=[("ps", bu)])
                        s = (f * 2 + tt) % 2
                        p.act(lambda e, s=s, psg=psg: e.activation(out=sg[s][:], in_=psg[:], func=AF.Silu),
                              reads=[("ps", bg)], writes=[("sg", s)])
                        p.dve(lambda e, s=s, psu=psu, f=f, tt=tt: e.tensor_tensor(out=actb[:, f, tt * 512:(tt + 1) * 512],
                                                                               in0=sg[s][:], in1=psu[:], op=ALU.mult),
                              reads=[("sg", s), ("ps", bu)], writes=[("actb", f, tt)])
            for n in range(8):
                slot = n % 2
                for tt in range(2):
                    b, ps = cx.nextps()
                    for f in range(NF):
                        p.mm(lambda e, f=f, ps=ps, slot=slot, tt=tt: e.matmul(
                            ps[:], lhsT=wds[slot][:, f * 128:(f + 1) * 128], rhs=actb[:, f, tt * 512:(tt + 1) * 512],
                            start=(f == 0), stop=(f == NF - 1)),
                            reads=[("actb", f, tt), ("wds", slot)], writes=[("ps", b)])
                    p.dve(lambda e, n=n, ps=ps, tt=tt: e.tensor_tensor(out=xr[:, n, tt * 512:(tt + 1) * 512],
                                                                     in0=xr[:, n, tt * 512:(tt + 1) * 512], in1=ps[:], op=ALU.add),
                          reads=[("ps", b), ("xr", tt, n)], writes=[("xr", tt, n)])
                if n + 2 < 8:
                    load_d(n + 2, slot)
            for tt in range(2):
                t0 = hx + tt * 512
                if first or not fused:
                    outs.append(p.dma("sync", x2v[:, :, t0:t0 + 512], xr[:, :, tt * 512:(tt + 1) * 512], reads=xkeys(tt)))
                hi_ = tt % len(hn)
                norm_tile(xr[:, :, tt * 512:(tt + 1) * 512], xkeys(tt), gn, "gn", hn[hi_], ("hn", hi_))
                hdst = hnv[:, :, t0:t0 + 512] if not (fused and first) else cx.over["hn_tiles"][half * 2 + tt].rearrange("(c p) t -> p c t", p=128)
                gti = half * 2 + tt
                outs.append(p.dma("sync", hdst, hn[hi_][:], reads=[(("hn", hi_), c) for c in range(8)], writes=[("d_hn", gti)]))
                if fused and first:
                    p.cc(lambda e, a=cx.over["hn_tiles"][gti], b=cx.over["hn_all"][gti]: e.collective_compute(
                        "AllGather", ALU.bypass, replica_groups=[[0, 1, 2, 3], [4, 5, 6, 7]], ins=[a], outs=[b]),
                         reads=[("d_hn", gti)], writes=[("d_hnall", gti)])
        cx.last_outs = outs
        if standalone:
            p.emit(final_wait_ops=outs[-8:])
    return nc


def col8(v):
    return np.ascontiguousarray(np.asarray(v, np.float32).reshape(8, 128).T)


def slab_kn(w, ncol):
    K = w.shape[0] // 128
    ns = w.shape[1] // ncol
    a = np.asarray(w, np.float32).reshape(K, 128, ns, ncol).transpose(2, 1, 0, 3)
    return np.ascontiguousarray(a.reshape(ns, 128, K * ncol))


_NC_CACHE = {}


def get_nc(name, fn):
    if name not in _NC_CACHE:
        _NC_CACHE[name] = fn()
    return _NC_CACHE[name]


def run(nc, in_maps):
    res = run_bass_kernel_spmd(nc, in_maps, core_ids=list(range(8)))
    return res.results


def core_tokens(r):
    return r // 4, (r % 4) * T


def stage_p1(inp):
    x = np.asarray(inp["x"], np.float32)
    maps = []
    common = dict(
        gvec=col8(inp["norm_mix_g"][0]),
        w_in=np.ascontiguousarray(inp["a_w_in"][0], dtype=np.float32),
        convw=np.ascontiguousarray(np.concatenate([col8(inp["a_conv_w"][0][j]) for j in range(4)], axis=1)),
        convb=col8(inp["a_conv_b"][0]),
        w_r=np.ascontiguousarray(np.asarray(inp["a_w_r"][0], np.float32).reshape(D, 256)),
        w_i=np.ascontiguousarray(np.asarray(inp["a_w_i"][0], np.float32).reshape(D, 256)),
        b_r=col8(inp["a_b_r"][0]), b_i=col8(inp["a_b_i"][0]), lam=col8(inp["a_lambda"][0]),
        ident=np.eye(128, dtype=np.float32),
    )
    for r in range(8):
        b, t0 = core_tokens(r)
        xs = np.zeros((TH + T, D), np.float32)
        if t0 == 0:
            xs[TH:] = x[b, 0:T]
        else:
            xs[:] = x[b, t0 - TH:t0 + T]
        m = dict(common)
        m["xT"] = np.ascontiguousarray(xs.T)
        maps.append(m)
    return maps


def ffn_weights(inp, layer):
    return dict(
        g_ffn=col8(inp["norm_ffn_g"][layer]),
        wg=slab_kn(inp["ffn_w_gate"][layer], 256),
        wu=slab_kn(inp["ffn_w_up"][layer], 256),
        wd=slab_kn(inp["ffn_w_down"][layer], 128),
    )


def stage_p2(inp, r1):
    x = np.asarray(inp["x"], np.float32)
    common = ffn_weights(inp, 0)
    common["w_out"] = slab_kn(inp["a_w_out"][0], 128)
    common["g_next"] = col8(inp["norm_mix_g"][1])
    maps = []
    for r in range(8):
        b, t0 = core_tokens(r)
        q = r % 4
        m = dict(common)
        m["xT"] = np.ascontiguousarray(x[b, t0:t0 + T].T)
        m["hloc"], m["acum"], m["gate"] = r1[r]["hloc"], r1[r]["acum"], r1[r]["gate"]
        ends = np.zeros((128, 3, 2, 8), np.float32)
        for j in range(3):
            src = q - 3 + j
            if src >= 0:
                rr = b * 4 + src
                ends[:, j, 0, :] = col8(r1[rr]["hloc"][:, T - 1])
                ends[:, j, 1, :] = col8(r1[rr]["acum"][:, T - 1])
        m["ends"] = np.ascontiguousarray(ends.reshape(128, 48))
        maps.append(m)
    return maps


def build_p3(cx=None):
    standalone = cx is None
    if standalone:
        cx = Ctx()
    fused = cx.fused
    cx.prefix = "p3_" if fused else ""
    nc, p = cx.nc, cx.p
    hn = cx.din("hn", [D, S], BF16)
    wq = cx.din("wq", [128, 8 * 256])
    wk = cx.din("wk", [128, 8 * 256])
    wv = cx.din("wv", [128, 8 * 256])
    umat = cx.din("umat", [128, 128])
    masks = cx.din("masks", [128, 4 * 512])
    oT = cx.dout("oT", [256, S], BF16)
    NQ = S // 512
    if not fused:
        hnv = hn.rearrange("(c p) t -> p c t", p=128)
        hn_tile = lambda ti: hnv[:, :, ti * 512:(ti + 1) * 512]
        o_dst = lambda head, Qi: oT[head * 64:(head + 1) * 64, Qi * 512:(Qi + 1) * 512]
    else:
        hn_tile = lambda ti: hn[ti % 4][(ti // 4) * D:(ti // 4 + 1) * D, :].rearrange("(c p) t -> p c t", p=128)
        o_dst = lambda head, Qi: oT[Qi // 4][head * 64:(head + 1) * 64, (Qi % 4) * 512:(Qi % 4 + 1) * 512]

    with cx.phase("p3_"):
        cx.init_psum()
        sb = cx.sb
        hb = [sb("hb%d" % i, [128, 8, 512], BF16) for i in range(2)]
        wqs = sb("wqs", [128, 8 * 256], BF16)
        wks = sb("wks", [128, 8 * 256], BF16)
        wvs = sb("wvs", [128, 8 * 256], BF16)
        qs = sb("qs", [128, 2, S], BF16)
        kt = sb("kt", [128, 2, S], BF16)
        vv = sb("vv", [128, S // 128, 256], BF16)
        um = sb("um", [128, 128])
        onesr = sb("onesr", [128, 128])
        mk = sb("mk", [128, 4, 512])
        ob = [sb("ob%d" % i, [64, 512], BF16) for i in range(2)]

        p.dma("sync", mk[:], masks.rearrange("p (m f) -> p m f", m=4), writes=["mk"])
        umf = sb("umf", [128, 128])
        onesf = sb("onesf", [128, 128])
        p.dma("sync", umf[:], umat, writes=["umf"])
        p.dve(lambda e: e.memset(onesf[:], -1.0), writes=["onesf"])
        p.dve(lambda e: e.tensor_scalar(out=um[:].bitcast(F32R), in0=umf[:], scalar1=-1.0, scalar2=None, op0=ALU.mult), reads=["umf"], writes=["um"])
        p.dve(lambda e: e.tensor_copy(out=onesr[:].bitcast(F32R), in_=onesf[:]), reads=["onesf"], writes=["onesr"])
        p.dma("gpsimd", wqs[:], wq, writes=["wqs"])
        p.dma("gpsimd", wks[:], wk, writes=["wks"])
        p.dma("gpsimd", wvs[:], wv, writes=["wvs"])

        import os
        for oi_, ti in enumerate(sorted(range(int(os.environ.get("P3_PT", str(NQ)))), key=lambda a: (a % 4, a // 4))):
            hbuf = hb[oi_ % 2]
            hkey = ("hb", oi_ % 2)
            p.dma("sync", hbuf[:], hn_tile(ti), reads=[("d_hnall", ti % 4)], writes=[hkey])
            tsl = slice(ti * 512, (ti + 1) * 512)
            SKIP = os.environ.get("P3_SKIP", "")
            for pair in range(2):
                if "q" in SKIP:
                    break
                b, ps = cx.nextps()
                for k in range(8):
                    p.mm(lambda e, k=k, pair=pair, ps=ps, hbuf=hbuf: e.matmul(
                        ps[:], lhsT=wqs[:, k * 256 + pair * 128:k * 256 + (pair + 1) * 128], rhs=hbuf[:, k, :],
                        start=(k == 0), stop=(k == 7)), reads=[hkey, "wqs"], writes=[("ps", b)])
                p.act(lambda e, pair=pair, ps=ps, tsl=tsl: e.mul(out=qs[:, pair, tsl], in_=ps[:], mul=0.125),
                      reads=[("ps", b)], writes=[("qs", pair, ti)])
                b, ps = cx.nextps()
                for k in range(8):
                    p.mm(lambda e, k=k, pair=pair, ps=ps, hbuf=hbuf: e.matmul(
                        ps[:], lhsT=wks[:, k * 256 + pair * 128:k * 256 + (pair + 1) * 128], rhs=hbuf[:, k, :],
                        start=(k == 0), stop=(k == 7)), reads=[hkey, "wks"], writes=[("ps", b)])
                p.act(lambda e, pair=pair, ps=ps, tsl=tsl: e.copy(out=kt[:, pair, tsl], in_=ps[:]),
                      reads=[("ps", b)], writes=[("kt", pair, ti)])
            for tb in range(4):
                if "v" in SKIP:
                    break
                b, ps = cx.nextps()
                for k in range(8):
                    p.mm(lambda e, k=k, tb=tb, ps=ps, hbuf=hbuf: e.matmul(
                        ps[:, 0:256], lhsT=hbuf[:, k, tb * 128:(tb + 1) * 128], rhs=wvs[:, k * 256:(k + 1) * 256],
                        start=(k == 0), stop=(k == 7)), reads=[hkey, "wvs"], writes=[("ps", b)])
                p.dve(lambda e, tb=tb, ps=ps, ti=ti: e.tensor_copy(out=vv[:, ti * 4 + tb, :], in_=ps[:, 0:256]),
                      reads=[("ps", b)], writes=[("vv", ti)])

        outs = []
        um_r = um[:].bitcast(F32R)
        ones_r = onesr[:].bitcast(F32R)
        import os
        DBG_H = int(os.environ.get("P3_HEADS", "4"))
        DBG_Q = int(os.environ.get("P3_NQ", str(NQ)))
        LB = 1
        LC = 2
        NSP = LB + 2
        NWT = LC - LB + 2
        NRS = LB + 2
        etmp = [sb("etmp%d" % i, [128, 2, 512]) for i in range(NSP)]
        pw = [sb("pw%d" % i, [128, 2, 512]) for i in range(2)]
        spt = [sb("sptp%d" % i, [128, 2, 512]) for i in range(NSP)]
        wt = [sb("wtp%d" % i, [128, 2, 512], BF16) for i in range(NWT)]
        rsum = [sb("rsump%d" % i, [128, 2, 512]) for i in range(NRS)]
        one_t = sb("one_t", [128, 1])
        p.dve(lambda e: e.memset(one_t[:], 1.0), writes=["one"])
        psb = cx.psbig
        pairs = []
        gi = 0
        for Qi in range(DBG_Q):
            for pair in range(2):
                for hh in range(2):
                    head = pair * 2 + hh
                    if head >= DBG_H:
                        continue
                    nkb = 4 * (Qi + 1)
                    for kb in range(nkb - 1, -1, -2):
                        pairs.append(dict(pair=pair, hh=hh, head=head, Qi=Qi, kb=kb, first=(kb == nkb - 1), last=(kb == 1),
                                          diag=(kb - 4 * Qi >= 0), dpi=(0 if kb - 4 * Qi == 3 else 1), g=gi))
                    gi += 1
        n = len(pairs)
        for j, t in enumerate(pairs):
            t["zb"] = 2 * (j % 3)
            t["db"] = t["zb"]
            t["sp"] = j % NSP
            t["wt"] = j % NWT
            t["rs"] = j % NRS
            t["ob"] = 6 + t["g"] % 2
            t["rsp"] = None if t["first"] else pairs[j - 1]["rs"]

        def stage_a(t):
            pair, prt = t["pair"], slice(64 * t["hh"], 64 * t["hh"] + 64)
            qsl = slice(t["Qi"] * 512, (t["Qi"] + 1) * 512)
            zb, sp_t, e_t = t["zb"], spt[t["sp"]], etmp[t["sp"]]
            zkeys = [("ps", zb), ("ps", zb + 1)]
            for u in range(2):
                kb = t["kb"] - u
                ksl = slice(kb * 128, (kb + 1) * 128)
                p.mm(lambda e, u=u, ksl=ksl: e.matmul(psb[:, zb + u, :], lhsT=kt[prt, pair, ksl], rhs=qs[prt, pair, qsl],
                                                      start=True, stop=False),
                     reads=[("qs", pair, t["Qi"]), ("kt", pair, kb // 4)], writes=[("ps", zb + u)])
            p.act(lambda e: e.activation(out=e_t[:], in_=psb[:, zb:zb + 2, :], func=AF.Exp),
                  reads=zkeys, writes=[("etmp", t["sp"])])
            p.act(lambda e: e.activation(out=sp_t[:].bitcast(F32R), in_=e_t[:], func=AF.Ln, bias=1.0),
                  reads=[("etmp", t["sp"])], writes=[("spt", t["sp"])])
            if t["diag"]:
                d0 = 2 * t["dpi"]
                p.dve(lambda e: e.tensor_tensor(out=sp_t[:].bitcast(F32R), in0=sp_t[:], in1=mk[:, d0:d0 + 2, :], op=ALU.mult),
                      reads=[("spt", t["sp"]), "mk"], writes=[("spt", t["sp"])])
            rn = rsum[t["rs"]]
            if t["first"]:
                p.dve(lambda e: e.tensor_copy(out=rn[:, 0, :].bitcast(F32R), in_=sp_t[:, 0, :]),
                      reads=[("spt", t["sp"])], writes=[("rsum", t["rs"], 0)])
            else:
                rp = rsum[t["rsp"]]
                p.dve(lambda e: e.tensor_tensor(out=rn[:, 0, :].bitcast(F32R), in0=rp[:, 1, :], in1=sp_t[:, 0, :], op=ALU.add),
                      reads=[("spt", t["sp"]), ("rsum", t["rsp"], 1)], writes=[("rsum", t["rs"], 0)])
            if not t["last"]:
                p.dve(lambda e: e.tensor_tensor(out=rn[:, 1, :].bitcast(F32R), in0=rn[:, 0, :], in1=sp_t[:, 1, :], op=ALU.add),
                      reads=[("spt", t["sp"]), ("rsum", t["rs"], 0)], writes=[("rsum", t["rs"], 1)])

        def stage_b(t):
            db, sp_t, w_t, e_t = t["db"], spt[t["sp"]], wt[t["wt"]], etmp[t["sp"]]
            rn = rsum[t["rs"]]
            p_t = pw[t["wt"] % 2]
            pk = ("pw", t["wt"] % 2)
            for u in range(2):
                first_tile = t["first"] and u == 0
                p.mm(lambda e, u=u, first_tile=first_tile: e.matmul(psb[:, db + u, :], lhsT=um_r, rhs=sp_t[:, u, :].bitcast(F32R),
                                                                    start=False, stop=first_tile),
                     reads=[("spt", t["sp"]), "um"], writes=[("ps", db + u)])
            for u in range(2):
                first_tile = t["first"] and u == 0
                if not first_tile:
                    if u == 0:
                        rsrc, rkey = rsum[t["rsp"]][:, 1, :], ("rsum", t["rsp"], 1)
                    else:
                        rsrc, rkey = rn[:, 0, :], ("rsum", t["rs"], 0)
                    p.mm(lambda e, u=u, rsrc=rsrc: e.matmul(psb[:, db + u, :], lhsT=ones_r, rhs=rsrc.bitcast(F32R),
                                                            start=False, stop=True),
                         reads=[rkey, "onesr"], writes=[("ps", db + u)])
            p.act(lambda e: e.activation(out=w_t[:], in_=psb[:, db:db + 2, :], func=AF.Exp),
                  reads=[("ps", db), ("ps", db + 1)], writes=[("wt", t["wt"], 0), ("wt", t["wt"], 1)])
            if t["diag"]:
                d0 = 2 * t["dpi"]
                p.dve(lambda e: e.tensor_tensor(out=w_t[:], in0=w_t[:], in1=mk[:, d0:d0 + 2, :], op=ALU.mult),
                      reads=[("wt", t["wt"], 0), ("wt", t["wt"], 1), "mk"], writes=[("wt", t["wt"], 0), ("wt", t["wt"], 1)])

        def stage_c(t):
            ops_, w_t, head = cx.ps[t["ob"]], wt[t["wt"]], t["head"]
            for u in range(2):
                kb = t["kb"] - u
                p.mm(lambda e, u=u, kb=kb: e.matmul(ops_[0:64, :], lhsT=vv[:, kb, head * 64:(head + 1) * 64], rhs=w_t[:, u, :],
                                                    start=(t["first"] and u == 0), stop=(t["last"] and u == 1)),
                     reads=[("wt", t["wt"], u), ("vv", kb // 4)], writes=[("ps", t["ob"])])
            if t["last"]:
                obuf = ob[t["g"] % 2]
                p.dve(lambda e: e.tensor_copy(out=obuf[:], in_=ops_[0:64, :]),
                      reads=[("ps", t["ob"])], writes=[("ob", t["g"] % 2)])
                qk = t["Qi"] // 4
                outs.append(p.dma("sync", o_dst(head, t["Qi"]), obuf[:], reads=[("ob", t["g"] % 2)], writes=[("d_o", qk, t["Qi"] % 4, head)]))
                if fused and t["Qi"] % 4 == 3 and head == 3:
                    p.cc(lambda e, a=cx.over["oT"][qk], b=cx.over["o_all"][qk]: e.collective_compute(
                        "AllGather", ALU.bypass, replica_groups=[[0, 1, 2, 3], [4, 5, 6, 7]], ins=[a], outs=[b]),
                         reads=[("d_o", qk, a, b) for a in range(4) for b in range(4)], writes=[("d_oall", qk)])

        for i in range(n + LC):
            if i < n:
                stage_a(pairs[i])
            if 0 <= i - LB < n:
                stage_b(pairs[i - LB])
            if 0 <= i - LC < n:
                stage_c(pairs[i - LC])
        cx.last_outs = outs
        if standalone:
            p.emit(final_wait_ops=outs[-8:])
    return nc


def attn_consts():
    pp = np.arange(128)[:, None]
    um = (pp >= np.arange(128)[None, :]).astype(np.float32)
    f = np.arange(512)[None, :]
    mk = np.concatenate([(f > (128 * m + pp)).astype(np.float32) for m in (3, 2, 1, 0)], axis=1)
    return um, np.ascontiguousarray(mk)


def stage_p3(inp, r2):
    um, mk = attn_consts()
    wqkv = np.asarray(inp["b_w_qkv"][0], np.float32)
    maps = []
    for r in range(8):
        b, hq = r // 4, r % 4
        hn = np.concatenate([r2[b * 4 + q]["hnT"] for q in range(4)], axis=1)
        cols = slice(hq * 256, (hq + 1) * 256)
        m = dict(
            hn=np.ascontiguousarray(hn),
            wq=slab_kn(wqkv[:, 0:D][:, cols], 256)[0],
            wk=slab_kn(wqkv[:, D:2 * D][:, cols], 256)[0],
            wv=slab_kn(wqkv[:, 2 * D:3 * D][:, cols], 256)[0],
            umat=um, masks=mk,
        )
        maps.append(m)
    return maps


def stage_p4(inp, r2, r3):
    common = ffn_weights(inp, 1)
    common["w_out"] = slab_kn(inp["b_w_out"][0], 128)
    common["g_next"] = col8(inp["final_g"])
    maps = []
    for r in range(8):
        b, q = r // 4, r % 4
        m = dict(common)
        m["xT"] = r2[r]["x2T"]
        m["oT"] = np.ascontiguousarray(
            np.concatenate([r3[b * 4 + hq]["oT"][:, q * T:(q + 1) * T] for hq in range(4)], axis=0))
        maps.append(m)
    return maps


def build_fused():
    cx = Ctx()
    cx.fused = True
    nc, p = cx.nc, cx.p
    xT = nc.dram_tensor("p1_xT", [D, TH + T], F32, kind="ExternalInput").ap()
    out = nc.dram_tensor("out", [D, T], F32, kind="ExternalOutput").ap()
    hloc = cx.dint("i_hloc", [D, T])
    acum = cx.dint("i_acum", [D, T])
    gate = cx.dint("i_gate", [D, T], BF16)
    ends_src = cx.dint("i_ends_src", [128, 16])
    ends_all = cx.dint("i_ends_all", [4 * 128, 16])
    x2 = cx.dint("i_x2", [D, T])
    hn_src = [cx.dint("i_hn_src%d" % k, [D, 512], BF16) for k in range(4)]
    hn_all = [cx.dint("i_hn_all%d" % k, [4 * D, 512], BF16) for k in range(4)]
    o_src = [cx.dint("i_o_src%d" % k, [256, T], BF16) for k in range(4)]
    o_all = [cx.dint("i_o_all%d" % k, [4 * 256, T], BF16) for k in range(4)]
    grp4 = [[0, 1, 2, 3], [4, 5, 6, 7]]
    grp8 = [list(range(8))]

    cx.over = dict(xT=xT, hloc=hloc, acum=acum, gate=gate, ends_src=ends_src)
    build_p1(cx)
    p.barrier()
    import os
    STOP = int(os.environ.get("FUSED_STOP", "99"))
    if STOP <= 0:
        p.emit()
        return nc
    p.cc(lambda e: e.collective_compute("AllGather", ALU.bypass, replica_groups=grp4, ins=[ends_src], outs=[ends_all]),
         writes=["d_ends_all"])
    pass
    if STOP <= 1:
        p.emit()
        return nc
    cx.over = dict(xT=xT[:, TH:TH + T], hloc=hloc, acum=acum, gate=gate, ends_all=ends_all, x2T=x2, hnT=x2, hn_tiles=hn_src, hn_all=hn_all)
    build_p24(True, False, cx)
    p.barrier(skip_cc=True)
    if STOP <= 2:
        p.emit()
        return nc
    pass
    if STOP <= 3:
        p.emit()
        return nc
    cx.over = dict(hn=hn_all, oT=o_src, o_all=o_all)
    build_p3(cx)
    p.barrier(skip_cc=True)
    if STOP <= 4:
        p.emit()
        return nc
    pass
    if STOP <= 5:
        p.emit()
        return nc
    cx.over = dict(xT=x2, x2T=x2, hnT=out, o_all=o_all)
    build_p24(False, True, cx)
    p.emit(final_wait_ops=cx.last_outs[-8:])
    return nc


def stage_fused(inp):
    m1 = stage_p1(inp)
    um, mk = attn_consts()
    wqkv = np.asarray(inp["b_w_qkv"][0], np.float32)
    f0 = ffn_weights(inp, 0)
    f0["w_out"] = slab_kn(inp["a_w_out"][0], 128)
    f0["g_next"] = col8(inp["norm_mix_g"][1])
    f1 = ffn_weights(inp, 1)
    f1["w_out"] = slab_kn(inp["b_w_out"][0], 128)
    f1["g_next"] = col8(inp["final_g"])
    maps = []
    for r in range(8):
        b, q = r // 4, r % 4
        m = {}
        for k, v in m1[r].items():
            m["p1_" + k] = v
        for k, v in f0.items():
            m["p2_" + k] = v
        for k, v in f1.items():
            m["p4_" + k] = v
        keep = np.zeros((128, 4), np.float32)
        for rr in range(q):
            keep[:, rr] = 1.0
        m["p2_km"] = keep
        m["p2_hm"] = 1.0 - keep
        sel = np.zeros((128, 4), np.float32)
        sel[:, q] = 1.0
        m["p4_sel"] = sel
        cols = slice(q * 256, (q + 1) * 256)
        m["p3_wq"] = slab_kn(wqkv[:, 0:D][:, cols], 256)[0]
        m["p3_wk"] = slab_kn(wqkv[:, D:2 * D][:, cols], 256)[0]
        m["p3_wv"] = slab_kn(wqkv[:, 2 * D:3 * D][:, cols], 256)[0]
        m["p3_umat"] = um
        m["p3_masks"] = mk
        maps.append(m)
    return maps


def kernel(**inp):
    inp = {k: np.asarray(v) for k, v in inp.items()}
    res = run(get_nc("fused", build_fused), stage_fused(inp))
    out = np.stack([np.concatenate([res[b * 4 + q]["out"].T for q in range(4)], axis=0) for b in range(2)])
    return np.ascontiguousarray(out.astype(np.float32))
```

```python
import contextlib
import numpy as np
import ml_dtypes
import concourse.bass as bass
import concourse.mybir as mybir
from concourse.bass_utils import run_bass_kernel_spmd

F32 = mybir.dt.float32
F32R = mybir.dt.float32r
BF16 = mybir.dt.bfloat16
AF = mybir.ActivationFunctionType
ALU = mybir.AluOpType
NPBF = ml_dtypes.bfloat16

D = 1024
DFF = 2816
NF = DFF // 128
T = 2048
S = 8192
TH = 4
EPS = 1e-6
COMPUTE = ("tensor", "vector", "scalar", "gpsimd")


class Prog:
    def __init__(self, nc, n_dma_sems=8):
        self.nc = nc
        self.ops = []
        self.state = {}
        self.n_dma_sems = n_dma_sems

    def add(self, eng, fn, reads=(), writes=(), dma=False):
        idx = len(self.ops)
        ops = self.ops
        deps = set()
        for k in reads:
            st = self.state.setdefault(k, [None, []])
            if st[0] is not None:
                deps.add(st[0])
        for k in writes:
            st = self.state.setdefault(k, [None, []])
            if st[0] is not None:
                deps.add(st[0])
            deps.update(st[1])
        for k in reads:
            st = self.state[k]
            if not dma:
                st[1] = [r for r in st[1] if ops[r]["dma"] or ops[r]["eng"] != eng]
            st[1].append(idx)
        for k in writes:
            self.state[k] = [idx, []]
        deps.discard(idx)
        ops.append(dict(eng=eng, fn=fn, deps=deps, dma=dma))
        return idx

    def mm(self, fn, reads=(), writes=()):
        return self.add("tensor", fn, reads, writes)

    def act(self, fn, reads=(), writes=()):
        return self.add("scalar", fn, reads, writes)

    def dve(self, fn, reads=(), writes=()):
        return self.add("vector", fn, reads, writes)

    def pool(self, fn, reads=(), writes=()):
        return self.add("gpsimd", fn, reads, writes)

    def dma(self, q, out, in_, reads=(), writes=()):
        return self.add(q, lambda e: e.dma_start(out=out, in_=in_), reads, writes, dma=True)

    def barrier(self, skip_cc=False):
        for e in ("sync", "scalar", "vector", "gpsimd", "tensor"):
            self.ops.append(dict(eng=e, fn=None, deps=set(), dma=False, barrier=True, skip_cc=skip_cc))

    def cc(self, fn, reads=(), writes=()):
        idx = self.add("gpsimd", fn, reads, writes)
        self.ops[idx]["cc"] = True
        return idx

    def emit(self, final_wait_ops=()):
        nc = self.nc
        ops = self.ops
        n = len(ops)

        def skip(d, e):
            od = ops[d]
            return od["eng"] == "tensor" and e == "tensor" and not od["dma"]

        has_dep = [False] * n
        last_op = {}
        for i, o in enumerate(ops):
            if o.get("barrier"):
                for e2 in COMPUTE:
                    if e2 in last_op:
                        has_dep[last_op[e2]] = True
                continue
            if o["eng"] in COMPUTE and not o["dma"] and not o.get("cc"):
                last_op[o["eng"]] = i
            for d in o["deps"]:
                if not skip(d, o["eng"]):
                    has_dep[d] = True
        for d in final_wait_ops:
            has_dep[d] = True
        engs = ("sync", "scalar", "vector", "gpsimd", "tensor")
        stack = contextlib.ExitStack()
        sems = {}
        for e in COMPUTE:
            sems[e] = stack.enter_context(nc.semaphore("s_" + e))
        dma_sems = {}
        for q in ("sync", "scalar", "gpsimd"):
            for j in range(self.n_dma_sems):
                dma_sems[(q, j)] = stack.enter_context(nc.semaphore("d_%s%d" % (q, j)))
        sems["cc"] = stack.enter_context(nc.semaphore("s_cc"))
        cnt = {k: 0 for k in list(sems) + list(dma_sems)}
        rr = {"sync": 0, "scalar": 0, "gpsimd": 0}
        sig = [None] * n
        waits = [None] * n
        waited = {e: {} for e in engs}
        for i, o in enumerate(ops):
            e = o["eng"]
            w = []

            def need(key, val):
                if val > 0 and waited[e].get(key, 0) < val:
                    waited[e][key] = val
                    w.append((key, val))

            if o.get("barrier"):
                for key in list(cnt):
                    if key != e and not (o.get("skip_cc") and key == "cc"):
                        need(key, cnt[key])
                waits[i] = w
                continue
            for d in sorted(o["deps"]):
                if skip(d, e):
                    continue
                need(*sig[d])
            if o.get("cc"):
                cnt["cc"] += 1
                sig[i] = ("cc", cnt["cc"])
            elif o["dma"]:
                j = rr[e]
                rr[e] = (j + 1) % self.n_dma_sems
                key = (e, j)
                need(key, cnt[key])
                cnt[key] += 16
                sig[i] = (key, cnt[key])
            elif has_dep[i]:
                cnt[e] += 1
                sig[i] = (e, cnt[e])
            waits[i] = w
        allsems = dict(sems)
        allsems.update(dma_sems)
        self.max_counts = dict(cnt)
        final = [sig[d] for d in final_wait_ops]

        with stack:
            with nc.Block() as block:
                def make(ename):
                    def body(eng):
                        for i, o in enumerate(ops):
                            if o["eng"] != ename:
                                continue
                            for key, val in waits[i]:
                                eng.wait_ge(allsems[key], val)
                            if o["fn"] is None:
                                continue
                            inst = o["fn"](eng)
                            if sig[i] is not None:
                                inst.then_inc(allsems[sig[i][0]], 16 if o["dma"] else 1)
                        if ename == "sync":
                            for key, val in final:
                                eng.wait_ge(allsems[key], val)
                    return body

                block.sync(make("sync"))
                block.scalar(make("scalar"))
                block.vector(make("vector"))
                block.gpsimd(make("gpsimd"))
                block.tensor(make("tensor"))


class Ctx:
    def __init__(self):
        self.nc = bass.Bass("TRN2", target_bir_lowering=False)
        self.stack = contextlib.ExitStack()
        self.gstack = contextlib.ExitStack()
        self.p = Prog(self.nc)
        self.ps = None
        self.psi = 0
        self.prefix = ""
        self.over = {}
        self.fused = False

    def din(self, name, shape, dt=F32):
        if name in self.over:
            return self.over[name]
        return self.nc.dram_tensor(self.prefix + name, list(shape), dt, kind="ExternalInput").ap()

    def dout(self, name, shape, dt=F32):
        if name in self.over:
            return self.over[name]
        return self.nc.dram_tensor(self.prefix + name, list(shape), dt, kind="ExternalOutput").ap()

    def dint(self, name, shape, dt=F32):
        return self.nc.dram_tensor(name, list(shape), dt, addr_space="Local", kind="Internal").ap()

    def sb(self, name, shape, dt=F32):
        return self.stack.enter_context(self.nc.sbuf_tensor(self.prefix + "s_" + name, list(shape), dt))

    def init_psum(self):
        if self.ps is None:
            self.psbig = self.gstack.enter_context(self.nc.psum_tensor("psbig", [128, 8, 512], F32))
            self.ps = [self.psbig[:, i, :] for i in range(8)]

    def phase(self, prefix):
        self.stack = contextlib.ExitStack()
        return self.stack

    def nextps(self):
        b = self.psi
        self.psi = (b + 1) % 8
        return b, self.ps[b]


def emit_norm(cx, x3, xkey, N, g_sb, h3, hkey, sq, rs, rstd, ones):
    p = cx.p
    p.act(lambda e: e.activation(out=sq[:, :, :N], in_=x3, func=AF.Square), reads=[xkey], writes=["sq"])
    b, ps = cx.nextps()
    for c in range(8):
        p.mm(lambda e, c=c: e.matmul(ps[:, :N], lhsT=ones[:], rhs=sq[:, c, :N], start=(c == 0), stop=(c == 7)),
             reads=["sq", "ones"], writes=[("ps", b)])
    p.act(lambda e, eps_t=cx.eps_t: e.activation(out=rs[:, :N], in_=ps[:, :N], func=AF.Sqrt, scale=1.0 / D, bias=eps_t[:, 0:1]),
          reads=[("ps", b), "eps"], writes=["rs"])
    p.dve(lambda e: e.reciprocal(out=rstd[:, :N], in_=rs[:, :N]), reads=["rs"], writes=["rstd"])
    for c in range(8):
        eng = "vector"
        p.add(eng, lambda e, c=c: e.scalar_tensor_tensor(out=h3[:, c, :], in0=x3[:, c, :], scalar=g_sb[:, c:c + 1],
                                                     in1=rstd[:, :N], op0=ALU.mult, op1=ALU.mult),
              reads=[xkey, "rstd", "gvec"], writes=[(hkey, c)])


def load_w_bf16(cx, dst, src, K, key, nsplit=None):
    v = src.rearrange("(k p) n -> p k n", p=128)
    for k in range(K):
        cx.p.dma("gpsimd", dst[:, k, :], v[:, k, :], writes=[(key, k)])


def build_p1(cx=None):
    standalone = cx is None
    if standalone:
        cx = Ctx()
    cx.prefix = "p1_" if cx.fused else ""
    nc, p = cx.nc, cx.p
    xT = cx.din("xT", [D, TH + T])
    gvec = cx.din("gvec", [128, 8])
    w_in = cx.din("w_in", [D, 2 * D])
    convw = cx.din("convw", [128, 32])
    convb = cx.din("convb", [128, 8])
    w_r = cx.din("w_r", [D, 256])
    w_i = cx.din("w_i", [D, 256])
    b_r = cx.din("b_r", [128, 8])
    b_i = cx.din("b_i", [128, 8])
    lam = cx.din("lam", [128, 8])
    ident = cx.din("ident", [128, 128])
    hloc = cx.dout("hloc", [D, T])
    acum = cx.dout("acum", [D, T])
    gate = cx.dout("gate", [D, T], BF16)

    xTv = xT.rearrange("(c p) t -> p c t", p=128)
    hlv = hloc.rearrange("(c p) t -> p c t", p=128)
    acv = acum.rearrange("(c p) t -> p c t", p=128)
    gtv = gate.rearrange("(c p) t -> p c t", p=128)

    with cx.phase("p1_"):
        cx.init_psum()
        sb = cx.sb
        xt = [sb("xt%d" % i, [128, 8, 512]) for i in range(1)]
        sq = sb("sq", [128, 8, 512], BF16)
        rs = sb("rs", [128, 512])
        rstd = sb("rstd", [128, 512])
        h = sb("h", [128, 8, 512], BF16)
        win = sb("win", [128, 8, 2 * D], BF16)
        wr = sb("wr", [128, 8, 256], BF16)
        wi = sb("wi", [128, 8, 256], BF16)
        diag = sb("diag", [128, 32, 128], BF16)
        identf = sb("identf", [128, 128])
        ones = sb("ones", [128, 128], BF16)
        zeros = sb("zeros", [128, 512])
        cx.eps_t = sb("eps", [128, 1])
        g_sb = sb("g_sb", [128, 8])
        cw = sb("cw", [128, 32])
        cb = sb("cb", [128, 8])
        br = sb("br", [128, 8])
        bi = sb("bi", [128, 8])
        lm = sb("lm", [128, 8])
        c8 = sb("c8", [128, 8])
        c16 = sb("c16", [128, 8])
        xb = sb("xb", [128, 8, TH + T], BF16)
        gt = [sb("gt%d" % i, [128, 8, 512], BF16) for i in range(2)]
        xc = sb("xc", [128, 2, 512])
        xcb = sb("xcb", [128, 2, 512], BF16)
        rt = [sb("rt%d" % i, [128, 512]) for i in range(2)]
        it = [sb("it%d" % i, [128, 512]) for i in range(2)]
        a2 = [sb("a2%d" % i, [128, 512]) for i in range(2)]
        at = sb("at", [128, 2, 512])
        ut = sb("ut", [128, 2, 512])
        hl = [sb("hl%d" % i, [128, 8, 512]) for i in range(1)]
        ac = [sb("ac%d" % i, [128, 8, 512]) for i in range(1)]
        hlast = sb("hlast", [128, 8])
        alast = sb("alast", [128, 8])

        for dst, src, key in ((g_sb, gvec, "gvec"), (cw, convw, "cw"), (cb, convb, "cb"), (br, b_r, "br"),
                              (bi, b_i, "bi"), (lm, lam, "lm"), (identf, ident, "identf")):
            p.dma("sync", dst[:], src, writes=[key])
        p.dve(lambda e: e.memset(ones[:], 1.0), writes=["ones"])
        p.dve(lambda e: e.memset(zeros[:], 0.0), writes=["zeros"])
        p.dve(lambda e, eps_t=cx.eps_t: e.memset(eps_t[:], EPS), writes=["eps"])
        load_w_bf16(cx, win, w_in, 8, "win")
        load_w_bf16(cx, wr, w_r, 8, "wr")
        load_w_bf16(cx, wi, w_i, 8, "wi")
        p.act(lambda e: e.activation(out=c8[:], in_=lm[:], func=AF.Sigmoid), reads=["lm"], writes=["c8"])
        p.act(lambda e: e.activation(out=c8[:], in_=c8[:], func=AF.Ln), reads=["c8"], writes=["c8"])
        p.act(lambda e: e.mul(out=c16[:], in_=c8[:], mul=16.0), reads=["c8"], writes=["c16"])
        p.act(lambda e: e.mul(out=c8[:], in_=c8[:], mul=8.0), reads=["c8"], writes=["c8"])
        for jc in range(32):
            p.dve(lambda e, jc=jc: e.tensor_scalar(out=diag[:, jc, :], in0=identf[:], scalar1=cw[:, jc:jc + 1],
                                                 scalar2=None, op0=ALU.mult),
                  reads=["identf", "cw"], writes=[("diag", jc)])

        winkeys = [("win", k) for k in range(8)]
        wrkeys = [("wr", k) for k in range(8)]
        wikeys = [("wi", k) for k in range(8)]

        def do_tile(ti):
            halo = ti < 0
            N = TH if halo else 512
            t0 = 0 if halo else TH + ti * 512
            xbuf = xt[0]
            xkey = ("xt", 0)
            x3 = xbuf[:, :, :N]
            p.dma("sync", x3, xTv[:, :, t0:t0 + N], writes=[xkey])
            emit_norm(cx, x3, xkey, N, g_sb, h[:, :, :N], "h", sq, rs, rstd, ones)
            hkeys = [("h", c) for c in range(8)]
            gbuf = gt[ti % 2]
            gkey = ("gt", ti % 2)
            for n in range(0 if not halo else 8, 16):
                b, ps = cx.nextps()
                for k in range(8):
                    p.mm(lambda e, k=k, n=n, ps=ps: e.matmul(ps[:, :N], lhsT=win[:, k, n * 128:(n + 1) * 128],
                                                          rhs=h[:, k, :N], start=(k == 0), stop=(k == 7)),
                         reads=hkeys + winkeys, writes=[("ps", b)])
                if n < 8:
                    p.act(lambda e, n=n, ps=ps: e.activation(out=gbuf[:, n, :], in_=ps[:, :N], func=AF.Gelu_apprx_tanh),
                          reads=[("ps", b)], writes=[gkey])
                else:
                    p.act(lambda e, n=n, ps=ps: e.copy(out=xb[:, n - 8, t0:t0 + N], in_=ps[:, :N]),
                          reads=[("ps", b)], writes=[("xb", n - 8, ti)])
            if halo:
                return
            p.dma("sync", gtv[:, :, ti * 512:(ti + 1) * 512], gbuf[:], reads=[gkey])
            hbuf, abuf = hl[0], ac[0]
            for blk in range(4):
                for cc in range(2):
                    c = 2 * blk + cc
                    b, ps = cx.nextps()
                    for j in range(4):
                        p.mm(lambda e, j=j, c=c, ps=ps: e.matmul(ps[:], lhsT=diag[:, j * 8 + c, :],
                                                              rhs=xb[:, c, t0 - 3 + j:t0 - 3 + j + 512],
                                                              start=(j == 0), stop=(j == 3)),
                             reads=[("xb", c, ti), ("xb", c, ti - 1), ("diag", j * 8 + c)], writes=[("ps", b)])
                    p.act(lambda e, c=c, cc=cc, ps=ps: e.activation(out=xc[:, cc, :], in_=ps[:], func=AF.Identity,
                                                                bias=cb[:, c:c + 1]),
                          reads=[("ps", b), "cb"], writes=[("xc", cc)])
                    p.dve(lambda e, cc=cc: e.tensor_copy(out=xcb[:, cc, :], in_=xc[:, cc, :]),
                          reads=[("xc", cc)], writes=[("xcb", cc)])
                for cc in range(2):
                    n = 2 * blk + cc
                    rb, ib, ab = rt[cc], it[cc], a2[cc]
                    b, ps = cx.nextps()
                    for kk in range(2):
                        p.mm(lambda e, kk=kk, cc=cc, ps=ps, blk=blk: e.matmul(ps[:], lhsT=wr[:, blk * 2 + kk, cc * 128:(cc + 1) * 128],
                                                                  rhs=xcb[:, kk, :], start=(kk == 0), stop=(kk == 1)),
                             reads=[("xcb", 0), ("xcb", 1)] + wrkeys, writes=[("ps", b)])
                    p.act(lambda e, n=n, ps=ps, rb=rb: e.activation(out=rb[:], in_=ps[:], func=AF.Sigmoid, bias=br[:, n:n + 1]),
                          reads=[("ps", b), "br"], writes=[("rt", cc)])
                    b, ps = cx.nextps()
                    for kk in range(2):
                        p.mm(lambda e, kk=kk, cc=cc, ps=ps, blk=blk: e.matmul(ps[:], lhsT=wi[:, blk * 2 + kk, cc * 128:(cc + 1) * 128],
                                                                  rhs=xcb[:, kk, :], start=(kk == 0), stop=(kk == 1)),
                             reads=[("xcb", 0), ("xcb", 1)] + wikeys, writes=[("ps", b)])
                    p.act(lambda e, n=n, ps=ps, ib=ib: e.activation(out=ib[:], in_=ps[:], func=AF.Sigmoid, bias=bi[:, n:n + 1]),
                          reads=[("ps", b), "bi"], writes=[("it", cc)])
                for cc in range(2):
                    n = 2 * blk + cc
                    rb, ib, ab = rt[cc], it[cc], a2[cc]
                    p.act(lambda e, n=n, rb=rb, cc=cc: e.activation(out=at[:, cc, :], in_=rb[:], func=AF.Exp, scale=c8[:, n:n + 1]),
                          reads=[("rt", cc), "c8"], writes=[("at", cc)])
                    p.act(lambda e, n=n, rb=rb, ab=ab: e.activation(out=ab[:], in_=rb[:], func=AF.Exp, scale=c16[:, n:n + 1]),
                          reads=[("rt", cc), "c16"], writes=[("a2", cc)])
                for cc in range(2):
                    n = 2 * blk + cc
                    rb, ib, ab = rt[cc], it[cc], a2[cc]
                    p.act(lambda e, ab=ab, one_t=cx.one_t: e.activation(out=ab[:], in_=ab[:], func=AF.Sqrt, scale=-1.0, bias=one_t[:, 0:1]),
                          reads=[("a2", cc), "one"], writes=[("a2", cc)])
                    p.pool(lambda e, ib=ib, cc=cc: e.tensor_tensor(out=ib[:], in0=ib[:], in1=xc[:, cc, :], op=ALU.mult),
                           reads=[("it", cc), ("xc", cc)], writes=[("it", cc)])
                    p.dve(lambda e, cc=cc, ib=ib, ab=ab: e.tensor_tensor(out=ut[:, cc, :], in0=ib[:], in1=ab[:], op=ALU.mult),
                          reads=[("it", cc), ("a2", cc)], writes=[("ut", cc)])
                    if ti == 0:
                        hinit, ainit = 0.0, 1.0
                        ir = []
                    else:
                        hinit, ainit = hlast[:, n:n + 1], alast[:, n:n + 1]
                        ir = [("hlast", n), ("alast", n)]
                    p.dve(lambda e, n=n, cc=cc, hinit=hinit: e.tensor_tensor_scan(out=hbuf[:, n, :], data0=at[:, cc, :], data1=ut[:, cc, :],
                                                                        initial=hinit, op0=ALU.mult, op1=ALU.add),
                          reads=[("at", cc), ("ut", cc)] + ir, writes=[("hl", 0, n)])
                    p.dve(lambda e, n=n, cc=cc, ainit=ainit: e.tensor_tensor_scan(out=abuf[:, n, :], data0=at[:, cc, :], data1=zeros[:],
                                                                        initial=ainit, op0=ALU.mult, op1=ALU.add),
                          reads=[("at", cc), "zeros"] + ir, writes=[("ac", 0, n)])
                    p.dve(lambda e, n=n: e.tensor_copy(out=hlast[:, n:n + 1], in_=hbuf[:, n, 511:512]),
                          reads=[("hl", 0, n)], writes=[("hlast", n)])
                    p.dve(lambda e, n=n: e.tensor_copy(out=alast[:, n:n + 1], in_=abuf[:, n, 511:512]),
                          reads=[("ac", 0, n)], writes=[("alast", n)])
            p.dma("sync", hlv[:, :, ti * 512:(ti + 1) * 512], hbuf[:], reads=[("hl", 0, n) for n in range(8)])
            return p.dma("sync", acv[:, :, ti * 512:(ti + 1) * 512], abuf[:], reads=[("ac", 0, n) for n in range(8)])

        cx.one_t = sb("one_t", [128, 1])
        p.dve(lambda e, one_t=cx.one_t: e.memset(one_t[:], 1.0), writes=["one"])
        do_tile(-1)
        last = None
        for ti in range(4):
            last = do_tile(ti)
        if cx.fused:
            es = cx.over["ends_src"]
            p.dma("sync", es[:, 0:8], hlast[:], reads=[("hlast", n) for n in range(8)])
            p.dma("sync", es[:, 8:16], alast[:], reads=[("alast", n) for n in range(8)])
        if standalone:
            outs = [i for i, o in enumerate(p.ops) if o["dma"] and o["eng"] == "sync"]
            p.emit(final_wait_ops=outs[-12:])
    return nc


def build_p24(first, out_f32, cx=None):
    standalone = cx is None
    if standalone:
        cx = Ctx()
    fused = cx.fused
    cx.prefix = ("p2_" if first else "p4_") if fused else ""
    nc, p = cx.nc, cx.p
    xT = cx.din("xT", [D, T])
    if first:
        hloc = cx.din("hloc", [D, T])
        acum = cx.din("acum", [D, T])
        gate = cx.din("gate", [D, T], BF16)
        if not fused:
            ends = cx.din("ends", [128, 48])
        else:
            km_d = cx.din("km", [128, 4])
            hm_d = cx.din("hm", [128, 4])
    else:
        if not fused:
            oT = cx.din("oT", [D, T], BF16)
        else:
            sel_d = cx.din("sel", [128, 4])
    w_out = cx.din("w_out", [8, 128, 8 * 128])
    g_ffn = cx.din("g_ffn", [128, 8])
    g_next = cx.din("g_next", [128, 8])
    wg = cx.din("wg", [NF // 2, 128, 8 * 256])
    wu = cx.din("wu", [NF // 2, 128, 8 * 256])
    wd = cx.din("wd", [8, 128, NF * 128])
    x2T = cx.dout("x2T", [D, T])
    hnT = cx.dout("hnT", [D, T], F32 if out_f32 else BF16)

    v3 = lambda a: a.rearrange("(c p) t -> p c t", p=128)
    xTv, x2v, hnv = v3(xT), v3(x2T), v3(hnT)

    with cx.phase("p2_" if first else "p4_"):
        cx.init_psum()
        sb = cx.sb
        xr = sb("xr", [128, 8, 1024])
        y = [sb("y%d" % i, [128, 8, 512], BF16) for i in range(2)]
        sq = sb("sq", [128, 8, 512], BF16)
        rs = sb("rs", [128, 512])
        rstd = sb("rstd", [128, 512])
        h2 = sb("h2", [128, 8, 1024], BF16)
        actb = sb("actb", [128, NF, 1024], BF16)
        sg = [sb("sg%d" % i, [128, 512]) for i in range(2)]
        wo = sb("wo", [128, 8, 8 * 128], BF16)
        wgs = [sb("wgs%d" % i, [128, 8 * 256], BF16) for i in range(3)]
        wus = [sb("wus%d" % i, [128, 8 * 256], BF16) for i in range(3)]
        wds = [sb("wds%d" % i, [128, NF * 128], BF16) for i in range(2)]
        hn = [sb("hn%d" % i, [128, 8, 512], F32 if out_f32 else BF16) for i in range(1 if out_f32 else 2)]
        ones = sb("ones", [128, 128], BF16)
        cx.eps_t = sb("eps", [128, 1])
        gf = sb("gf", [128, 8])
        gn = sb("gn", [128, 8])
        p.dma("sync", gf[:], g_ffn, writes=["gf"])
        p.dma("sync", gn[:], g_next, writes=["gn"])
        p.dve(lambda e: e.memset(ones[:], 1.0), writes=["ones"])
        p.dve(lambda e, eps_t=cx.eps_t: e.memset(eps_t[:], EPS), writes=["eps"])
        for n in range(8):
            p.dma("gpsimd", wo[:, n, :], w_out[n], writes=[("wo", n)])
        if first:
            en = sb("en", [128, 48])
            carry = sb("carry", [128, 8])
            ctmp = sb("ctmp", [128, 8])
            hlc = [sb("hlc%d" % i, [128, 512]) for i in range(2)]
            acc = [sb("acc%d" % i, [128, 512]) for i in range(2)]
            gtc = [sb("gtc%d" % i, [128, 512], BF16) for i in range(2)]
            hlv, acv, gtv = v3(hloc), v3(acum), v3(gate)
            p.dve(lambda e: e.memset(carry[:], 0.0), writes=["carry"])
            if fused:
                en_all = sb("en_all", [128, 4, 16])
                km = sb("km", [128, 4])
                hm = sb("hm", [128, 4])
                ap_t = sb("ap_t", [128, 8])
                hp_t = sb("hp_t", [128, 8])
                p.dma("sync", en_all[:], cx.over["ends_all"].rearrange("(r p) f -> p r f", p=128), reads=["d_ends_all"], writes=["en_all"])
                p.dma("sync", km[:], km_d, writes=["km"])
                p.dma("sync", hm[:], hm_d, writes=["hm"])
                for r in range(4):
                    p.dve(lambda e, r=r: e.tensor_scalar(out=ap_t[:], in0=en_all[:, r, 8:16], scalar1=km[:, r:r + 1],
                                                       scalar2=hm[:, r:r + 1], op0=ALU.mult, op1=ALU.add),
                          reads=["en_all", "km", "hm"], writes=["ap_t"])
                    p.dve(lambda e, r=r: e.tensor_scalar(out=hp_t[:], in0=en_all[:, r, 0:8], scalar1=km[:, r:r + 1],
                                                       scalar2=None, op0=ALU.mult),
                          reads=["en_all", "km"], writes=["hp_t"])
                    p.dve(lambda e: e.tensor_tensor(out=ctmp[:], in0=ap_t[:], in1=carry[:], op=ALU.mult),
                          reads=["ap_t", "carry"], writes=["ctmp"])
                    p.dve(lambda e: e.tensor_tensor(out=carry[:], in0=ctmp[:], in1=hp_t[:], op=ALU.add),
                          reads=["ctmp", "hp_t"], writes=["carry"])
            else:
                p.dma("sync", en[:], ends, writes=["en"])
            for j in range(0 if fused else 3):
                p.dve(lambda e, j=j: e.tensor_tensor(out=ctmp[:], in0=en[:, j * 16 + 8:j * 16 + 16], in1=carry[:], op=ALU.mult),
                      reads=["en", "carry"], writes=["ctmp"])
                p.dve(lambda e, j=j: e.tensor_tensor(out=carry[:], in0=ctmp[:], in1=en[:, j * 16:j * 16 + 8], op=ALU.add),
                      reads=["en", "ctmp"], writes=["carry"])
        elif not fused:
            oTv = v3(oT)
        else:
            oallv = [a.rearrange("(c p) t -> p c t", p=128) for a in cx.over["o_all"]]
            sel = sb("sel", [128, 4])
            cand = [sb("cand%d" % i, [128, 8, 512], BF16) for i in range(2)]
            p.dma("sync", sel[:], sel_d, writes=["sel"])

        def norm_tile(x3, xkey, g_sb, gkey, h3, hkey):
            N = 512
            p.act(lambda e: e.activation(out=sq[:], in_=x3, func=AF.Square), reads=xkey, writes=["sq"])
            b, ps = cx.nextps()
            for c in range(8):
                p.mm(lambda e, c=c: e.matmul(ps[:], lhsT=ones[:], rhs=sq[:, c, :], start=(c == 0), stop=(c == 7)),
                     reads=["sq", "ones"], writes=[("ps", b)])
            p.act(lambda e, eps_t=cx.eps_t: e.activation(out=rs[:], in_=ps[:], func=AF.Sqrt, scale=1.0 / D, bias=eps_t[:, 0:1]),
                  reads=[("ps", b), "eps"], writes=["rs"])
            p.dve(lambda e: e.reciprocal(out=rstd[:], in_=rs[:]), reads=["rs"], writes=["rstd"])
            for c in range(8):
                p.dve(lambda e, c=c: e.scalar_tensor_tensor(out=h3[:, c, :], in0=x3[:, c, :], scalar=g_sb[:, c:c + 1],
                                                          in1=rstd[:], op0=ALU.mult, op1=ALU.mult),
                      reads=list(xkey) + ["rstd", gkey], writes=[(hkey, c)])

        wokeys = [("wo", n) for n in range(8)]
        cidx = [0]

        def load_gu(f2, slot):
            p.dma("gpsimd", wgs[slot][:], wg[f2], writes=[("wgs", slot)])
            p.dma("gpsimd", wus[slot][:], wu[f2], writes=[("wus", slot)])

        def load_d(n, slot):
            p.dma("gpsimd", wds[slot][:], wd[n], writes=[("wds", slot)])

        outs = []
        for half in range(2):
            hx = half * 1024
            xkeys = lambda tt: [("xr", tt, c) for c in range(8)]
            for tt in range(2):
                p.dma("sync", xr[:, :, tt * 512:(tt + 1) * 512], xTv[:, :, hx + tt * 512:hx + (tt + 1) * 512],
                      writes=xkeys(tt))
            load_gu(0, 0)
            load_gu(1, 1)
            for tt in range(2):
                t0 = hx + tt * 512
                yb = y[tt]
                ykeys = [("y", tt, c) for c in range(8)]
                if first:
                    for c in range(8):
                        s = cidx[0] % 2
                        cidx[0] += 1
                        p.dma("sync", hlc[s][:], hlv[:, c, t0:t0 + 512], writes=[("hlc", s)])
                        p.dma("scalar", acc[s][:], acv[:, c, t0:t0 + 512], writes=[("acc", s)])
                        p.dma("sync", gtc[s][:], gtv[:, c, t0:t0 + 512], writes=[("gtc", s)])
                        p.dve(lambda e, s=s, c=c: e.scalar_tensor_tensor(out=hlc[s][:], in0=acc[s][:], scalar=carry[:, c:c + 1],
                                                                      in1=hlc[s][:], op0=ALU.mult, op1=ALU.add),
                              reads=[("hlc", s), ("acc", s), "carry"], writes=[("hlc", s)])
                        p.pool(lambda e, s=s, c=c, yb=yb: e.tensor_tensor(out=yb[:, c, :], in0=hlc[s][:], in1=gtc[s][:], op=ALU.mult),
                               reads=[("hlc", s), ("gtc", s)], writes=[("y", tt, c)])
                elif not fused:
                    p.dma("sync", yb[:], oTv[:, :, t0:t0 + 512], writes=ykeys)
                else:
                    for j in range(4):
                        cb = cand[j % 2]
                        ck = ("cand", j % 2)
                        p.dma("sync", cb[:], oallv[j][:, :, t0:t0 + 512], reads=[("d_oall", j)], writes=[ck])
                        if j == 0:
                            p.dve(lambda e, cb=cb, yb=yb: e.tensor_scalar(out=yb[:], in0=cb[:], scalar1=sel[:, 0:1], scalar2=None,
                                                                       op0=ALU.mult),
                                  reads=[ck, "sel"], writes=ykeys)
                        else:
                            p.dve(lambda e, cb=cb, yb=yb, j=j: e.scalar_tensor_tensor(out=yb[:], in0=cb[:], scalar=sel[:, j:j + 1],
                                                                                   in1=yb[:], op0=ALU.mult, op1=ALU.add),
                                  reads=[ck, "sel"] + ykeys, writes=ykeys)
                for n in range(8):
                    b, ps = cx.nextps()
                    for k in range(8):
                        p.mm(lambda e, k=k, n=n, ps=ps, yb=yb: e.matmul(ps[:], lhsT=wo[:, n, k * 128:(k + 1) * 128], rhs=yb[:, k, :],
                                                                     start=(k == 0), stop=(k == 7)),
                             reads=ykeys + [("wo", n)], writes=[("ps", b)])
                    p.dve(lambda e, n=n, ps=ps, tt=tt: e.tensor_tensor(out=xr[:, n, tt * 512:(tt + 1) * 512],
                                                                     in0=xr[:, n, tt * 512:(tt + 1) * 512], in1=ps[:], op=ALU.add),
                          reads=[("ps", b), ("xr", tt, n)], writes=[("xr", tt, n)])
                norm_tile(xr[:, :, tt * 512:(tt + 1) * 512], xkeys(tt), gf, "gf", h2[:, :, tt * 512:(tt + 1) * 512], ("h2", tt))
            for f2 in range(NF // 2):
                slot = f2 % 3
                if f2 + 2 < NF // 2:
                    load_gu(f2 + 2, (f2 + 2) % 3)
                if f2 == NF // 2 - 2:
                    load_d(0, 0)
                if f2 == NF // 2 - 1:
                    load_d(1, 1)
                for ff in range(2):
                    f = 2 * f2 + ff
                    for tt in range(2):
                        h2k = [(("h2", tt), c) for c in range(8)]
                        bg, psg = cx.nextps()
                        for k in range(8):
                            p.mm(lambda e, k=k, ff=ff, psg=psg, slot=slot, tt=tt: e.matmul(
                                psg[:], lhsT=wgs[slot][:, k * 256 + ff * 128:k * 256 + (ff + 1) * 128],
                                rhs=h2[:, k, tt * 512:(tt + 1) * 512], start=(k == 0), stop=(k == 7)),
                                reads=h2k + [("wgs", slot)], writes=[("ps", bg)])
                        bu, psu = cx.nextps()
                        for k in range(8):
                            p.mm(lambda e, k=k, ff=ff, psu=psu, slot=slot, tt=tt: e.matmul(
                                psu[:], lhsT=wus[slot][:, k * 256 + ff * 128:k * 256 + (ff + 1) * 128],
                                rhs=h2[:, k, tt * 512:(tt + 1) * 512], start=(k == 0), stop=(k == 7)),
                                reads=h2k + [("wus", slot)], writes=[("ps", bu)])
                        s = (f * 2 + tt) % 2
                        p.act(lambda e, s=s, psg=psg: e.activation(out=sg[s][:], in_=psg[:], func=AF.Silu),
                              reads=[("ps", bg)], writes=[("sg", s)])
                        p.dve(lambda e, s=s, psu=psu, f=f, tt=tt: e.tensor_tensor(out=actb[:, f, tt * 512:(tt + 1) * 512],
                                                                               in0=sg[s][:], in1=psu[:], op=ALU.mult),
                              reads=[("sg", s), ("ps", bu)], writes=[("actb", f, tt)])
            for n in range(8):
                slot = n % 2
                for tt in range(2):
                    b, ps = cx.nextps()
                    for f in range(NF):
                        p.mm(lambda e, f=f, ps=ps, slot=slot, tt=tt: e.matmul(
                            ps[:], lhsT=wds[slot][:, f * 128:(f + 1) * 128], rhs=actb[:, f, tt * 512:(tt + 1) * 512],
                            start=(f == 0), stop=(f == NF - 1)),
                            reads=[("actb", f, tt), ("wds", slot)], writes=[("ps", b)])
                    p.dve(lambda e, n=n, ps=ps, tt=tt: e.tensor_tensor(out=xr[:, n, tt * 512:(tt + 1) * 512],
                                                                     in0=xr[:, n, tt * 512:(tt + 1) * 512], in1=ps[:], op=ALU.add),
                          reads=[("ps", b), ("xr", tt, n)], writes=[("xr", tt, n)])
                if n + 2 < 8:
                    load_d(n + 2, slot)
            for tt in range(2):
                t0 = hx + tt * 512
                if first or not fused:
                    outs.append(p.dma("sync", x2v[:, :, t0:t0 + 512], xr[:, :, tt * 512:(tt + 1) * 512], reads=xkeys(tt)))
                hi_ = tt % len(hn)
                norm_tile(xr[:, :, tt * 512:(tt + 1) * 512], xkeys(tt), gn, "gn", hn[hi_], ("hn", hi_))
                hdst = hnv[:, :, t0:t0 + 512] if not (fused and first) else cx.over["hn_tiles"][half * 2 + tt].rearrange("(c p) t -> p c t", p=128)
                gti = half * 2 + tt
                outs.append(p.dma("sync", hdst, hn[hi_][:], reads=[(("hn", hi_), c) for c in range(8)], writes=[("d_hn", gti)]))
                if fused and first:
                    p.cc(lambda e, a=cx.over["hn_tiles"][gti], b=cx.over["hn_all"][gti]: e.collective_compute(
                        "AllGather", ALU.bypass, replica_groups=[[0, 1, 2, 3], [4, 5, 6, 7]], ins=[a], outs=[b]),
                         reads=[("d_hn", gti)], writes=[("d_hnall", gti)])
        cx.last_outs = outs
        if standalone:
            p.emit(final_wait_ops=outs[-8:])
    return nc


def col8(v):
    return np.ascontiguousarray(np.asarray(v, np.float32).reshape(8, 128).T)


def slab_kn(w, ncol):
    K = w.shape[0] // 128
    ns = w.shape[1] // ncol
    a = np.asarray(w, np.float32).reshape(K, 128, ns, ncol).transpose(2, 1, 0, 3)
    return np.ascontiguousarray(a.reshape(ns, 128, K * ncol))


_NC_CACHE = {}


def get_nc(name, fn):
    if name not in _NC_CACHE:
        _NC_CACHE[name] = fn()
    return _NC_CACHE[name]


def run(nc, in_maps):
    res = run_bass_kernel_spmd(nc, in_maps, core_ids=list(range(8)))
    return res.results


def core_tokens(r):
    return r // 4, (r % 4) * T


def stage_p1(inp):
    x = np.asarray(inp["x"], np.float32)
    maps = []
    common = dict(
        gvec=col8(inp["norm_mix_g"][0]),
        w_in=np.ascontiguousarray(inp["a_w_in"][0], dtype=np.float32),
        convw=np.ascontiguousarray(np.concatenate([col8(inp["a_conv_w"][0][j]) for j in range(4)], axis=1)),
        convb=col8(inp["a_conv_b"][0]),
        w_r=np.ascontiguousarray(np.asarray(inp["a_w_r"][0], np.float32).reshape(D, 256)),
        w_i=np.ascontiguousarray(np.asarray(inp["a_w_i"][0], np.float32).reshape(D, 256)),
        b_r=col8(inp["a_b_r"][0]), b_i=col8(inp["a_b_i"][0]), lam=col8(inp["a_lambda"][0]),
        ident=np.eye(128, dtype=np.float32),
    )
    for r in range(8):
        b, t0 = core_tokens(r)
        xs = np.zeros((TH + T, D), np.float32)
        if t0 == 0:
            xs[TH:] = x[b, 0:T]
        else:
            xs[:] = x[b, t0 - TH:t0 + T]
        m = dict(common)
        m["xT"] = np.ascontiguousarray(xs.T)
        maps.append(m)
    return maps


def ffn_weights(inp, layer):
    return dict(
        g_ffn=col8(inp["norm_ffn_g"][layer]),
        wg=slab_kn(inp["ffn_w_gate"][layer], 256),
        wu=slab_kn(inp["ffn_w_up"][layer], 256),
        wd=slab_kn(inp["ffn_w_down"][layer], 128),
    )


def stage_p2(inp, r1):
    x = np.asarray(inp["x"], np.float32)
    common = ffn_weights(inp, 0)
    common["w_out"] = slab_kn(inp["a_w_out"][0], 128)
    common["g_next"] = col8(inp["norm_mix_g"][1])
    maps = []
    for r in range(8):
        b, t0 = core_tokens(r)
        q = r % 4
        m = dict(common)
        m["xT"] = np.ascontiguousarray(x[b, t0:t0 + T].T)
        m["hloc"], m["acum"], m["gate"] = r1[r]["hloc"], r1[r]["acum"], r1[r]["gate"]
        ends = np.zeros((128, 3, 2, 8), np.float32)
        for j in range(3):
            src = q - 3 + j
            if src >= 0:
                rr = b * 4 + src
                ends[:, j, 0, :] = col8(r1[rr]["hloc"][:, T - 1])
                ends[:, j, 1, :] = col8(r1[rr]["acum"][:, T - 1])
        m["ends"] = np.ascontiguousarray(ends.reshape(128, 48))
        maps.append(m)
    return maps


def build_p3(cx=None):
    standalone = cx is None
    if standalone:
        cx = Ctx()
    fused = cx.fused
    cx.prefix = "p3_" if fused else ""
    nc, p = cx.nc, cx.p
    hn = cx.din("hn", [D, S], BF16)
    wq = cx.din("wq", [128, 8 * 256])
    wk = cx.din("wk", [128, 8 * 256])
    wv = cx.din("wv", [128, 8 * 256])
    umat = cx.din("umat", [128, 128])
    masks = cx.din("masks", [128, 4 * 512])
    oT = cx.dout("oT", [256, S], BF16)
    NQ = S // 512
    if not fused:
        hnv = hn.rearrange("(c p) t -> p c t", p=128)
        hn_tile = lambda ti: hnv[:, :, ti * 512:(ti + 1) * 512]
        o_dst = lambda head, Qi: oT[head * 64:(head + 1) * 64, Qi * 512:(Qi + 1) * 512]
    else:
        hn_tile = lambda ti: hn[ti % 4][(ti // 4) * D:(ti // 4 + 1) * D, :].rearrange("(c p) t -> p c t", p=128)
        o_dst = lambda head, Qi: oT[Qi // 4][head * 64:(head + 1) * 64, (Qi % 4) * 512:(Qi % 4 + 1) * 512]

    with cx.phase("p3_"):
        cx.init_psum()
        sb = cx.sb
        hb = [sb("hb%d" % i, [128, 8, 512], BF16) for i in range(2)]
        wqs = sb("wqs", [128, 8 * 256], BF16)
        wks = sb("wks", [128, 8 * 256], BF16)
        wvs = sb("wvs", [128, 8 * 256], BF16)
        qs = sb("qs", [128, 2, S], BF16)
        kt = sb("kt", [128, 2, S], BF16)
        vv = sb("vv", [128, S // 128, 256], BF16)
        um = sb("um", [128, 128])
        onesr = sb("onesr", [128, 128])
        mk = sb("mk", [128, 4, 512])
        ob = [sb("ob%d" % i, [64, 512], BF16) for i in range(2)]

        p.dma("sync", mk[:], masks.rearrange("p (m f) -> p m f", m=4), writes=["mk"])
        umf = sb("umf", [128, 128])
        onesf = sb("onesf", [128, 128])
        p.dma("sync", umf[:], umat, writes=["umf"])
        p.dve(lambda e: e.memset(onesf[:], -1.0), writes=["onesf"])
        p.dve(lambda e: e.tensor_scalar(out=um[:].bitcast(F32R), in0=umf[:], scalar1=-1.0, scalar2=None, op0=ALU.mult), reads=["umf"], writes=["um"])
        p.dve(lambda e: e.tensor_copy(out=onesr[:].bitcast(F32R), in_=onesf[:]), reads=["onesf"], writes=["onesr"])
        p.dma("gpsimd", wqs[:], wq, writes=["wqs"])
        p.dma("gpsimd", wks[:], wk, writes=["wks"])
        p.dma("gpsimd", wvs[:], wv, writes=["wvs"])

        import os
        for oi_, ti in enumerate(sorted(range(int(os.environ.get("P3_PT", str(NQ)))), key=lambda a: (a % 4, a // 4))):
            hbuf = hb[oi_ % 2]
            hkey = ("hb", oi_ % 2)
            p.dma("sync", hbuf[:], hn_tile(ti), reads=[("d_hnall", ti % 4)], writes=[hkey])
            tsl = slice(ti * 512, (ti + 1) * 512)
            SKIP = os.environ.get("P3_SKIP", "")
            for pair in range(2):
                if "q" in SKIP:
                    break
                b, ps = cx.nextps()
                for k in range(8):
                    p.mm(lambda e, k=k, pair=pair, ps=ps, hbuf=hbuf: e.matmul(
                        ps[:], lhsT=wqs[:, k * 256 + pair * 128:k * 256 + (pair + 1) * 128], rhs=hbuf[:, k, :],
                        start=(k == 0), stop=(k == 7)), reads=[hkey, "wqs"], writes=[("ps", b)])
                p.act(lambda e, pair=pair, ps=ps, tsl=tsl: e.mul(out=qs[:, pair, tsl], in_=ps[:], mul=0.125),
                      reads=[("ps", b)], writes=[("qs", pair, ti)])
                b, ps = cx.nextps()
                for k in range(8):
                    p.mm(lambda e, k=k, pair=pair, ps=ps, hbuf=hbuf: e.matmul(
                        ps[:], lhsT=wks[:, k * 256 + pair * 128:k * 256 + (pair + 1) * 128], rhs=hbuf[:, k, :],
                        start=(k == 0), stop=(k == 7)), reads=[hkey, "wks"], writes=[("ps", b)])
                p.act(lambda e, pair=pair, ps=ps, tsl=tsl: e.copy(out=kt[:, pair, tsl], in_=ps[:]),
                      reads=[("ps", b)], writes=[("kt", pair, ti)])
            for tb in range(4):
                if "v" in SKIP:
                    break
                b, ps = cx.nextps()
                for k in range(8):
                    p.mm(lambda e, k=k, tb=tb, ps=ps, hbuf=hbuf: e.matmul(
                        ps[:, 0:256], lhsT=hbuf[:, k, tb * 128:(tb + 1) * 128], rhs=wvs[:, k * 256:(k + 1) * 256],
                        start=(k == 0), stop=(k == 7)), reads=[hkey, "wvs"], writes=[("ps", b)])
                p.dve(lambda e, tb=tb, ps=ps, ti=ti: e.tensor_copy(out=vv[:, ti * 4 + tb, :], in_=ps[:, 0:256]),
                      reads=[("ps", b)], writes=[("vv", ti)])

        outs = []
        um_r = um[:].bitcast(F32R)
        ones_r = onesr[:].bitcast(F32R)
        import os
        DBG_H = int(os.environ.get("P3_HEADS", "4"))
        DBG_Q = int(os.environ.get("P3_NQ", str(NQ)))
        LB = 1
        LC = 2
        NSP = LB + 2
        NWT = LC - LB + 2
        NRS = LB + 2
        etmp = [sb("etmp%d" % i, [128, 2, 512]) for i in range(NSP)]
        pw = [sb("pw%d" % i, [128, 2, 512]) for i in range(2)]
        spt = [sb("sptp%d" % i, [128, 2, 512]) for i in range(NSP)]
        wt = [sb("wtp%d" % i, [128, 2, 512], BF16) for i in range(NWT)]
        rsum = [sb("rsump%d" % i, [128, 2, 512]) for i in range(NRS)]
        one_t = sb("one_t", [128, 1])
        p.dve(lambda e: e.memset(one_t[:], 1.0), writes=["one"])
        psb = cx.psbig
        pairs = []
        gi = 0
        for Qi in range(DBG_Q):
            for pair in range(2):
                for hh in range(2):
                    head = pair * 2 + hh
                    if head >= DBG_H:
                        continue
                    nkb = 4 * (Qi + 1)
                    for kb in range(nkb - 1, -1, -2):
                        pairs.append(dict(pair=pair, hh=hh, head=head, Qi=Qi, kb=kb, first=(kb == nkb - 1), last=(kb == 1),
                                          diag=(kb - 4 * Qi >= 0), dpi=(0 if kb - 4 * Qi == 3 else 1), g=gi))
                    gi += 1
        n = len(pairs)
        for j, t in enumerate(pairs):
            t["zb"] = 2 * (j % 3)
            t["db"] = t["zb"]
            t["sp"] = j % NSP
            t["wt"] = j % NWT
            t["rs"] = j % NRS
            t["ob"] = 6 + t["g"] % 2
            t["rsp"] = None if t["first"] else pairs[j - 1]["rs"]

        def stage_a(t):
            pair, prt = t["pair"], slice(64 * t["hh"], 64 * t["hh"] + 64)
            qsl = slice(t["Qi"] * 512, (t["Qi"] + 1) * 512)
            zb, sp_t, e_t = t["zb"], spt[t["sp"]], etmp[t["sp"]]
            zkeys = [("ps", zb), ("ps", zb + 1)]
            for u in range(2):
                kb = t["kb"] - u
                ksl = slice(kb * 128, (kb + 1) * 128)
                p.mm(lambda e, u=u, ksl=ksl: e.matmul(psb[:, zb + u, :], lhsT=kt[prt, pair, ksl], rhs=qs[prt, pair, qsl],
                                                      start=True, stop=False),
                     reads=[("qs", pair, t["Qi"]), ("kt", pair, kb // 4)], writes=[("ps", zb + u)])
            p.act(lambda e: e.activation(out=e_t[:], in_=psb[:, zb:zb + 2, :], func=AF.Exp),
                  reads=zkeys, writes=[("etmp", t["sp"])])
            p.act(lambda e: e.activation(out=sp_t[:].bitcast(F32R), in_=e_t[:], func=AF.Ln, bias=1.0),
                  reads=[("etmp", t["sp"])], writes=[("spt", t["sp"])])
            if t["diag"]:
                d0 = 2 * t["dpi"]
                p.dve(lambda e: e.tensor_tensor(out=sp_t[:].bitcast(F32R), in0=sp_t[:], in1=mk[:, d0:d0 + 2, :], op=ALU.mult),
                      reads=[("spt", t["sp"]), "mk"], writes=[("spt", t["sp"])])
            rn = rsum[t["rs"]]
            if t["first"]:
                p.dve(lambda e: e.tensor_copy(out=rn[:, 0, :].bitcast(F32R), in_=sp_t[:, 0, :]),
                      reads=[("spt", t["sp"])], writes=[("rsum", t["rs"], 0)])
            else:
                rp = rsum[t["rsp"]]
                p.dve(lambda e: e.tensor_tensor(out=rn[:, 0, :].bitcast(F32R), in0=rp[:, 1, :], in1=sp_t[:, 0, :], op=ALU.add),
                      reads=[("spt", t["sp"]), ("rsum", t["rsp"], 1)], writes=[("rsum", t["rs"], 0)])
            if not t["last"]:
                p.dve(lambda e: e.tensor_tensor(out=rn[:, 1, :].bitcast(F32R), in0=rn[:, 0, :], in1=sp_t[:, 1, :], op=ALU.add),
                      reads=[("spt", t["sp"]), ("rsum", t["rs"], 0)], writes=[("rsum", t["rs"], 1)])

        def stage_b(t):
            db, sp_t, w_t, e_t = t["db"], spt[t["sp"]], wt[t["wt"]], etmp[t["sp"]]
            rn = rsum[t["rs"]]
            p_t = pw[t["wt"] % 2]
            pk = ("pw", t["wt"] % 2)
            for u in range(2):
                first_tile = t["first"] and u == 0
                p.mm(lambda e, u=u, first_tile=first_tile: e.matmul(psb[:, db + u, :], lhsT=um_r, rhs=sp_t[:, u, :].bitcast(F32R),
                                                                    start=False, stop=first_tile),
                     reads=[("spt", t["sp"]), "um"], writes=[("ps", db + u)])
            for u in range(2):
                first_tile = t["first"] and u == 0
                if not first_tile:
                    if u == 0:
                        rsrc, rkey = rsum[t["rsp"]][:, 1, :], ("rsum", t["rsp"], 1)
                    else:
                        rsrc, rkey = rn[:, 0, :], ("rsum", t["rs"], 0)
                    p.mm(lambda e, u=u, rsrc=rsrc: e.matmul(psb[:, db + u, :], lhsT=ones_r, rhs=rsrc.bitcast(F32R),
                                                            start=False, stop=True),
                         reads=[rkey, "onesr"], writes=[("ps", db + u)])
            p.act(lambda e: e.activation(out=w_t[:], in_=psb[:, db:db + 2, :], func=AF.Exp),
                  reads=[("ps", db), ("ps", db + 1)], writes=[("wt", t["wt"], 0), ("wt", t["wt"], 1)])
            if t["diag"]:
                d0 = 2 * t["dpi"]
                p.dve(lambda e: e.tensor_tensor(out=w_t[:], in0=w_t[:], in1=mk[:, d0:d0 + 2, :], op=ALU.mult),
                      reads=[("wt", t["wt"], 0), ("wt", t["wt"], 1), "mk"], writes=[("wt", t["wt"], 0), ("wt", t["wt"], 1)])

        def stage_c(t):
            ops_, w_t, head = cx.ps[t["ob"]], wt[t["wt"]], t["head"]
            for u in range(2):
                kb = t["kb"] - u
                p.mm(lambda e, u=u, kb=kb: e.matmul(ops_[0:64, :], lhsT=vv[:, kb, head * 64:(head + 1) * 64], rhs=w_t[:, u, :],
                                                    start=(t["first"] and u == 0), stop=(t["last"] and u == 1)),
                     reads=[("wt", t["wt"], u), ("vv", kb // 4)], writes=[("ps", t["ob"])])
            if t["last"]:
                obuf = ob[t["g"] % 2]
                p.act(lambda e: e.copy(out=obuf[:], in_=ops_[0:64, :]),
                      reads=[("ps", t["ob"])], writes=[("ob", t["g"] % 2)])
                qk = t["Qi"] // 4
                outs.append(p.dma("sync", o_dst(head, t["Qi"]), obuf[:], reads=[("ob", t["g"] % 2)], writes=[("d_o", qk, t["Qi"] % 4, head)]))
                if fused and t["Qi"] % 4 == 3 and head == 3:
                    p.cc(lambda e, a=cx.over["oT"][qk], b=cx.over["o_all"][qk]: e.collective_compute(
                        "AllGather", ALU.bypass, replica_groups=[[0, 1, 2, 3], [4, 5, 6, 7]], ins=[a], outs=[b]),
                         reads=[("d_o", qk, a, b) for a in range(4) for b in range(4)], writes=[("d_oall", qk)])

        for i in range(n + LC):
            if i < n:
                stage_a(pairs[i])
            if 0 <= i - LB < n:
                stage_b(pairs[i - LB])
            if 0 <= i - LC < n:
                stage_c(pairs[i - LC])
        cx.last_outs = outs
        if standalone:
            p.emit(final_wait_ops=outs[-8:])
    return nc


def attn_consts():
    pp = np.arange(128)[:, None]
    um = (pp >= np.arange(128)[None, :]).astype(np.float32)
    f = np.arange(512)[None, :]
    mk = np.concatenate([(f > (128 * m + pp)).astype(np.float32) for m in (3, 2, 1, 0)], axis=1)
    return um, np.ascontiguousarray(mk)


def stage_p3(inp, r2):
    um, mk = attn_consts()
    wqkv = np.asarray(inp["b_w_qkv"][0], np.float32)
    maps = []
    for r in range(8):
        b, hq = r // 4, r % 4
        hn = np.concatenate([r2[b * 4 + q]["hnT"] for q in range(4)], axis=1)
        cols = slice(hq * 256, (hq + 1) * 256)
        m = dict(
            hn=np.ascontiguousarray(hn),
            wq=slab_kn(wqkv[:, 0:D][:, cols], 256)[0],
            wk=slab_kn(wqkv[:, D:2 * D][:, cols], 256)[0],
            wv=slab_kn(wqkv[:, 2 * D:3 * D][:, cols], 256)[0],
            umat=um, masks=mk,
        )
        maps.append(m)
    return maps


def stage_p4(inp, r2, r3):
    common = ffn_weights(inp, 1)
    common["w_out"] = slab_kn(inp["b_w_out"][0], 128)
    common["g_next"] = col8(inp["final_g"])
    maps = []
    for r in range(8):
        b, q = r // 4, r % 4
        m = dict(common)
        m["xT"] = r2[r]["x2T"]
        m["oT"] = np.ascontiguousarray(
            np.concatenate([r3[b * 4 + hq]["oT"][:, q * T:(q + 1) * T] for hq in range(4)], axis=0))
        maps.append(m)
    return maps


def build_fused():
    cx = Ctx()
    cx.fused = True
    nc, p = cx.nc, cx.p
    xT = nc.dram_tensor("p1_xT", [D, TH + T], F32, kind="ExternalInput").ap()
    out = nc.dram_tensor("out", [D, T], F32, kind="ExternalOutput").ap()
    hloc = cx.dint("i_hloc", [D, T])
    acum = cx.dint("i_acum", [D, T])
    gate = cx.dint("i_gate", [D, T], BF16)
    ends_src = cx.dint("i_ends_src", [128, 16])
    ends_all = cx.dint("i_ends_all", [4 * 128, 16])
    x2 = cx.dint("i_x2", [D, T])
    hn_src = [cx.dint("i_hn_src%d" % k, [D, 512], BF16) for k in range(4)]
    hn_all = [cx.dint("i_hn_all%d" % k, [4 * D, 512], BF16) for k in range(4)]
    o_src = [cx.dint("i_o_src%d" % k, [256, T], BF16) for k in range(4)]
    o_all = [cx.dint("i_o_all%d" % k, [4 * 256, T], BF16) for k in range(4)]
    grp4 = [[0, 1, 2, 3], [4, 5, 6, 7]]
    grp8 = [list(range(8))]

    cx.over = dict(xT=xT, hloc=hloc, acum=acum, gate=gate, ends_src=ends_src)
    build_p1(cx)
    p.barrier()
    import os
    STOP = int(os.environ.get("FUSED_STOP", "99"))
    if STOP <= 0:
        p.emit()
        return nc
    p.cc(lambda e: e.collective_compute("AllGather", ALU.bypass, replica_groups=grp4, ins=[ends_src], outs=[ends_all]),
         writes=["d_ends_all"])
    pass
    if STOP <= 1:
        p.emit()
        return nc
    cx.over = dict(xT=xT[:, TH:TH + T], hloc=hloc, acum=acum, gate=gate, ends_all=ends_all, x2T=x2, hnT=x2, hn_tiles=hn_src, hn_all=hn_all)
    build_p24(True, False, cx)
    p.barrier(skip_cc=True)
    if STOP <= 2:
        p.emit()
        return nc
    pass
    if STOP <= 3:
        p.emit()
        return nc
    cx.over = dict(hn=hn_all, oT=o_src, o_all=o_all)
    build_p3(cx)
    p.barrier(skip_cc=True)
    if STOP <= 4:
        p.emit()
        return nc
    pass
    if STOP <= 5:
        p.emit()
        return nc
    cx.over = dict(xT=x2, x2T=x2, hnT=out, o_all=o_all)
    build_p24(False, True, cx)
    p.emit(final_wait_ops=cx.last_outs[-8:])
    return nc


def stage_fused(inp):
    m1 = stage_p1(inp)
    um, mk = attn_consts()
    wqkv = np.asarray(inp["b_w_qkv"][0], np.float32)
    f0 = ffn_weights(inp, 0)
    f0["w_out"] = slab_kn(inp["a_w_out"][0], 128)
    f0["g_next"] = col8(inp["norm_mix_g"][1])
    f1 = ffn_weights(inp, 1)
    f1["w_out"] = slab_kn(inp["b_w_out"][0], 128)
    f1["g_next"] = col8(inp["final_g"])
    maps = []
    for r in range(8):
        b, q = r // 4, r % 4
        m = {}
        for k, v in m1[r].items():
            m["p1_" + k] = v
        for k, v in f0.items():
            m["p2_" + k] = v
        for k, v in f1.items():
            m["p4_" + k] = v
        keep = np.zeros((128, 4), np.float32)
        for rr in range(q):
            keep[:, rr] = 1.0
        m["p2_km"] = keep
        m["p2_hm"] = 1.0 - keep
        sel = np.zeros((128, 4), np.float32)
        sel[:, q] = 1.0
        m["p4_sel"] = sel
        cols = slice(q * 256, (q + 1) * 256)
        m["p3_wq"] = slab_kn(wqkv[:, 0:D][:, cols], 256)[0]
        m["p3_wk"] = slab_kn(wqkv[:, D:2 * D][:, cols], 256)[0]
        m["p3_wv"] = slab_kn(wqkv[:, 2 * D:3 * D][:, cols], 256)[0]
        m["p3_umat"] = um
        m["p3_masks"] = mk
        maps.append(m)
    return maps


def kernel(**inp):
    inp = {k: np.asarray(v) for k, v in inp.items()}
    res = run(get_nc("fused", build_fused), stage_fused(inp))
    out = np.stack([np.concatenate([res[b * 4 + q]["out"].T for q in range(4)], axis=0) for b in range(2)])
    return np.ascontiguousarray(out.astype(np.float32))
```

```python
import contextlib
import numpy as np
import ml_dtypes
import concourse.bass as bass
import concourse.mybir as mybir
from concourse.bass_utils import run_bass_kernel_spmd

F32 = mybir.dt.float32
F32R = mybir.dt.float32r
BF16 = mybir.dt.bfloat16
AF = mybir.ActivationFunctionType
ALU = mybir.AluOpType
NPBF = ml_dtypes.bfloat16

D = 1024
DFF = 2816
NF = DFF // 128
T = 2048
S = 8192
TH = 4
EPS = 1e-6
COMPUTE = ("tensor", "vector", "scalar", "gpsimd")


class Prog:
    def __init__(self, nc, n_dma_sems=8):
        self.nc = nc
        self.ops = []
        self.state = {}
        self.n_dma_sems = n_dma_sems

    def add(self, eng, fn, reads=(), writes=(), dma=False):
        idx = len(self.ops)
        ops = self.ops
        deps = set()
        for k in reads:
            st = self.state.setdefault(k, [None, []])
            if st[0] is not None:
                deps.add(st[0])
        for k in writes:
            st = self.state.setdefault(k, [None, []])
            if st[0] is not None:
                deps.add(st[0])
            deps.update(st[1])
        for k in reads:
            st = self.state[k]
            if not dma:
                st[1] = [r for r in st[1] if ops[r]["dma"] or ops[r]["eng"] != eng]
            st[1].append(idx)
        for k in writes:
            self.state[k] = [idx, []]
        deps.discard(idx)
        ops.append(dict(eng=eng, fn=fn, deps=deps, dma=dma))
        return idx

    def mm(self, fn, reads=(), writes=()):
        return self.add("tensor", fn, reads, writes)

    def act(self, fn, reads=(), writes=()):
        return self.add("scalar", fn, reads, writes)

    def dve(self, fn, reads=(), writes=()):
        return self.add("vector", fn, reads, writes)

    def pool(self, fn, reads=(), writes=()):
        return self.add("gpsimd", fn, reads, writes)

    def dma(self, q, out, in_, reads=(), writes=()):
        return self.add(q, lambda e: e.dma_start(out=out, in_=in_), reads, writes, dma=True)

    def barrier(self, skip_cc=False):
        for e in ("sync", "scalar", "vector", "gpsimd", "tensor"):
            self.ops.append(dict(eng=e, fn=None, deps=set(), dma=False, barrier=True, skip_cc=skip_cc))

    def cc(self, fn, reads=(), writes=()):
        idx = self.add("gpsimd", fn, reads, writes)
        self.ops[idx]["cc"] = True
        return idx

    def emit(self, final_wait_ops=()):
        nc = self.nc
        ops = self.ops
        n = len(ops)

        def skip(d, e):
            od = ops[d]
            return od["eng"] == "tensor" and e == "tensor" and not od["dma"]

        has_dep = [False] * n
        last_op = {}
        for i, o in enumerate(ops):
            if o.get("barrier"):
                for e2 in COMPUTE:
                    if e2 in last_op:
                        has_dep[last_op[e2]] = True
                continue
            if o["eng"] in COMPUTE and not o["dma"] and not o.get("cc"):
                last_op[o["eng"]] = i
            for d in o["deps"]:
                if not skip(d, o["eng"]):
                    has_dep[d] = True
        for d in final_wait_ops:
            has_dep[d] = True
        engs = ("sync", "scalar", "vector", "gpsimd", "tensor")
        stack = contextlib.ExitStack()
        sems = {}
        for e in COMPUTE:
            sems[e] = stack.enter_context(nc.semaphore("s_" + e))
        dma_sems = {}
        for q in ("sync", "scalar", "gpsimd"):
            for j in range(self.n_dma_sems):
                dma_sems[(q, j)] = stack.enter_context(nc.semaphore("d_%s%d" % (q, j)))
        sems["cc"] = stack.enter_context(nc.semaphore("s_cc"))
        cnt = {k: 0 for k in list(sems) + list(dma_sems)}
        rr = {"sync": 0, "scalar": 0, "gpsimd": 0}
        sig = [None] * n
        waits = [None] * n
        waited = {e: {} for e in engs}
        for i, o in enumerate(ops):
            e = o["eng"]
            w = []

            def need(key, val):
                if val > 0 and waited[e].get(key, 0) < val:
                    waited[e][key] = val
                    w.append((key, val))

            if o.get("barrier"):
                for key in list(cnt):
                    if key != e and not (o.get("skip_cc") and key == "cc"):
                        need(key, cnt[key])
                waits[i] = w
                continue
            for d in sorted(o["deps"]):
                if skip(d, e):
                    continue
                need(*sig[d])
            if o.get("cc"):
                cnt["cc"] += 1
                sig[i] = ("cc", cnt["cc"])
            elif o["dma"]:
                j = rr[e]
                rr[e] = (j + 1) % self.n_dma_sems
                key = (e, j)
                need(key, cnt[key])
                cnt[key] += 16
                sig[i] = (key, cnt[key])
            elif has_dep[i]:
                cnt[e] += 1
                sig[i] = (e, cnt[e])
            waits[i] = w
        allsems = dict(sems)
        allsems.update(dma_sems)
        self.max_counts = dict(cnt)
        final = [sig[d] for d in final_wait_ops]

        with stack:
            with nc.Block() as block:
                def make(ename):
                    def body(eng):
                        for i, o in enumerate(ops):
                            if o["eng"] != ename:
                                continue
                            for key, val in waits[i]:
                                eng.wait_ge(allsems[key], val)
                            if o["fn"] is None:
                                continue
                            inst = o["fn"](eng)
                            if sig[i] is not None:
                                inst.then_inc(allsems[sig[i][0]], 16 if o["dma"] else 1)
                        if ename == "sync":
                            for key, val in final:
                                eng.wait_ge(allsems[key], val)
                    return body

                block.sync(make("sync"))
                block.scalar(make("scalar"))
                block.vector(make("vector"))
                block.gpsimd(make("gpsimd"))
                block.tensor(make("tensor"))


class Ctx:
    def __init__(self):
        self.nc = bass.Bass("TRN2", target_bir_lowering=False)
        self.stack = contextlib.ExitStack()
        self.gstack = contextlib.ExitStack()
        self.p = Prog(self.nc)
        self.ps = None
        self.psi = 0
        self.prefix = ""
        self.over = {}
        self.fused = False

    def din(self, name, shape, dt=F32):
        if name in self.over:
            return self.over[name]
        return self.nc.dram_tensor(self.prefix + name, list(shape), dt, kind="ExternalInput").ap()

    def dout(self, name, shape, dt=F32):
        if name in self.over:
            return self.over[name]
        return self.nc.dram_tensor(self.prefix + name, list(shape), dt, kind="ExternalOutput").ap()

    def dint(self, name, shape, dt=F32):
        return self.nc.dram_tensor(name, list(shape), dt, addr_space="Local", kind="Internal").ap()

    def sb(self, name, shape, dt=F32):
        return self.stack.enter_context(self.nc.sbuf_tensor(self.prefix + "s_" + name, list(shape), dt))

    def init_psum(self):
        if self.ps is None:
            self.psbig = self.gstack.enter_context(self.nc.psum_tensor("psbig", [128, 8, 512], F32))
            self.ps = [self.psbig[:, i, :] for i in range(8)]

    def phase(self, prefix):
        self.stack = contextlib.ExitStack()
        return self.stack

    def nextps(self):
        b = self.psi
        self.psi = (b + 1) % 8
        return b, self.ps[b]


def emit_norm(cx, x3, xkey, N, g_sb, h3, hkey, sq, rs, rstd, ones):
    p = cx.p
    p.act(lambda e: e.activation(out=sq[:, :, :N], in_=x3, func=AF.Square), reads=[xkey], writes=["sq"])
    b, ps = cx.nextps()
    for c in range(8):
        p.mm(lambda e, c=c: e.matmul(ps[:, :N], lhsT=ones[:], rhs=sq[:, c, :N], start=(c == 0), stop=(c == 7)),
             reads=["sq", "ones"], writes=[("ps", b)])
    p.act(lambda e, eps_t=cx.eps_t: e.activation(out=rs[:, :N], in_=ps[:, :N], func=AF.Sqrt, scale=1.0 / D, bias=eps_t[:, 0:1]),
          reads=[("ps", b), "eps"], writes=["rs"])
    p.dve(lambda e: e.reciprocal(out=rstd[:, :N], in_=rs[:, :N]), reads=["rs"], writes=["rstd"])
    for c in range(8):
        eng = "vector"
        p.add(eng, lambda e, c=c: e.scalar_tensor_tensor(out=h3[:, c, :], in0=x3[:, c, :], scalar=g_sb[:, c:c + 1],
                                                     in1=rstd[:, :N], op0=ALU.mult, op1=ALU.mult),
              reads=[xkey, "rstd", "gvec"], writes=[(hkey, c)])


def load_w_bf16(cx, dst, src, K, key, nsplit=None):
    v = src.rearrange("(k p) n -> p k n", p=128)
    for k in range(K):
        cx.p.dma("gpsimd", dst[:, k, :], v[:, k, :], writes=[(key, k)])


def build_p1(cx=None):
    standalone = cx is None
    if standalone:
        cx = Ctx()
    cx.prefix = "p1_" if cx.fused else ""
    nc, p = cx.nc, cx.p
    xT = cx.din("xT", [D, TH + T])
    gvec = cx.din("gvec", [128, 8])
    w_in = cx.din("w_in", [D, 2 * D])
    convw = cx.din("convw", [128, 32])
    convb = cx.din("convb", [128, 8])
    w_r = cx.din("w_r", [D, 256])
    w_i = cx.din("w_i", [D, 256])
    b_r = cx.din("b_r", [128, 8])
    b_i = cx.din("b_i", [128, 8])
    lam = cx.din("lam", [128, 8])
    ident = cx.din("ident", [128, 128])
    hloc = cx.dout("hloc", [D, T])
    acum = cx.dout("acum", [D, T])
    gate = cx.dout("gate", [D, T], BF16)

    xTv = xT.rearrange("(c p) t -> p c t", p=128)
    hlv = hloc.rearrange("(c p) t -> p c t", p=128)
    acv = acum.rearrange("(c p) t -> p c t", p=128)
    gtv = gate.rearrange("(c p) t -> p c t", p=128)

    with cx.phase("p1_"):
        cx.init_psum()
        sb = cx.sb
        xt = [sb("xt%d" % i, [128, 8, 512]) for i in range(1)]
        sq = sb("sq", [128, 8, 512], BF16)
        rs = sb("rs", [128, 512])
        rstd = sb("rstd", [128, 512])
        h = sb("h", [128, 8, 512], BF16)
        win = sb("win", [128, 8, 2 * D], BF16)
        wr = sb("wr", [128, 8, 256], BF16)
        wi = sb("wi", [128, 8, 256], BF16)
        diag = sb("diag", [128, 32, 128], BF16)
        identf = sb("identf", [128, 128])
        ones = sb("ones", [128, 128], BF16)
        zeros = sb("zeros", [128, 512])
        cx.eps_t = sb("eps", [128, 1])
        g_sb = sb("g_sb", [128, 8])
        cw = sb("cw", [128, 32])
        cb = sb("cb", [128, 8])
        br = sb("br", [128, 8])
        bi = sb("bi", [128, 8])
        lm = sb("lm", [128, 8])
        c8 = sb("c8", [128, 8])
        c16 = sb("c16", [128, 8])
        xb = sb("xb", [128, 8, TH + T], BF16)
        gt = [sb("gt%d" % i, [128, 8, 512], BF16) for i in range(2)]
        xc = sb("xc", [128, 2, 512])
        xcb = sb("xcb", [128, 2, 512], BF16)
        rt = [sb("rt%d" % i, [128, 512]) for i in range(2)]
        it = [sb("it%d" % i, [128, 512]) for i in range(2)]
        a2 = [sb("a2%d" % i, [128, 512]) for i in range(2)]
        at = sb("at", [128, 2, 512])
        ut = sb("ut", [128, 2, 512])
        hl = [sb("hl%d" % i, [128, 8, 512]) for i in range(1)]
        ac = [sb("ac%d" % i, [128, 8, 512]) for i in range(1)]
        hlast = sb("hlast", [128, 8])
        alast = sb("alast", [128, 8])

        for dst, src, key in ((g_sb, gvec, "gvec"), (cw, convw, "cw"), (cb, convb, "cb"), (br, b_r, "br"),
                              (bi, b_i, "bi"), (lm, lam, "lm"), (identf, ident, "identf")):
            p.dma("sync", dst[:], src, writes=[key])
        p.dve(lambda e: e.memset(ones[:], 1.0), writes=["ones"])
        p.dve(lambda e: e.memset(zeros[:], 0.0), writes=["zeros"])
        p.dve(lambda e, eps_t=cx.eps_t: e.memset(eps_t[:], EPS), writes=["eps"])
        load_w_bf16(cx, win, w_in, 8, "win")
        load_w_bf16(cx, wr, w_r, 8, "wr")
        load_w_bf16(cx, wi, w_i, 8, "wi")
        p.act(lambda e: e.activation(out=c8[:], in_=lm[:], func=AF.Sigmoid), reads=["lm"], writes=["c8"])
        p.act(lambda e: e.activation(out=c8[:], in_=c8[:], func=AF.Ln), reads=["c8"], writes=["c8"])
        p.act(lambda e: e.mul(out=c16[:], in_=c8[:], mul=16.0), reads=["c8"], writes=["c16"])
        p.act(lambda e: e.mul(out=c8[:], in_=c8[:], mul=8.0), reads=["c8"], writes=["c8"])
        for jc in range(32):
            p.dve(lambda e, jc=jc: e.tensor_scalar(out=diag[:, jc, :], in0=identf[:], scalar1=cw[:, jc:jc + 1],
                                                 scalar2=None, op0=ALU.mult),
                  reads=["identf", "cw"], writes=[("diag", jc)])

        winkeys = [("win", k) for k in range(8)]
        wrkeys = [("wr", k) for k in range(8)]
        wikeys = [("wi", k) for k in range(8)]

        def do_tile(ti):
            halo = ti < 0
            N = TH if halo else 512
            t0 = 0 if halo else TH + ti * 512
            xbuf = xt[0]
            xkey = ("xt", 0)
            x3 = xbuf[:, :, :N]
            p.dma("sync", x3, xTv[:, :, t0:t0 + N], writes=[xkey])
            emit_norm(cx, x3, xkey, N, g_sb, h[:, :, :N], "h", sq, rs, rstd, ones)
            hkeys = [("h", c) for c in range(8)]
            gbuf = gt[ti % 2]
            gkey = ("gt", ti % 2)
            for n in range(0 if not halo else 8, 16):
                b, ps = cx.nextps()
                for k in range(8):
                    p.mm(lambda e, k=k, n=n, ps=ps: e.matmul(ps[:, :N], lhsT=win[:, k, n * 128:(n + 1) * 128],
                                                          rhs=h[:, k, :N], start=(k == 0), stop=(k == 7)),
                         reads=hkeys + winkeys, writes=[("ps", b)])
                if n < 8:
                    p.act(lambda e, n=n, ps=ps: e.activation(out=gbuf[:, n, :], in_=ps[:, :N], func=AF.Gelu_apprx_tanh),
                          reads=[("ps", b)], writes=[gkey])
                else:
                    p.act(lambda e, n=n, ps=ps: e.copy(out=xb[:, n - 8, t0:t0 + N], in_=ps[:, :N]),
                          reads=[("ps", b)], writes=[("xb", n - 8, ti)])
            if halo:
                return
            p.dma("sync", gtv[:, :, ti * 512:(ti + 1) * 512], gbuf[:], reads=[gkey])
            hbuf, abuf = hl[0], ac[0]
            for blk in range(4):
                for cc in range(2):
                    c = 2 * blk + cc
                    b, ps = cx.nextps()
                    for j in range(4):
                        p.mm(lambda e, j=j, c=c, ps=ps: e.matmul(ps[:], lhsT=diag[:, j * 8 + c, :],
                                                              rhs=xb[:, c, t0 - 3 + j:t0 - 3 + j + 512],
                                                              start=(j == 0), stop=(j == 3)),
                             reads=[("xb", c, ti), ("xb", c, ti - 1), ("diag", j * 8 + c)], writes=[("ps", b)])
                    p.act(lambda e, c=c, cc=cc, ps=ps: e.activation(out=xc[:, cc, :], in_=ps[:], func=AF.Identity,
                                                                bias=cb[:, c:c + 1]),
                          reads=[("ps", b), "cb"], writes=[("xc", cc)])
                    p.dve(lambda e, cc=cc: e.tensor_copy(out=xcb[:, cc, :], in_=xc[:, cc, :]),
                          reads=[("xc", cc)], writes=[("xcb", cc)])
                for cc in range(2):
                    n = 2 * blk + cc
                    rb, ib, ab = rt[cc], it[cc], a2[cc]
                    b, ps = cx.nextps()
                    for kk in range(2):
                        p.mm(lambda e, kk=kk, cc=cc, ps=ps, blk=blk: e.matmul(ps[:], lhsT=wr[:, blk * 2 + kk, cc * 128:(cc + 1) * 128],
                                                                  rhs=xcb[:, kk, :], start=(kk == 0), stop=(kk == 1)),
                             reads=[("xcb", 0), ("xcb", 1)] + wrkeys, writes=[("ps", b)])
                    p.act(lambda e, n=n, ps=ps, rb=rb: e.activation(out=rb[:], in_=ps[:], func=AF.Sigmoid, bias=br[:, n:n + 1]),
                          reads=[("ps", b), "br"], writes=[("rt", cc)])
                    b, ps = cx.nextps()
                    for kk in range(2):
                        p.mm(lambda e, kk=kk, cc=cc, ps=ps, blk=blk: e.matmul(ps[:], lhsT=wi[:, blk * 2 + kk, cc * 128:(cc + 1) * 128],
                                                                  rhs=xcb[:, kk, :], start=(kk == 0), stop=(kk == 1)),
                             reads=[("xcb", 0), ("xcb", 1)] + wikeys, writes=[("ps", b)])
                    p.act(lambda e, n=n, ps=ps, ib=ib: e.activation(out=ib[:], in_=ps[:], func=AF.Sigmoid, bias=bi[:, n:n + 1]),
                          reads=[("ps", b), "bi"], writes=[("it", cc)])
                for cc in range(2):
                    n = 2 * blk + cc
                    rb, ib, ab = rt[cc], it[cc], a2[cc]
                    p.act(lambda e, n=n, rb=rb, cc=cc: e.activation(out=at[:, cc, :], in_=rb[:], func=AF.Exp, scale=c8[:, n:n + 1]),
                          reads=[("rt", cc), "c8"], writes=[("at", cc)])
                    p.act(lambda e, n=n, rb=rb, ab=ab: e.activation(out=ab[:], in_=rb[:], func=AF.Exp, scale=c16[:, n:n + 1]),
                          reads=[("rt", cc), "c16"], writes=[("a2", cc)])
                for cc in range(2):
                    n = 2 * blk + cc
                    rb, ib, ab = rt[cc], it[cc], a2[cc]
                    p.act(lambda e, ab=ab, one_t=cx.one_t: e.activation(out=ab[:], in_=ab[:], func=AF.Sqrt, scale=-1.0, bias=one_t[:, 0:1]),
                          reads=[("a2", cc), "one"], writes=[("a2", cc)])
                    p.pool(lambda e, ib=ib, cc=cc: e.tensor_tensor(out=ib[:], in0=ib[:], in1=xc[:, cc, :], op=ALU.mult),
                           reads=[("it", cc), ("xc", cc)], writes=[("it", cc)])
                    p.dve(lambda e, cc=cc, ib=ib, ab=ab: e.tensor_tensor(out=ut[:, cc, :], in0=ib[:], in1=ab[:], op=ALU.mult),
                          reads=[("it", cc), ("a2", cc)], writes=[("ut", cc)])
                    if ti == 0:
                        hinit, ainit = 0.0, 1.0
                        ir = []
                    else:
                        hinit, ainit = hlast[:, n:n + 1], alast[:, n:n + 1]
                        ir = [("hlast", n), ("alast", n)]
                    p.dve(lambda e, n=n, cc=cc, hinit=hinit: e.tensor_tensor_scan(out=hbuf[:, n, :], data0=at[:, cc, :], data1=ut[:, cc, :],
                                                                        initial=hinit, op0=ALU.mult, op1=ALU.add),
                          reads=[("at", cc), ("ut", cc)] + ir, writes=[("hl", 0, n)])
                    p.dve(lambda e, n=n, cc=cc, ainit=ainit: e.tensor_tensor_scan(out=abuf[:, n, :], data0=at[:, cc, :], data1=zeros[:],
                                                                        initial=ainit, op0=ALU.mult, op1=ALU.add),
                          reads=[("at", cc), "zeros"] + ir, writes=[("ac", 0, n)])
                    p.dve(lambda e, n=n: e.tensor_copy(out=hlast[:, n:n + 1], in_=hbuf[:, n, 511:512]),
                          reads=[("hl", 0, n)], writes=[("hlast", n)])
                    p.dve(lambda e, n=n: e.tensor_copy(out=alast[:, n:n + 1], in_=abuf[:, n, 511:512]),
                          reads=[("ac", 0, n)], writes=[("alast", n)])
            p.dma("sync", hlv[:, :, ti * 512:(ti + 1) * 512], hbuf[:], reads=[("hl", 0, n) for n in range(8)])
            return p.dma("sync", acv[:, :, ti * 512:(ti + 1) * 512], abuf[:], reads=[("ac", 0, n) for n in range(8)])

        cx.one_t = sb("one_t", [128, 1])
        p.dve(lambda e, one_t=cx.one_t: e.memset(one_t[:], 1.0), writes=["one"])
        do_tile(-1)
        last = None
        for ti in range(4):
            last = do_tile(ti)
        if cx.fused:
            es = cx.over["ends_src"]
            p.dma("sync", es[:, 0:8], hlast[:], reads=[("hlast", n) for n in range(8)])
            p.dma("sync", es[:, 8:16], alast[:], reads=[("alast", n) for n in range(8)])
        if standalone:
            outs = [i for i, o in enumerate(p.ops) if o["dma"] and o["eng"] == "sync"]
            p.emit(final_wait_ops=outs[-12:])
    return nc


def build_p24(first, out_f32, cx=None):
    standalone = cx is None
    if standalone:
        cx = Ctx()
    fused = cx.fused
    cx.prefix = ("p2_" if first else "p4_") if fused else ""
    nc, p = cx.nc, cx.p
    xT = cx.din("xT", [D, T])
    if first:
        hloc = cx.din("hloc", [D, T])
        acum = cx.din("acum", [D, T])
        gate = cx.din("gate", [D, T], BF16)
        if not fused:
            ends = cx.din("ends", [128, 48])
        else:
            km_d = cx.din("km", [128, 4])
            hm_d = cx.din("hm", [128, 4])
    else:
        if not fused:
            oT = cx.din("oT", [D, T], BF16)
        else:
            sel_d = cx.din("sel", [128, 4])
    w_out = cx.din("w_out", [8, 128, 8 * 128])
    g_ffn = cx.din("g_ffn", [128, 8])
    g_next = cx.din("g_next", [128, 8])
    wg = cx.din("wg", [NF // 2, 128, 8 * 256])
    wu = cx.din("wu", [NF // 2, 128, 8 * 256])
    wd = cx.din("wd", [8, 128, NF * 128])
    x2T = cx.dout("x2T", [D, T])
    hnT = cx.dout("hnT", [D, T], F32 if out_f32 else BF16)

    v3 = lambda a: a.rearrange("(c p) t -> p c t", p=128)
    xTv, x2v, hnv = v3(xT), v3(x2T), v3(hnT)

    with cx.phase("p2_" if first else "p4_"):
        cx.init_psum()
        sb = cx.sb
        xr = sb("xr", [128, 8, 1024])
        y = [sb("y%d" % i, [128, 8, 512], BF16) for i in range(2)]
        sq = sb("sq", [128, 8, 512], BF16)
        rs = sb("rs", [128, 512])
        rstd = sb("rstd", [128, 512])
        h2 = sb("h2", [128, 8, 1024], BF16)
        actb = sb("actb", [128, NF, 1024], BF16)
        sg = [sb("sg%d" % i, [128, 512]) for i in range(2)]
        wo = sb("wo", [128, 8, 8 * 128], BF16)
        wgs = [sb("wgs%d" % i, [128, 8 * 256], BF16) for i in range(3)]
        wus = [sb("wus%d" % i, [128, 8 * 256], BF16) for i in range(3)]
        wds = [sb("wds%d" % i, [128, NF * 128], BF16) for i in range(2)]
        hn = [sb("hn%d" % i, [128, 8, 512], F32 if out_f32 else BF16) for i in range(1 if out_f32 else 2)]
        ones = sb("ones", [128, 128], BF16)
        cx.eps_t = sb("eps", [128, 1])
        gf = sb("gf", [128, 8])
        gn = sb("gn", [128, 8])
        p.dma("sync", gf[:], g_ffn, writes=["gf"])
        p.dma("sync", gn[:], g_next, writes=["gn"])
        p.dve(lambda e: e.memset(ones[:], 1.0), writes=["ones"])
        p.dve(lambda e, eps_t=cx.eps_t: e.memset(eps_t[:], EPS), writes=["eps"])
        for n in range(8):
            p.dma("gpsimd", wo[:, n, :], w_out[n], writes=[("wo", n)])
        if first:
            en = sb("en", [128, 48])
            carry = sb("carry", [128, 8])
            ctmp = sb("ctmp", [128, 8])
            hlc = [sb("hlc%d" % i, [128, 512]) for i in range(2)]
            acc = [sb("acc%d" % i, [128, 512]) for i in range(2)]
            gtc = [sb("gtc%d" % i, [128, 512], BF16) for i in range(2)]
            hlv, acv, gtv = v3(hloc), v3(acum), v3(gate)
            p.dve(lambda e: e.memset(carry[:], 0.0), writes=["carry"])
            if fused:
                en_all = sb("en_all", [128, 4, 16])
                km = sb("km", [128, 4])
                hm = sb("hm", [128, 4])
                ap_t = sb("ap_t", [128, 8])
                hp_t = sb("hp_t", [128, 8])
                p.dma("sync", en_all[:], cx.over["ends_all"].rearrange("(r p) f -> p r f", p=128), reads=["d_ends_all"], writes=["en_all"])
                p.dma("sync", km[:], km_d, writes=["km"])
                p.dma("sync", hm[:], hm_d, writes=["hm"])
                for r in range(4):
                    p.dve(lambda e, r=r: e.tensor_scalar(out=ap_t[:], in0=en_all[:, r, 8:16], scalar1=km[:, r:r + 1],
                                                       scalar2=hm[:, r:r + 1], op0=ALU.mult, op1=ALU.add),
                          reads=["en_all", "km", "hm"], writes=["ap_t"])
                    p.dve(lambda e, r=r: e.tensor_scalar(out=hp_t[:], in0=en_all[:, r, 0:8], scalar1=km[:, r:r + 1],
                                                       scalar2=None, op0=ALU.mult),
                          reads=["en_all", "km"], writes=["hp_t"])
                    p.dve(lambda e: e.tensor_tensor(out=ctmp[:], in0=ap_t[:], in1=carry[:], op=ALU.mult),
                          reads=["ap_t", "carry"], writes=["ctmp"])
                    p.dve(lambda e: e.tensor_tensor(out=carry[:], in0=ctmp[:], in1=hp_t[:], op=ALU.add),
                          reads=["ctmp", "hp_t"], writes=["carry"])
            else:
                p.dma("sync", en[:], ends, writes=["en"])
            for j in range(0 if fused else 3):
                p.dve(lambda e, j=j: e.tensor_tensor(out=ctmp[:], in0=en[:, j * 16 + 8:j * 16 + 16], in1=carry[:], op=ALU.mult),
                      reads=["en", "carry"], writes=["ctmp"])
                p.dve(lambda e, j=j: e.tensor_tensor(out=carry[:], in0=ctmp[:], in1=en[:, j * 16:j * 16 + 8], op=ALU.add),
                      reads=["en", "ctmp"], writes=["carry"])
        elif not fused:
            oTv = v3(oT)
        else:
            oallv = [a.rearrange("(c p) t -> p c t", p=128) for a in cx.over["o_all"]]
            sel = sb("sel", [128, 4])
            cand = [sb("cand%d" % i, [128, 8, 512], BF16) for i in range(2)]
            p.dma("sync", sel[:], sel_d, writes=["sel"])

        def norm_tile(x3, xkey, g_sb, gkey, h3, hkey):
            N = 512
            p.act(lambda e: e.activation(out=sq[:], in_=x3, func=AF.Square), reads=xkey, writes=["sq"])
            b, ps = cx.nextps()
            for c in range(8):
                p.mm(lambda e, c=c: e.matmul(ps[:], lhsT=ones[:], rhs=sq[:, c, :], start=(c == 0), stop=(c == 7)),
                     reads=["sq", "ones"], writes=[("ps", b)])
            p.act(lambda e, eps_t=cx.eps_t: e.activation(out=rs[:], in_=ps[:], func=AF.Sqrt, scale=1.0 / D, bias=eps_t[:, 0:1]),
                  reads=[("ps", b), "eps"], writes=["rs"])
            p.dve(lambda e: e.reciprocal(out=rstd[:], in_=rs[:]), reads=["rs"], writes=["rstd"])
            for c in range(8):
                p.dve(lambda e, c=c: e.scalar_tensor_tensor(out=h3[:, c, :], in0=x3[:, c, :], scalar=g_sb[:, c:c + 1],
                                                          in1=rstd[:], op0=ALU.mult, op1=ALU.mult),
                      reads=list(xkey) + ["rstd", gkey], writes=[(hkey, c)])

        wokeys = [("wo", n) for n in range(8)]
        cidx = [0]

        def load_gu(f2, slot):
            p.dma("gpsimd", wgs[slot][:], wg[f2], writes=[("wgs", slot)])
            p.dma("gpsimd", wus[slot][:], wu[f2], writes=[("wus", slot)])

        def load_d(n, slot):
            p.dma("gpsimd", wds[slot][:], wd[n], writes=[("wds", slot)])

        outs = []
        for half in range(2):
            hx = half * 1024
            xkeys = lambda tt: [("xr", tt, c) for c in range(8)]
            for tt in range(2):
                p.dma("sync", xr[:, :, tt * 512:(tt + 1) * 512], xTv[:, :, hx + tt * 512:hx + (tt + 1) * 512],
                      writes=xkeys(tt))
            load_gu(0, 0)
            load_gu(1, 1)
            for tt in range(2):
                t0 = hx + tt * 512
                yb = y[tt]
                ykeys = [("y", tt, c) for c in range(8)]
                if first:
                    for c in range(8):
                        s = cidx[0] % 2
                        cidx[0] += 1
                        p.dma("sync", hlc[s][:], hlv[:, c, t0:t0 + 512], writes=[("hlc", s)])
                        p.dma("sync", acc[s][:], acv[:, c, t0:t0 + 512], writes=[("acc", s)])
                        p.dma("sync", gtc[s][:], gtv[:, c, t0:t0 + 512], writes=[("gtc", s)])
                        p.dve(lambda e, s=s, c=c: e.scalar_tensor_tensor(out=hlc[s][:], in0=acc[s][:], scalar=carry[:, c:c + 1],
                                                                      in1=hlc[s][:], op0=ALU.mult, op1=ALU.add),
                              reads=[("hlc", s), ("acc", s), "carry"], writes=[("hlc", s)])
                        p.pool(lambda e, s=s, c=c, yb=yb: e.tensor_tensor(out=yb[:, c, :], in0=hlc[s][:], in1=gtc[s][:], op=ALU.mult),
                               reads=[("hlc", s), ("gtc", s)], writes=[("y", tt, c)])
                elif not fused:
                    p.dma("sync", yb[:], oTv[:, :, t0:t0 + 512], writes=ykeys)
                else:
                    for j in range(4):
                        cb = cand[j % 2]
                        ck = ("cand", j % 2)
                        p.dma("sync", cb[:], oallv[j][:, :, t0:t0 + 512], reads=[("d_oall", j)], writes=[ck])
                        if j == 0:
                            p.dve(lambda e, cb=cb, yb=yb: e.tensor_scalar(out=yb[:], in0=cb[:], scalar1=sel[:, 0:1], scalar2=None,
                                                                       op0=ALU.mult),
                                  reads=[ck, "sel"], writes=ykeys)
                        else:
                            p.dve(lambda e, cb=cb, yb=yb, j=j: e.scalar_tensor_tensor(out=yb[:], in0=cb[:], scalar=sel[:, j:j + 1],
                                                                                   in1=yb[:], op0=ALU.mult, op1=ALU.add),
                                  reads=[ck, "sel"] + ykeys, writes=ykeys)
                for n in range(8):
                    b, ps = cx.nextps()
                    for k in range(8):
                        p.mm(lambda e, k=k, n=n, ps=ps, yb=yb: e.matmul(ps[:], lhsT=wo[:, n, k * 128:(k + 1) * 128], rhs=yb[:, k, :],
                                                                     start=(k == 0), stop=(k == 7)),
                             reads=ykeys + [("wo", n)], writes=[("ps", b)])
                    p.dve(lambda e, n=n, ps=ps, tt=tt: e.tensor_tensor(out=xr[:, n, tt * 512:(tt + 1) * 512],
                                                                     in0=xr[:, n, tt * 512:(tt + 1) * 512], in1=ps[:], op=ALU.add),
                          reads=[("ps", b), ("xr", tt, n)], writes=[("xr", tt, n)])
                norm_tile(xr[:, :, tt * 512:(tt + 1) * 512], xkeys(tt), gf, "gf", h2[:, :, tt * 512:(tt + 1) * 512], ("h2", tt))
            for f2 in range(NF // 2):
                slot = f2 % 3
                if f2 + 2 < NF // 2:
                    load_gu(f2 + 2, (f2 + 2) % 3)
                if f2 == NF // 2 - 2:
                    load_d(0, 0)
                if f2 == NF // 2 - 1:
                    load_d(1, 1)
                for ff in range(2):
                    f = 2 * f2 + ff
                    for tt in range(2):
                        h2k = [(("h2", tt), c) for c in range(8)]
                        bg, psg = cx.nextps()
                        for k in range(8):
                            p.mm(lambda e, k=k, ff=ff, psg=psg, slot=slot, tt=tt: e.matmul(
                                psg[:], lhsT=wgs[slot][:, k * 256 + ff * 128:k * 256 + (ff + 1) * 128],
                                rhs=h2[:, k, tt * 512:(tt + 1) * 512], start=(k == 0), stop=(k == 7)),
                                reads=h2k + [("wgs", slot)], writes=[("ps", bg)])
                        bu, psu = cx.nextps()
                        for k in range(8):
                            p.mm(lambda e, k=k, ff=ff, psu=psu, slot=slot, tt=tt: e.matmul(
                                psu[:], lhsT=wus[slot][:, k * 256 + ff * 128:k * 256 + (ff + 1) * 128],
                                rhs=h2[:, k, tt * 512:(tt + 1) * 512], start=(k == 0), stop=(k == 7)),
                                reads=h2k + [("wus", slot)], writes=[("ps", bu)])
                        s = (f * 2 + tt) % 2
                        p.act(lambda e, s=s, psg=psg: e.activation(out=sg[s][:], in_=psg[:], func=AF.Silu),
                              reads=[("ps", bg)], writes=[("sg", s)])
                        p.dve(lambda e, s=s, psu=psu, f=f, tt=tt: e.tensor_tensor(out=actb[:, f, tt * 512:(tt + 1) * 512],
                                                                               in0=sg[s][:], in1=psu[:], op=ALU.mult),
                              reads=[("sg", s), ("ps", bu)], writes=[("actb", f, tt)])
            for n in range(8):
                slot = n % 2
                for tt in range(2):
                    b, ps = cx.nextps()
                    for f in range(NF):
                        p.mm(lambda e, f=f, ps=ps, slot=slot, tt=tt: e.matmul(
                            ps[:], lhsT=wds[slot][:, f * 128:(f + 1) * 128], rhs=actb[:, f, tt * 512:(tt + 1) * 512],
                            start=(f == 0), stop=(f == NF - 1)),
                            reads=[("actb", f, tt), ("wds", slot)], writes=[("ps", b)])
                    p.dve(lambda e, n=n, ps=ps, tt=tt: e.tensor_tensor(out=xr[:, n, tt * 512:(tt + 1) * 512],
                                                                     in0=xr[:, n, tt * 512:(tt + 1) * 512], in1=ps[:], op=ALU.add),
                          reads=[("ps", b), ("xr", tt, n)], writes=[("xr", tt, n)])
                if n + 2 < 8:
                    load_d(n + 2, slot)
            for tt in range(2):
                t0 = hx + tt * 512
                if first or not fused:
                    outs.append(p.dma("sync", x2v[:, :, t0:t0 + 512], xr[:, :, tt * 512:(tt + 1) * 512], reads=xkeys(tt)))
                hi_ = tt % len(hn)
                norm_tile(xr[:, :, tt * 512:(tt + 1) * 512], xkeys(tt), gn, "gn", hn[hi_], ("hn", hi_))
                hdst = hnv[:, :, t0:t0 + 512] if not (fused and first) else cx.over["hn_tiles"][half * 2 + tt].rearrange("(c p) t -> p c t", p=128)
                gti = half * 2 + tt
                outs.append(p.dma("sync", hdst, hn[hi_][:], reads=[(("hn", hi_), c) for c in range(8)], writes=[("d_hn", gti)]))
                if fused and first:
                    p.cc(lambda e, a=cx.over["hn_tiles"][gti], b=cx.over["hn_all"][gti]: e.collective_compute(
                        "AllGather", ALU.bypass, replica_groups=[[0, 1, 2, 3], [4, 5, 6, 7]], ins=[a], outs=[b]),
                         reads=[("d_hn", gti)], writes=[("d_hnall", gti)])
        cx.last_outs = outs
        if standalone:
            p.emit(final_wait_ops=outs[-8:])
    return nc


def col8(v):
    return np.ascontiguousarray(np.asarray(v, np.float32).reshape(8, 128).T)


def slab_kn(w, ncol):
    K = w.shape[0] // 128
    ns = w.shape[1] // ncol
    a = np.asarray(w, np.float32).reshape(K, 128, ns, ncol).transpose(2, 1, 0, 3)
    return np.ascontiguousarray(a.reshape(ns, 128, K * ncol))


_NC_CACHE = {}


def get_nc(name, fn):
    if name not in _NC_CACHE:
        _NC_CACHE[name] = fn()
    return _NC_CACHE[name]


def run(nc, in_maps):
    res = run_bass_kernel_spmd(nc, in_maps, core_ids=list(range(8)))
    return res.results


def core_tokens(r):
    return r // 4, (r % 4) * T


def stage_p1(inp):
    x = np.asarray(inp["x"], np.float32)
    maps = []
    common = dict(
        gvec=col8(inp["norm_mix_g"][0]),
        w_in=np.ascontiguousarray(inp["a_w_in"][0], dtype=np.float32),
        convw=np.ascontiguousarray(np.concatenate([col8(inp["a_conv_w"][0][j]) for j in range(4)], axis=1)),
        convb=col8(inp["a_conv_b"][0]),
        w_r=np.ascontiguousarray(np.asarray(inp["a_w_r"][0], np.float32).reshape(D, 256)),
        w_i=np.ascontiguousarray(np.asarray(inp["a_w_i"][0], np.float32).reshape(D, 256)),
        b_r=col8(inp["a_b_r"][0]), b_i=col8(inp["a_b_i"][0]), lam=col8(inp["a_lambda"][0]),
        ident=np.eye(128, dtype=np.float32),
    )
    for r in range(8):
        b, t0 = core_tokens(r)
        xs = np.zeros((TH + T, D), np.float32)
        if t0 == 0:
            xs[TH:] = x[b, 0:T]
        else:
            xs[:] = x[b, t0 - TH:t0 + T]
        m = dict(common)
        m["xT"] = np.ascontiguousarray(xs.T)
        maps.append(m)
    return maps


def ffn_weights(inp, layer):
    return dict(
        g_ffn=col8(inp["norm_ffn_g"][layer]),
        wg=slab_kn(inp["ffn_w_gate"][layer], 256),
        wu=slab_kn(inp["ffn_w_up"][layer], 256),
        wd=slab_kn(inp["ffn_w_down"][layer], 128),
    )


def stage_p2(inp, r1):
    x = np.asarray(inp["x"], np.float32)
    common = ffn_weights(inp, 0)
    common["w_out"] = slab_kn(inp["a_w_out"][0], 128)
    common["g_next"] = col8(inp["norm_mix_g"][1])
    maps = []
    for r in range(8):
        b, t0 = core_tokens(r)
        q = r % 4
        m = dict(common)
        m["xT"] = np.ascontiguousarray(x[b, t0:t0 + T].T)
        m["hloc"], m["acum"], m["gate"] = r1[r]["hloc"], r1[r]["acum"], r1[r]["gate"]
        ends = np.zeros((128, 3, 2, 8), np.float32)
        for j in range(3):
            src = q - 3 + j
            if src >= 0:
                rr = b * 4 + src
                ends[:, j, 0, :] = col8(r1[rr]["hloc"][:, T - 1])
                ends[:, j, 1, :] = col8(r1[rr]["acum"][:, T - 1])
        m["ends"] = np.ascontiguousarray(ends.reshape(128, 48))
        maps.append(m)
    return maps


def build_p3(cx=None):
    standalone = cx is None
    if standalone:
        cx = Ctx()
    fused = cx.fused
    cx.prefix = "p3_" if fused else ""
    nc, p = cx.nc, cx.p
    hn = cx.din("hn", [D, S], BF16)
    wq = cx.din("wq", [128, 8 * 256])
    wk = cx.din("wk", [128, 8 * 256])
    wv = cx.din("wv", [128, 8 * 256])
    umat = cx.din("umat", [128, 128])
    masks = cx.din("masks", [128, 4 * 512])
    oT = cx.dout("oT", [256, S], BF16)
    NQ = S // 512
    if not fused:
        hnv = hn.rearrange("(c p) t -> p c t", p=128)
        hn_tile = lambda ti: hnv[:, :, ti * 512:(ti + 1) * 512]
        o_dst = lambda head, Qi: oT[head * 64:(head + 1) * 64, Qi * 512:(Qi + 1) * 512]
    else:
        hn_tile = lambda ti: hn[ti % 4][(ti // 4) * D:(ti // 4 + 1) * D, :].rearrange("(c p) t -> p c t", p=128)
        o_dst = lambda head, Qi: oT[Qi // 4][head * 64:(head + 1) * 64, (Qi % 4) * 512:(Qi % 4 + 1) * 512]

    with cx.phase("p3_"):
        cx.init_psum()
        sb = cx.sb
        hb = [sb("hb%d" % i, [128, 8, 512], BF16) for i in range(2)]
        wqs = sb("wqs", [128, 8 * 256], BF16)
        wks = sb("wks", [128, 8 * 256], BF16)
        wvs = sb("wvs", [128, 8 * 256], BF16)
        qs = sb("qs", [128, 2, S], BF16)
        kt = sb("kt", [128, 2, S], BF16)
        vv = sb("vv", [128, S // 128, 256], BF16)
        um = sb("um", [128, 128])
        onesr = sb("onesr", [128, 128])
        mk = sb("mk", [128, 4, 512])
        ob = [sb("ob%d" % i, [64, 512], BF16) for i in range(2)]

        p.dma("sync", mk[:], masks.rearrange("p (m f) -> p m f", m=4), writes=["mk"])
        umf = sb("umf", [128, 128])
        onesf = sb("onesf", [128, 128])
        p.dma("sync", umf[:], umat, writes=["umf"])
        p.dve(lambda e: e.memset(onesf[:], -1.0), writes=["onesf"])
        p.dve(lambda e: e.tensor_scalar(out=um[:].bitcast(F32R), in0=umf[:], scalar1=-1.0, scalar2=None, op0=ALU.mult), reads=["umf"], writes=["um"])
        p.dve(lambda e: e.tensor_copy(out=onesr[:].bitcast(F32R), in_=onesf[:]), reads=["onesf"], writes=["onesr"])
        p.dma("gpsimd", wqs[:], wq, writes=["wqs"])
        p.dma("gpsimd", wks[:], wk, writes=["wks"])
        p.dma("gpsimd", wvs[:], wv, writes=["wvs"])

        import os
        for oi_, ti in enumerate(sorted(range(int(os.environ.get("P3_PT", str(NQ)))), key=lambda a: (a % 4, a // 4))):
            hbuf = hb[oi_ % 2]
            hkey = ("hb", oi_ % 2)
            p.dma("sync", hbuf[:], hn_tile(ti), reads=[("d_hnall", ti % 4)], writes=[hkey])
            tsl = slice(ti * 512, (ti + 1) * 512)
            SKIP = os.environ.get("P3_SKIP", "")
            for pair in range(2):
                if "q" in SKIP:
                    break
                b, ps = cx.nextps()
                for k in range(8):
                    p.mm(lambda e, k=k, pair=pair, ps=ps, hbuf=hbuf: e.matmul(
                        ps[:], lhsT=wqs[:, k * 256 + pair * 128:k * 256 + (pair + 1) * 128], rhs=hbuf[:, k, :],
                        start=(k == 0), stop=(k == 7)), reads=[hkey, "wqs"], writes=[("ps", b)])
                p.act(lambda e, pair=pair, ps=ps, tsl=tsl: e.mul(out=qs[:, pair, tsl], in_=ps[:], mul=0.125),
                      reads=[("ps", b)], writes=[("qs", pair, ti)])
                b, ps = cx.nextps()
                for k in range(8):
                    p.mm(lambda e, k=k, pair=pair, ps=ps, hbuf=hbuf: e.matmul(
                        ps[:], lhsT=wks[:, k * 256 + pair * 128:k * 256 + (pair + 1) * 128], rhs=hbuf[:, k, :],
                        start=(k == 0), stop=(k == 7)), reads=[hkey, "wks"], writes=[("ps", b)])
                p.act(lambda e, pair=pair, ps=ps, tsl=tsl: e.copy(out=kt[:, pair, tsl], in_=ps[:]),
                      reads=[("ps", b)], writes=[("kt", pair, ti)])
            for tb in range(4):
                if "v" in SKIP:
                    break
                b, ps = cx.nextps()
                for k in range(8):
                    p.mm(lambda e, k=k, tb=tb, ps=ps, hbuf=hbuf: e.matmul(
                        ps[:, 0:256], lhsT=hbuf[:, k, tb * 128:(tb + 1) * 128], rhs=wvs[:, k * 256:(k + 1) * 256],
                        start=(k == 0), stop=(k == 7)), reads=[hkey, "wvs"], writes=[("ps", b)])
                p.dve(lambda e, tb=tb, ps=ps, ti=ti: e.tensor_copy(out=vv[:, ti * 4 + tb, :], in_=ps[:, 0:256]),
                      reads=[("ps", b)], writes=[("vv", ti)])

        outs = []
        um_r = um[:].bitcast(F32R)
        ones_r = onesr[:].bitcast(F32R)
        import os
        DBG_H = int(os.environ.get("P3_HEADS", "4"))
        DBG_Q = int(os.environ.get("P3_NQ", str(NQ)))
        LB = 1
        LC = 3
        NSP = LB + 2
        NWT = LC - LB + 2
        NRS = LB + 2
        etmp = [sb("etmp%d" % i, [128, 2, 512]) for i in range(NSP)]
        pw = [sb("pw%d" % i, [128, 2, 512]) for i in range(2)]
        spt = [sb("sptp%d" % i, [128, 2, 512]) for i in range(NSP)]
        wt = [sb("wtp%d" % i, [128, 2, 512], BF16) for i in range(NWT)]
        rsum = [sb("rsump%d" % i, [128, 2, 512]) for i in range(NRS)]
        one_t = sb("one_t", [128, 1])
        p.dve(lambda e: e.memset(one_t[:], 1.0), writes=["one"])
        psb = cx.psbig
        pairs = []
        gi = 0
        for Qi in range(DBG_Q):
            for pair in range(2):
                for hh in range(2):
                    head = pair * 2 + hh
                    if head >= DBG_H:
                        continue
                    nkb = 4 * (Qi + 1)
                    for kb in range(nkb - 1, -1, -2):
                        pairs.append(dict(pair=pair, hh=hh, head=head, Qi=Qi, kb=kb, first=(kb == nkb - 1), last=(kb == 1),
                                          diag=(kb - 4 * Qi >= 0), dpi=(0 if kb - 4 * Qi == 3 else 1), g=gi))
                    gi += 1
        n = len(pairs)
        for j, t in enumerate(pairs):
            t["zb"] = 2 * (j % 3)
            t["db"] = t["zb"]
            t["sp"] = j % NSP
            t["wt"] = j % NWT
            t["rs"] = j % NRS
            t["ob"] = 6 + t["g"] % 2
            t["rsp"] = None if t["first"] else pairs[j - 1]["rs"]

        def stage_a(t):
            pair, prt = t["pair"], slice(64 * t["hh"], 64 * t["hh"] + 64)
            qsl = slice(t["Qi"] * 512, (t["Qi"] + 1) * 512)
            zb, sp_t, e_t = t["zb"], spt[t["sp"]], etmp[t["sp"]]
            zkeys = [("ps", zb), ("ps", zb + 1)]
            for u in range(2):
                kb = t["kb"] - u
                ksl = slice(kb * 128, (kb + 1) * 128)
                p.mm(lambda e, u=u, ksl=ksl: e.matmul(psb[:, zb + u, :], lhsT=kt[prt, pair, ksl], rhs=qs[prt, pair, qsl],
                                                      start=True, stop=False),
                     reads=[("qs", pair, t["Qi"]), ("kt", pair, kb // 4)], writes=[("ps", zb + u)])
            p.act(lambda e: e.activation(out=e_t[:], in_=psb[:, zb:zb + 2, :], func=AF.Exp),
                  reads=zkeys, writes=[("etmp", t["sp"])])
            p.act(lambda e: e.activation(out=sp_t[:].bitcast(F32R), in_=e_t[:], func=AF.Ln, bias=1.0),
                  reads=[("etmp", t["sp"])], writes=[("spt", t["sp"])])
            if t["diag"]:
                d0 = 2 * t["dpi"]
                p.dve(lambda e: e.tensor_tensor(out=sp_t[:].bitcast(F32R), in0=sp_t[:], in1=mk[:, d0:d0 + 2, :], op=ALU.mult),
                      reads=[("spt", t["sp"]), "mk"], writes=[("spt", t["sp"])])
            rn = rsum[t["rs"]]
            if t["first"]:
                p.dve(lambda e: e.tensor_copy(out=rn[:, 0, :].bitcast(F32R), in_=sp_t[:, 0, :]),
                      reads=[("spt", t["sp"])], writes=[("rsum", t["rs"], 0)])
            else:
                rp = rsum[t["rsp"]]
                p.dve(lambda e: e.tensor_tensor(out=rn[:, 0, :].bitcast(F32R), in0=rp[:, 1, :], in1=sp_t[:, 0, :], op=ALU.add),
                      reads=[("spt", t["sp"]), ("rsum", t["rsp"], 1)], writes=[("rsum", t["rs"], 0)])
            if not t["last"]:
                p.dve(lambda e: e.tensor_tensor(out=rn[:, 1, :].bitcast(F32R), in0=rn[:, 0, :], in1=sp_t[:, 1, :], op=ALU.add),
                      reads=[("spt", t["sp"]), ("rsum", t["rs"], 0)], writes=[("rsum", t["rs"], 1)])

        def stage_b(t):
            db, sp_t, w_t, e_t = t["db"], spt[t["sp"]], wt[t["wt"]], etmp[t["sp"]]
            rn = rsum[t["rs"]]
            p_t = pw[t["wt"] % 2]
            pk = ("pw", t["wt"] % 2)
            for u in range(2):
                first_tile = t["first"] and u == 0
                p.mm(lambda e, u=u, first_tile=first_tile: e.matmul(psb[:, db + u, :], lhsT=um_r, rhs=sp_t[:, u, :].bitcast(F32R),
                                                                    start=False, stop=first_tile),
                     reads=[("spt", t["sp"]), "um"], writes=[("ps", db + u)])
            for u in range(2):
                first_tile = t["first"] and u == 0
                if not first_tile:
                    if u == 0:
                        rsrc, rkey = rsum[t["rsp"]][:, 1, :], ("rsum", t["rsp"], 1)
                    else:
                        rsrc, rkey = rn[:, 0, :], ("rsum", t["rs"], 0)
                    p.mm(lambda e, u=u, rsrc=rsrc: e.matmul(psb[:, db + u, :], lhsT=ones_r, rhs=rsrc.bitcast(F32R),
                                                            start=False, stop=True),
                         reads=[rkey, "onesr"], writes=[("ps", db + u)])
            p.act(lambda e: e.activation(out=w_t[:], in_=psb[:, db:db + 2, :], func=AF.Exp),
                  reads=[("ps", db), ("ps", db + 1)], writes=[("wt", t["wt"], 0), ("wt", t["wt"], 1)])
            if t["diag"]:
                d0 = 2 * t["dpi"]
                p.dve(lambda e: e.tensor_tensor(out=w_t[:], in0=w_t[:], in1=mk[:, d0:d0 + 2, :], op=ALU.mult),
                      reads=[("wt", t["wt"], 0), ("wt", t["wt"], 1), "mk"], writes=[("wt", t["wt"], 0), ("wt", t["wt"], 1)])

        def stage_c(t):
            ops_, w_t, head = cx.ps[t["ob"]], wt[t["wt"]], t["head"]
            for u in range(2):
                kb = t["kb"] - u
                p.mm(lambda e, u=u, kb=kb: e.matmul(ops_[0:64, :], lhsT=vv[:, kb, head * 64:(head + 1) * 64], rhs=w_t[:, u, :],
                                                    start=(t["first"] and u == 0), stop=(t["last"] and u == 1)),
                     reads=[("wt", t["wt"], u), ("vv", kb // 4)], writes=[("ps", t["ob"])])
            if t["last"]:
                obuf = ob[t["g"] % 2]
                p.act(lambda e: e.copy(out=obuf[:], in_=ops_[0:64, :]),
                      reads=[("ps", t["ob"])], writes=[("ob", t["g"] % 2)])
                qk = t["Qi"] // 4
                outs.append(p.dma("sync", o_dst(head, t["Qi"]), obuf[:], reads=[("ob", t["g"] % 2)], writes=[("d_o", qk, t["Qi"] % 4, head)]))
                if fused and t["Qi"] % 4 == 3 and head == 3:
                    p.cc(lambda e, a=cx.over["oT"][qk], b=cx.over["o_all"][qk]: e.collective_compute(
                        "AllGather", ALU.bypass, replica_groups=[[0, 1, 2, 3], [4, 5, 6, 7]], ins=[a], outs=[b]),
                         reads=[("d_o", qk, a, b) for a in range(4) for b in range(4)], writes=[("d_oall", qk)])

        for i in range(n + LC):
            if i < n:
                stage_a(pairs[i])
            if 0 <= i - LB < n:
                stage_b(pairs[i - LB])
            if 0 <= i - LC < n:
                stage_c(pairs[i - LC])
        cx.last_outs = outs
        if standalone:
            p.emit(final_wait_ops=outs[-8:])
    return nc


def attn_consts():
    pp = np.arange(128)[:, None]
    um = (pp >= np.arange(128)[None, :]).astype(np.float32)
    f = np.arange(512)[None, :]
    mk = np.concatenate([(f > (128 * m + pp)).astype(np.float32) for m in (3, 2, 1, 0)], axis=1)
    return um, np.ascontiguousarray(mk)


def stage_p3(inp, r2):
    um, mk = attn_consts()
    wqkv = np.asarray(inp["b_w_qkv"][0], np.float32)
    maps = []
    for r in range(8):
        b, hq = r // 4, r % 4
        hn = np.concatenate([r2[b * 4 + q]["hnT"] for q in range(4)], axis=1)
        cols = slice(hq * 256, (hq + 1) * 256)
        m = dict(
            hn=np.ascontiguousarray(hn),
            wq=slab_kn(wqkv[:, 0:D][:, cols], 256)[0],
            wk=slab_kn(wqkv[:, D:2 * D][:, cols], 256)[0],
            wv=slab_kn(wqkv[:, 2 * D:3 * D][:, cols], 256)[0],
            umat=um, masks=mk,
        )
        maps.append(m)
    return maps


def stage_p4(inp, r2, r3):
    common = ffn_weights(inp, 1)
    common["w_out"] = slab_kn(inp["b_w_out"][0], 128)
    common["g_next"] = col8(inp["final_g"])
    maps = []
    for r in range(8):
        b, q = r // 4, r % 4
        m = dict(common)
        m["xT"] = r2[r]["x2T"]
        m["oT"] = np.ascontiguousarray(
            np.concatenate([r3[b * 4 + hq]["oT"][:, q * T:(q + 1) * T] for hq in range(4)], axis=0))
        maps.append(m)
    return maps


def build_fused():
    cx = Ctx()
    cx.fused = True
    nc, p = cx.nc, cx.p
    xT = nc.dram_tensor("p1_xT", [D, TH + T], F32, kind="ExternalInput").ap()
    out = nc.dram_tensor("out", [D, T], F32, kind="ExternalOutput").ap()
    hloc = cx.dint("i_hloc", [D, T])
    acum = cx.dint("i_acum", [D, T])
    gate = cx.dint("i_gate", [D, T], BF16)
    ends_src = cx.dint("i_ends_src", [128, 16])
    ends_all = cx.dint("i_ends_all", [4 * 128, 16])
    x2 = cx.dint("i_x2", [D, T])
    hn_src = [cx.dint("i_hn_src%d" % k, [D, 512], BF16) for k in range(4)]
    hn_all = [cx.dint("i_hn_all%d" % k, [4 * D, 512], BF16) for k in range(4)]
    o_src = [cx.dint("i_o_src%d" % k, [256, T], BF16) for k in range(4)]
    o_all = [cx.dint("i_o_all%d" % k, [4 * 256, T], BF16) for k in range(4)]
    grp4 = [[0, 1, 2, 3], [4, 5, 6, 7]]
    grp8 = [list(range(8))]

    cx.over = dict(xT=xT, hloc=hloc, acum=acum, gate=gate, ends_src=ends_src)
    build_p1(cx)
    p.barrier()
    import os
    STOP = int(os.environ.get("FUSED_STOP", "99"))
    if STOP <= 0:
        p.emit()
        return nc
    p.cc(lambda e: e.collective_compute("AllGather", ALU.bypass, replica_groups=grp4, ins=[ends_src], outs=[ends_all]),
         writes=["d_ends_all"])
    pass
    if STOP <= 1:
        p.emit()
        return nc
    cx.over = dict(xT=xT[:, TH:TH + T], hloc=hloc, acum=acum, gate=gate, ends_all=ends_all, x2T=x2, hnT=x2, hn_tiles=hn_src, hn_all=hn_all)
    build_p24(True, False, cx)
    p.barrier(skip_cc=True)
    if STOP <= 2:
        p.emit()
        return nc
    pass
    if STOP <= 3:
        p.emit()
        return nc
    cx.over = dict(hn=hn_all, oT=o_src, o_all=o_all)
    build_p3(cx)
    p.barrier(skip_cc=True)
    if STOP <= 4:
        p.emit()
        return nc
    pass
    if STOP <= 5:
        p.emit()
        return nc
    cx.over = dict(xT=x2, x2T=x2, hnT=out, o_all=o_all)
    build_p24(False, True, cx)
    p.emit(final_wait_ops=cx.last_outs[-8:])
    return nc


def stage_fused(inp):
    m1 = stage_p1(inp)
    um, mk = attn_consts()
    wqkv = np.asarray(inp["b_w_qkv"][0], np.float32)
    f0 = ffn_weights(inp, 0)
    f0["w_out"] = slab_kn(inp["a_w_out"][0], 128)
    f0["g_next"] = col8(inp["norm_mix_g"][1])
    f1 = ffn_weights(inp, 1)
    f1["w_out"] = slab_kn(inp["b_w_out"][0], 128)
    f1["g_next"] = col8(inp["final_g"])
    maps = []
    for r in range(8):
        b, q = r // 4, r % 4
        m = {}
        for k, v in m1[r].items():
            m["p1_" + k] = v
        for k, v in f0.items():
            m["p2_" + k] = v
        for k, v in f1.items():
            m["p4_" + k] = v
        keep = np.zeros((128, 4), np.float32)
        for rr in range(q):
            keep[:, rr] = 1.0
        m["p2_km"] = keep
        m["p2_hm"] = 1.0 - keep
        sel = np.zeros((128, 4), np.float32)
        sel[:, q] = 1.0
        m["p4_sel"] = sel
        cols = slice(q * 256, (q + 1) * 256)
        m["p3_wq"] = slab_kn(wqkv[:, 0:D][:, cols], 256)[0]
        m["p3_wk"] = slab_kn(wqkv[:, D:2 * D][:, cols], 256)[0]
        m["p3_wv"] = slab_kn(wqkv[:, 2 * D:3 * D][:, cols], 256)[0]
        m["p3_umat"] = um
        m["p3_masks"] = mk
        maps.append(m)
    return maps


def kernel(**inp):
    inp = {k: np.asarray(v) for k, v in inp.items()}
    res = run(get_nc("fused", build_fused), stage_fused(inp))
    out = np.stack([np.concatenate([res[b * 4 + q]["out"].T for q in range(4)], axis=0) for b in range(2)])
    return np.ascontiguousarray(out.astype(np.float32))
```
